# Optimizing a Trainium2 kernel written in Bass

```python
import jax
import jax.numpy as jnp
from jax import lax
import numpy as np

D_MODEL = 1024
BATCH = 32
SEQ = 2048
DEPTH = 2

NSA_HEADS = 8
NSA_GROUPS = 2
NSA_REP = NSA_HEADS // NSA_GROUPS
NSA_DH = 64
NSA_WIDTH = NSA_HEADS * NSA_DH
NSA_KV_WIDTH = NSA_GROUPS * NSA_DH
CMP_LEN = 32
CMP_STRIDE = 16
CMP_HIDDEN = 2 * NSA_DH
SEL_LEN = 64
SEL_TOPK = 8
FORCE_SCORE = 1.0e4
WINDOW = 512
Q_BLOCK = 128
HGRN_HEADS = 4
HGRN_DK = 128
HGRN_DV = 128
HGRN_WIDTH = HGRN_HEADS * HGRN_DK
HGRN_VWIDTH = HGRN_HEADS * HGRN_DV
HGRN_CHUNK = 64
MLP_HIDDEN = 4 * D_MODEL
ROPE_THETA = 10000.0
LN_EPS = 1e-5
RMS_EPS = 1e-6
DEEPNORM_ALPHA = (2 * DEPTH) ** 0.25
DEEPNORM_BETA = (8 * DEPTH) ** -0.25
IN_SIZES = (NSA_WIDTH,) + (NSA_KV_WIDTH,) * 6 + (3 * NSA_HEADS,) + (HGRN_WIDTH, HGRN_WIDTH, HGRN_VWIDTH, HGRN_VWIDTH) + (D_MODEL, D_MODEL)
IN_OFFSETS = [int(v) for v in np.cumsum(IN_SIZES)[:-1]]
N_IN = int(sum(IN_SIZES))

kernel_name = 'hybrid_nsa_hgrn2_deepnorm_adaln'


def layer_norm(x, g, b):
    xf = x.astype(jnp.float32)
    mu = jnp.mean(xf, axis=-1, keepdims=True)
    var = jnp.mean(jnp.square(xf - mu), axis=-1, keepdims=True)
    y = (xf - mu) * lax.rsqrt(var + LN_EPS) * g.astype(jnp.float32) + b.astype(jnp.float32)
    return y.astype(x.dtype)


def masked_softmax(s, mask):
    s = jnp.where(mask, s.astype(jnp.float32), -jnp.inf)
    m = jnp.max(s, axis=-1, keepdims=True)
    m = jnp.where(jnp.isfinite(m), m, 0.0)
    e = jnp.where(mask, jnp.exp(s - m), 0.0)
    return e / jnp.maximum(jnp.sum(e, axis=-1, keepdims=True), 1e-30)


def rope_tables(S, dim, dtype):
    inv = 1.0 / (ROPE_THETA ** (jnp.arange(0, dim, 2, dtype=jnp.float32) / dim))
    ang = jnp.arange(S, dtype=jnp.float32)[:, None] * inv[None, :]
    return jnp.cos(ang).astype(dtype), jnp.sin(ang).astype(dtype)


def apply_rope(t, cos, sin):
    half = t.shape[-1] // 2
    t1, t2 = t[..., :half], t[..., half:]
    return jnp.concatenate([t1 * cos - t2 * sin, t2 * cos + t1 * sin], axis=-1)


def nsa_mixer(q, k_c, v_c, k_s, v_s, k_w, v_w, gate_logits, pe_k, pe_v, wk1, wk2, wv1, wv2):
    B, S, _ = q.shape
    dt = q.dtype
    G, R, dh = NSA_GROUPS, NSA_REP, NSA_DH
    scale = dh ** -0.5
    cos, sin = rope_tables(S, dh, dt)
    qh = apply_rope(q.reshape(B, S, G, R, dh).transpose(0, 2, 3, 1, 4), cos, sin)

    def kv_heads(t):
        return t.reshape(B, S, G, dh).transpose(0, 2, 1, 3)

    kc = apply_rope(kv_heads(k_c), cos, sin)
    vc = kv_heads(v_c)
    ks = apply_rope(kv_heads(k_s), cos, sin)
    vs = kv_heads(v_s)
    kw = apply_rope(kv_heads(k_w), cos, sin)
    vw = kv_heads(v_w)
    pos = np.arange(S)

    nc = (S - CMP_LEN) // CMP_STRIDE + 1
    cstart = np.arange(nc) * CMP_STRIDE
    cidx = cstart[:, None] + np.arange(CMP_LEN)[None, :]

    def compress(t, pe, w1, w2):
        blk = (t[:, :, cidx] + pe).reshape(B, G, nc, CMP_LEN * dh)
        return jax.nn.silu(blk @ w1) @ w2

    kcc = compress(kc, pe_k, wk1, wk2)
    vcc = compress(vc, pe_v, wv1, wv2)
    mask_c = jnp.asarray(cstart[None, :] + CMP_LEN - 1 <= pos[:, None])
    p_c = masked_softmax(jnp.einsum('bgrtd,bgnd->bgrtn', qh, kcc) * scale, mask_c)
    o_c = jnp.einsum('bgrtn,bgnd->bgrtd', p_c.astype(dt), vcc)

    nb = S // SEL_LEN
    sstart = np.arange(nb) * SEL_LEN
    overlap = ((cstart[:, None] < sstart[None, :] + SEL_LEN) & (cstart[:, None] + CMP_LEN > sstart[None, :])).astype(np.float32)
    imp = jnp.einsum('bgrtn,nj->bgtj', p_c, jnp.asarray(overlap))
    tb = pos // SEL_LEN
    jb = np.arange(nb)
    valid = jb[None, :] <= tb[:, None]
    forced = valid & ((jb[None, :] == 0) | (jb[None, :] == tb[:, None]) | (jb[None, :] == tb[:, None] - 1))
    score = jnp.where(jnp.asarray(forced), FORCE_SCORE, jnp.where(jnp.asarray(valid), imp, -1.0))
    n_sel = min(SEL_TOPK, nb)
    _, sel_idx = lax.top_k(score, n_sel)

    ks_blocks = ks.reshape(B, G, nb, SEL_LEN, dh)
    vs_blocks = vs.reshape(B, G, nb, SEL_LEN, dh)
    kw_pad = jnp.pad(kw, ((0, 0), (0, 0), (WINDOW, 0), (0, 0)))
    vw_pad = jnp.pad(vw, ((0, 0), (0, 0), (WINDOW, 0), (0, 0)))
    b_ix = jnp.arange(B)[:, None, None, None]
    g_ix = jnp.arange(G)[None, :, None, None]

    def block_fn(qb):
        s0 = qb * Q_BLOCK
        tq = s0 + jnp.arange(Q_BLOCK)
        qblk = lax.dynamic_slice_in_dim(qh, s0, Q_BLOCK, axis=3)
        idx = lax.dynamic_slice_in_dim(sel_idx, s0, Q_BLOCK, axis=2)
        kg = ks_blocks[b_ix, g_ix, idx]
        vg = vs_blocks[b_ix, g_ix, idx].reshape(B, G, Q_BLOCK, n_sel * SEL_LEN, dh)
        kpos = idx[..., None] * SEL_LEN + jnp.arange(SEL_LEN)
        m_s = (kpos <= tq[:, None, None]).reshape(B, G, 1, Q_BLOCK, n_sel * SEL_LEN)
        s_s = jnp.einsum('bgrqd,bgqnkd->bgrqnk', qblk, kg).reshape(B, G, R, Q_BLOCK, n_sel * SEL_LEN) * scale
        p_s = masked_softmax(s_s, m_s)
        o_s = jnp.einsum('bgrqm,bgqmd->bgrqd', p_s.astype(dt), vg)
        kwb = lax.dynamic_slice_in_dim(kw_pad, s0, WINDOW + Q_BLOCK, axis=2)
        vwb = lax.dynamic_slice_in_dim(vw_pad, s0, WINDOW + Q_BLOCK, axis=2)
        kp = s0 - WINDOW + jnp.arange(WINDOW + Q_BLOCK)
        dpos = tq[:, None] - kp[None, :]
        m_w = (kp[None, :] >= 0) & (dpos >= 0) & (dpos < WINDOW)
        p_w = masked_softmax(jnp.einsum('bgrqd,bgkd->bgrqk', qblk, kwb) * scale, m_w)
        o_w = jnp.einsum('bgrqk,bgkd->bgrqd', p_w.astype(dt), vwb)
        return o_s, o_w

    o_s, o_w = lax.map(block_fn, jnp.arange(S // Q_BLOCK))
    o_s = jnp.moveaxis(o_s, 0, 3).reshape(B, G, R, S, dh)
    o_w = jnp.moveaxis(o_w, 0, 3).reshape(B, G, R, S, dh)

    gl = jax.nn.sigmoid(gate_logits.reshape(B, S, G, R, 3)).transpose(0, 2, 3, 1, 4)
    o = gl[..., 0:1] * o_c + gl[..., 1:2] * o_s + gl[..., 2:3] * o_w
    return o.transpose(0, 3, 1, 2, 4).reshape(B, S, NSA_WIDTH).astype(dt)


def hgrn2_mixer(q, f, i, g, lb, norm_g):
    B, S, _ = q.shape
    dt = q.dtype
    H, dk, dv, C = HGRN_HEADS, HGRN_DK, HGRN_DV, HGRN_CHUNK
    f32 = jnp.float32
    qh = (jax.nn.silu(q.astype(f32)) * dk ** -0.5).reshape(B, S, H, dk)
    lb_h = lb.reshape(H, dk)
    log_f = jnp.logaddexp(jnp.log(lb_h), jnp.log1p(-lb_h) + jax.nn.log_sigmoid(f.astype(f32).reshape(B, S, H, dk)))
    kh = -jnp.expm1(log_f)
    vh = i.astype(f32).reshape(B, S, H, dv)
    nc = S // C

    def to_chunks(t):
        return t.reshape(B, nc, C, H, t.shape[-1]).transpose(1, 0, 3, 2, 4)

    causal = np.tril(np.ones((C, C), dtype=bool))

    def step(state, inp):
        qc, kc, vc, lfc = inp
        b = jnp.cumsum(lfc, axis=2)
        diff = jnp.where(causal[:, :, None], b[:, :, :, None, :] - b[:, :, None, :, :], -jnp.inf)
        a = jnp.sum(qc[:, :, :, None, :] * kc[:, :, None, :, :] * jnp.exp(diff), axis=-1)
        o = jnp.einsum('bhts,bhsv->bhtv', a, vc) + jnp.einsum('bhtd,bhdv->bhtv', qc * jnp.exp(b), state)
        b_last = b[:, :, -1:, :]
        new_state = jnp.exp(b_last[:, :, 0, :])[..., None] * state + jnp.einsum('bhsd,bhsv->bhdv', kc * jnp.exp(b_last - b), vc)
        return new_state, o

    state0 = jnp.zeros((B, H, dk, dv), f32)
    _, o = lax.scan(step, state0, (to_chunks(qh), to_chunks(kh), to_chunks(vh), to_chunks(log_f)))
    o = o.transpose(1, 0, 3, 2, 4).reshape(B, S, H, dv)
    o = o * lax.rsqrt(jnp.mean(jnp.square(o), axis=-1, keepdims=True) + RMS_EPS) * norm_g.astype(f32)
    o = o.reshape(B, S, HGRN_VWIDTH) * jax.nn.silu(g.astype(f32))
    return o.astype(dt)


def setup_inputs(seed: int = 0) -> dict:
    key = jax.random.key(seed)
    ks = jax.random.split(key, 23)
    L, D = DEPTH, D_MODEL

    def nrm(k, shape, scale):
        return scale * jax.random.normal(k, shape, jnp.float32)

    return {
        'x': nrm(ks[0], (BATCH, SEQ, D), 1.0),
        'c': nrm(ks[1], (BATCH, D), 1.0),
        'w_in': nrm(ks[2], (L, D, N_IN), D ** -0.5),
        'b_in': nrm(ks[3], (L, N_IN), 0.02),
        'cmp_pe_k': nrm(ks[4], (L, CMP_LEN, NSA_DH), 0.02),
        'cmp_pe_v': nrm(ks[5], (L, CMP_LEN, NSA_DH), 0.02),
        'cmp_wk1': nrm(ks[6], (L, CMP_LEN * NSA_DH, CMP_HIDDEN), (CMP_LEN * NSA_DH) ** -0.5),
        'cmp_wk2': nrm(ks[7], (L, CMP_HIDDEN, NSA_DH), CMP_HIDDEN ** -0.5),
        'cmp_wv1': nrm(ks[8], (L, CMP_LEN * NSA_DH, CMP_HIDDEN), (CMP_LEN * NSA_DH) ** -0.5),
        'cmp_wv2': nrm(ks[9], (L, CMP_HIDDEN, NSA_DH), CMP_HIDDEN ** -0.5),
        'hgrn_lb_logits': nrm(ks[10], (L, HGRN_WIDTH), 1.0),
        'hgrn_norm_g': 1.0 + nrm(ks[11], (L, HGRN_DV), 0.02),
        'w_branch_a': nrm(ks[12], (L, NSA_WIDTH, D), NSA_WIDTH ** -0.5),
        'w_branch_b': nrm(ks[13], (L, HGRN_VWIDTH, D), HGRN_VWIDTH ** -0.5),
        'w_out': nrm(ks[14], (L, D, D), DEEPNORM_BETA * D ** -0.5),
        'w_ada': nrm(ks[15], (L, D, 6 * D), 0.1 * D ** -0.5),
        'b_ada': nrm(ks[16], (L, 6 * D), 0.02),
        'ln1_g': 1.0 + nrm(ks[17], (L, D), 0.02),
        'ln1_b': nrm(ks[18], (L, D), 0.02),
        'w_mlp1': nrm(ks[19], (L, D, MLP_HIDDEN), D ** -0.5),
        'w_mlp2': nrm(ks[20], (L, MLP_HIDDEN, D), DEEPNORM_BETA * MLP_HIDDEN ** -0.5),
        'ln2_g': 1.0 + nrm(ks[21], (L, D), 0.02),
        'ln2_b': nrm(ks[22], (L, D), 0.02),
    }


def reference(x, c, w_in, b_in, cmp_pe_k, cmp_pe_v, cmp_wk1, cmp_wk2, cmp_wv1, cmp_wv2, hgrn_lb_logits, hgrn_norm_g, w_branch_a, w_branch_b, w_out, w_ada, b_ada, ln1_g, ln1_b, w_mlp1, w_mlp2, ln2_g, ln2_b):
    lb_all = jnp.cumsum(jax.nn.softmax(hgrn_lb_logits.astype(jnp.float32), axis=0), axis=0)
    lb_all = lb_all - lb_all[0:1]
    cond = jax.nn.silu(c)
    for l in range(DEPTH):
        mod = cond @ w_ada[l] + b_ada[l]
        sh1, sc1, gt1, sh2, sc2, gt2 = [m[:, None, :] for m in jnp.split(mod, 6, axis=-1)]
        u = x * (1.0 + sc1) + sh1
        h = u @ w_in[l] + b_in[l]
        q_a, k_c, v_c, k_s, v_s, k_w, v_w, g_a, q_b, f_b, i_b, g_b, gm_a, gm_b = jnp.split(h, IN_OFFSETS, axis=-1)
        y_a = nsa_mixer(q_a, k_c, v_c, k_s, v_s, k_w, v_w, g_a, cmp_pe_k[l], cmp_pe_v[l], cmp_wk1[l], cmp_wk2[l], cmp_wv1[l], cmp_wv2[l])
        y_b = hgrn2_mixer(q_b, f_b, i_b, g_b, lb_all[l], hgrn_norm_g[l])
        merged = jax.nn.sigmoid(gm_a) * (y_a @ w_branch_a[l]) + jax.nn.sigmoid(gm_b) * (y_b @ w_branch_b[l])
        y = merged @ w_out[l]
        x = layer_norm(DEEPNORM_ALPHA * x + (1.0 + gt1) * y, ln1_g[l], ln1_b[l])
        u = x * (1.0 + sc2) + sh2
        y = jnp.square(jax.nn.relu(u @ w_mlp1[l])) @ w_mlp2[l]
        x = layer_norm(DEEPNORM_ALPHA * x + (1.0 + gt2) * y, ln2_g[l], ln2_b[l])
    return x
```

```python
from contextlib import ExitStack
import numpy as np
import concourse.bass as bass
import concourse.mybir as mybir
from concourse.bass_utils import run_bass_kernel_spmd

F32 = mybir.dt.float32
BF16 = mybir.dt.bfloat16
AF = mybir.ActivationFunctionType
ALU = mybir.AluOpType

D = 1024
S = 2048
L = 2
NSEQ = 4
N_IN = 5400
NEG = -30000.0
ALPHA = (2 * L) ** 0.25
LN_EPS = 1e-5
RMS_EPS = 1e-6
BCOL_OFFS = [640] + [1304 + 128 * i for i in range(4)] + [1816 + 128 * i for i in range(4)] + \
    [2840 + 128 * i for i in range(4)] + [3352 + 128 * i for i in range(8)] + [4376 + 128 * i for i in range(8)]
BC_VC, BC_QB, BC_FB, BC_GB, BC_GMA, BC_GMB = 0, 1, 5, 9, 13, 21


class V:
    __slots__ = ("ap", "res")

    def __init__(self, ap, res):
        self.ap = ap
        self.res = res


class T:
    def __init__(self, name, handle):
        self.name = name
        self.h = handle

    def __getitem__(self, key):
        return V(self.h[key], (self.name,))

    def sub(self, skey, key=slice(None)):
        return V(self.h[key], ((self.name, skey),))

    def v(self, ap, skey=None):
        return V(ap, ((self.name,) if skey is None else ((self.name, skey),)))


class HalfBank:
    def __init__(self, t, i):
        self.t = t
        self.i = i

    def __getitem__(self, key):
        rows, cols = key
        c0 = (cols.start or 0) + 512 * self.i
        c1 = (cols.stop if cols.stop is not None else 512) + 512 * self.i
        return V(self.t.h[rows, c0:c1], ((self.t.name, self.i),))


class K:
    def __init__(self, nc):
        self.nc = nc
        self.es = ExitStack()
        self.engs = {"pe": nc.tensor, "act": nc.scalar, "dve": nc.vector, "pool": nc.gpsimd, "sp": nc.sync}
        self.sem = {}
        self.cnt = {}
        for n in ("pe", "act", "dve", "pool"):
            self.sem[n] = self.es.enter_context(nc.semaphore("s_" + n))
            self.cnt[n] = 0
        self.ndma = 32
        self.dsem = [self.es.enter_context(nc.semaphore("s_dma%d" % i)) for i in range(self.ndma)]
        self.dcnt = [0] * self.ndma
        self.dnext = 0
        self.seen = {n: {} for n in self.engs}
        self.writers = {}
        self.readers = {}
        self.n_inst = 0
        self.n_wait = 0
        self.taps = {}
        self.uid = 0

    def sb(self, name, shape, dtype, es=None):
        self.uid += 1
        name = "%s_%d" % (name, self.uid)
        h = (es or self.es).enter_context(self.nc.sbuf_tensor(name, list(shape), dtype))
        return T(name, h)

    def ps(self, name, shape, dtype=F32):
        h = self.es.enter_context(self.nc.psum_tensor(name, list(shape), dtype))
        return T(name, h)

    def dram(self, name, shape, dtype, kind):
        h = self.nc.dram_tensor(name, list(shape), dtype, kind=kind)
        return T(name, h)

    def _semobj(self, semkey):
        return self.sem[semkey] if isinstance(semkey, str) else self.dsem[semkey]

    def _deps(self, eng, reads, writes):
        deps = {}

        def add(d, same_ok):
            semkey, val, e = d
            if e == eng and same_ok:
                return
            if deps.get(semkey, 0) < val:
                deps[semkey] = val

        for r in reads:
            w = self.writers.get(r)
            if w is not None:
                add(w, eng == "pe")
        for r in writes:
            w = self.writers.get(r)
            if w is not None:
                add(w, eng == "pe")
            for rd in self.readers.get(r, ()):
                add(rd, eng == "pe")
        seen = self.seen[eng]
        e = self.engs[eng]
        for semkey, val in deps.items():
            if seen.get(semkey, 0) >= val:
                continue
            e.wait_ge(self._semobj(semkey), val)
            seen[semkey] = val
            self.n_wait += 1

    def _commit(self, tok, reads, writes):
        for r in reads:
            lst = self.readers.setdefault(r, [])
            lst[:] = [x for x in lst if x[0] != tok[0]]
            lst.append(tok)
        for r in writes:
            self.writers[r] = tok
            self.readers[r] = []

    @staticmethod
    def _res(vs):
        out = []
        for v in vs:
            if v is None:
                continue
            out.extend(v.res)
        return out

    def op(self, eng, fn, reads, writes, *args, signal=True, **kw):
        rr, ww = self._res(reads), self._res(writes)
        self._deps(eng, rr, ww)
        ins = fn(*args, **kw)
        if signal:
            self.cnt[eng] += 1
            ins.then_inc(self.sem[eng], 1)
            tok = (eng, self.cnt[eng], eng)
        else:
            tok = (eng, self.cnt[eng] + 1, eng)
        self._commit(tok, rr, ww)
        self.n_inst += 1
        return ins

    def dma(self, out, in_, queue="sp", **kw):
        rr, ww = self._res([in_]), self._res([out])
        i = self.dnext
        self.dnext = (self.dnext + 1) % self.ndma
        if self.dcnt[i] > 0 and self.seen[queue].get(i, 0) < self.dcnt[i]:
            self.engs[queue].wait_ge(self.dsem[i], self.dcnt[i])
            self.seen[queue][i] = self.dcnt[i]
        self._deps(queue, rr, ww)
        ins = self.engs[queue].dma_start(out=out.ap, in_=in_.ap, **kw)
        self.dcnt[i] += 16
        ins.then_inc(self.dsem[i], 16)
        self._commit((i, self.dcnt[i], "dma"), rr, ww)
        self.n_inst += 1
        return ins

    def barrier(self):
        for en in ("pe", "act", "dve", "pool", "sp"):
            e = self.engs[en]
            for sk in ("pe", "act", "dve", "pool"):
                if sk != en and self.cnt[sk] > self.seen[en].get(sk, 0):
                    e.wait_ge(self.sem[sk], self.cnt[sk])
                    self.seen[en][sk] = self.cnt[sk]
            for i in range(self.ndma):
                if self.dcnt[i] > self.seen[en].get(i, 0):
                    e.wait_ge(self.dsem[i], self.dcnt[i])
                    self.seen[en][i] = self.dcnt[i]
        self.writers = {}
        self.readers = {}

    def mm(self, out, lhsT, rhs, start=True, stop=True, **kw):
        return self.op("pe", self.nc.tensor.matmul, [lhsT, rhs], [out], out.ap, lhsT.ap, rhs.ap,
                       signal=bool(stop), start=start, stop=stop, **kw)

    def tr(self, out, in_, ident):
        return self.op("pe", self.nc.tensor.transpose, [in_, ident], [out], out.ap, in_.ap, ident.ap)

    def act(self, out, in_, func, bias=None, scale=None):
        kw = {}
        rd = [in_]
        if bias is not None:
            if isinstance(bias, V):
                kw["bias"] = bias.ap
                rd.append(bias)
            else:
                kw["bias"] = bias
        if scale is not None:
            if isinstance(scale, V):
                kw["scale"] = scale.ap
                rd.append(scale)
            else:
                kw["scale"] = scale
        return self.op("act", self.nc.scalar.activation, rd, [out], out.ap, in_.ap, func, **kw)

    def tt(self, out, a, b, op, eng="dve"):
        e = self.engs[eng]
        return self.op(eng, e.tensor_tensor, [a, b], [out], out.ap, a.ap, b.ap, op)

    def ts(self, out, a, s1, op0, s2=None, op1=None, eng="dve"):
        e = self.engs[eng]
        rd = [a]
        a1, a2 = s1, s2
        if isinstance(s1, V):
            rd.append(s1)
            a1 = s1.ap
        if isinstance(s2, V):
            rd.append(s2)
            a2 = s2.ap
        kw = {}
        if op1 is not None:
            kw["op1"] = op1
        return self.op(eng, e.tensor_scalar, rd, [out], out.ap, a.ap, a1, a2, op0, **kw)

    def stt(self, out, a, s, b, op0, op1):
        rd = [a, b]
        sv = s
        if isinstance(s, V):
            rd.append(s)
            sv = s.ap
        return self.op("dve", self.nc.vector.scalar_tensor_tensor, rd, [out], out.ap, a.ap, sv, b.ap, op0, op1)

    def copy(self, out, in_, eng="dve"):
        if eng == "act":
            return self.op("act", self.nc.scalar.copy, [in_], [out], out.ap, in_.ap)
        e = self.engs[eng]
        return self.op(eng, e.tensor_copy, [in_], [out], out.ap, in_.ap)

    def memset(self, out, val, eng="pool"):
        e = self.engs[eng]
        return self.op(eng, e.memset, [], [out], out.ap, val)

    def recip(self, out, in_):
        return self.op("dve", self.nc.vector.reciprocal, [in_], [out], out.ap, in_.ap)

    def tap(self, name, view, shape, dtype=F32):
        t = self.dram("tap_" + name, shape, dtype, "ExternalOutput")
        self.dma(t[:], view, queue="sp")
        self.taps[name] = t

    def finish(self):
        for i in range(self.ndma):
            if self.dcnt[i] > 0:
                self.nc.sync.wait_ge(self.dsem[i], self.dcnt[i])

    def close(self):
        self.es.close()


class Prog:
    def __init__(self, nseq=NSEQ, layers=(0, 1), first=True, last=True, taps=(), stop=None, skip=()):
        self.nseq = nseq
        self.layers = layers
        self.first = first
        self.last = last
        self.want = set(taps)
        self.stop = stop
        self.skip = set(skip)
        nc = bass.Bass("TRN2", target_bir_lowering=False)
        self.nc = nc
        k = K(nc)
        self.k = k
        I = {}
        self.I = I

        def inp(name, shape, dt=F32):
            I[name] = k.dram(name, shape, dt, "ExternalInput")

        inp("x", [nseq, S, D])
        inp("c_l", [128, 8, nseq])
        inp("w_in", [L, D, N_IN])
        inp("b_in", [L, N_IN])
        inp("bcols", [128, L, len(BCOL_OFFS)])
        inp("pe_kT", [L, 128, 32])
        inp("pe_vT", [L, 128, 32])
        inp("cmp_wk1", [L, 2048, 128])
        inp("cmp_wk2", [L, 128, 64])
        inp("cmp_wv1", [L, 2048, 128])
        inp("cmp_wv2", [L, 128, 64])
        inp("lb_l", [128, L, 4])
        inp("normg_l", [128, L])
        inp("w_branch_a", [L, 512, D])
        inp("w_branch_b", [L, 512, D])
        inp("w_out", [L, D, D])
        inp("w_ada", [L, D, 6 * D])
        inp("b_ada", [L, 6 * D])
        inp("b_adaT", [128, L, 48])
        inp("ln1_g", [L, D])
        inp("ln1_b", [L, D])
        inp("w_mlp1", [L, D, 4 * D])
        inp("w_mlp2", [L, 4 * D, D])
        inp("ln2_g", [L, D])
        inp("ln2_b", [L, D])
        inp("k_cmw", [8, 128, 512], BF16)
        inp("k_cv", [127, S], BF16)
        inp("k_e", [32, S], BF16)
        inp("k_ov", [127, 33], BF16)
        inp("k_rst", [128, S])
        inp("k_bdm", [128, 128])
        inp("k_ident", [128, 128])
        inp("k_identb", [128, 128], BF16)
        inp("k_cos", [128, 16, 32])
        inp("k_sin", [128, 16, 32])
        inp("k_vm", [128, 16, 32])
        inp("k_addc", [128, 16, 32])
        self.W16 = {}
        for nme, shp in (("w_in", [L, D, N_IN]), ("w_branch_a", [L, 512, D]), ("w_branch_b", [L, 512, D]),
                         ("w_out", [L, D, D]), ("w_mlp1", [L, D, 4 * D]), ("cmp_wk1", [L, 2048, 128]),
                         ("cmp_wv1", [L, 2048, 128]), ("cmp_wk2", [L, 128, 64]), ("cmp_wv2", [L, 128, 64])):
            self.W16[nme] = k.dram("s16_" + nme, shp, BF16, "Internal")
        self.W16["w_mlp2"] = k.dram("s16_w_mlp2", [L, 8, 128, 32, 128], BF16, "Internal")
        self.out = k.dram("out", [nseq, S, D], F32, "ExternalOutput")
        self.xres = k.dram("xres", [nseq, S, D], F32, "Internal")
        self.grow = k.dram("grow", [L, nseq, 2, D], F32, "Internal")

        self.pf = [k.ps("pf%d" % i, [128, 512], F32) for i in range(6)]
        self.pb = [k.ps("pb%d" % i, [128, 1024], BF16) for i in range(2)]
        self.psi = 0
        self.pmi = 0
        self.pfi = 0
        self.pai = 0
        self.pbi = 0

        self.IDF = k.sb("idf", [128, 128], F32)
        self.IDB = k.sb("idb", [128, 128], BF16)
        self.ONESF = k.sb("onesf", [128, 128], F32)
        self.MODT = k.sb("modt", [128, L, 48, nseq], F32)
        self.LB = k.sb("lb", [128, L, 4], F32)
        self.OML = k.sb("oml", [128, L, 4], F32)
        self.BCOL = k.sb("bcol", [128, L, len(BCOL_OFFS)], F32)
        self.NORMG = k.sb("normg", [128, L], F32)
        self.EPS = k.sb("eps", [128, 2], F32)
        self.UT = k.sb("ut", [128, 8, S], BF16)
        self.YAB = k.sb("yab", [128, 8, S], BF16)
        self.WB = [k.sb("wb%d" % i, [128, 4096], BF16) for i in range(3)]
        self.wbi = 0
        self.STG = [k.sb("stg%d" % i, [128, 256], F32) for i in range(2)]
        self.stgi = 0

        k.dma(self.IDF[:], I["k_ident"][:])
        k.dma(self.IDB[:], I["k_identb"][:])
        k.dma(self.BCOL[:], I["bcols"][:])
        k.dma(self.NORMG[:], I["normg_l"][:])
        k.memset(self.ONESF[:], 1.0)
        k.memset(self.EPS[:, 0:1], LN_EPS)
        k.memset(self.EPS[:, 1:2], RMS_EPS)

        self.prologue()
        for s in range(nseq):
            for l in layers:
                self.layer(s, l)
        k.finish()
        k.close()

    def bank(self):
        p = self.pf[self.pfi]
        self.pfi = (self.pfi + 1) % 4
        return p

    def sbank(self):
        p = self.pf[self.psi]
        self.psi = (self.psi + 1) % 3
        return p

    def mbank(self):
        return self.abank()

    def abank(self):
        p = self.pf[4 + self.pai]
        self.pai = (self.pai + 1) % 2
        return p

    def bbank(self):
        p = self.pb[self.pbi]
        self.pbi = (self.pbi + 1) % 2
        return p

    def wbuf(self):
        w = self.WB[self.wbi]
        self.wbi = (self.wbi + 1) % len(self.WB)
        return w

    def stg(self):
        t = self.STG[self.stgi]
        self.stgi = (self.stgi + 1) % len(self.STG)
        return t

    def cast_load(self, dst_t, dst_ap, src_ap):
        k = self.k
        shp = list(dst_ap.shape)
        p0 = dst_ap.base_partition()
        P = shp[0]
        if len(shp) == 2:
            B = shp[1]
            for b0 in range(0, B, 2048):
                b1 = min(B, b0 + 2048)
                st = self.stg()
                sv = st.h[p0:p0 + P, 0:b1 - b0]
                k.dma(st.v(sv), V(src_ap[:, b0:b1], ()))
                k.copy(dst_t.v(dst_ap[:, b0:b1]), st.v(sv), eng="pool")
            return
        A, B = shp[1], shp[2]
        per = max(1, 2048 // B)
        for a0 in range(0, A, per):
            a1 = min(A, a0 + per)
            st = self.stg()
            sv = st.h[p0:p0 + P, 0:(a1 - a0) * B].rearrange("p (a b) -> p a b", a=a1 - a0)
            k.dma(st.v(sv), V(src_ap[:, a0:a1, :], ()))
            k.copy(dst_t.v(dst_ap[:, a0:a1, :]), st.v(sv), eng="pool")

    def load_w(self, src_ap, kch, ncols, pretiled=False):
        w = self.wbuf()
        view = w.h[:, 0:kch * ncols].rearrange("p (a b) -> p a b", a=kch)
        src = src_ap if pretiled else src_ap.rearrange("(a p) n -> p a n", p=128)
        self.k.dma(w.v(view), V(src, ()))
        return w, view

    def tapv(self, name, view, shape, dtype=F32):
        if name in self.want:
            self.k.tap(name, view, shape, dtype)

    def convert_weights(self):
        k, I = self.k, self.I
        es = ExitStack()
        SF = [k.sb("cvf%d" % i, [128, 2048], F32, es) for i in range(4)]
        SH = [k.sb("cvh%d" % i, [128, 2048], BF16, es) for i in range(4)]
        cnt = [0]
        engs = ("pool", "dve", "act")

        def piece(src_ap, dst_ap, P, n):
            i = cnt[0]
            cnt[0] += 1
            f, h = SF[i % 4], SH[i % 4]
            shp = list(src_ap.shape)
            if len(shp) == 2:
                fv, hv = f.h[0:P, 0:n], h.h[0:P, 0:n]
            else:
                fv = f.h[0:P, 0:n].rearrange("p (a b) -> p a b", a=shp[1])
                hv = h.h[0:P, 0:n].rearrange("p (a b) -> p a b", a=shp[1])
            k.dma(f.v(fv), V(src_ap, ()))
            k.copy(h.v(hv), f.v(fv), eng=engs[i % 3])
            k.dma(V(dst_ap, (("w16", i),)), h.v(hv), queue="act")

        for l in self.layers:
            for nme in ("w_in", "w_branch_a", "w_branch_b", "w_out", "w_mlp1", "cmp_wk1", "cmp_wv1", "cmp_wk2", "cmp_wv2"):
                src = I[nme].h[l].rearrange("(p a) n -> p (a n)", p=128)
                dst = self.W16[nme].h[l].rearrange("(p a) n -> p (a n)", p=128)
                tot = src.shape[1]
                for j0 in range(0, tot, 2048):
                    j1 = min(tot, j0 + 2048)
                    piece(src[:, j0:j1], dst[:, j0:j1], 128, j1 - j0)
            for nch in range(8):
                srcv = I["w_mlp2"].h[l, :, nch * 128:(nch + 1) * 128].rearrange("(a p) n -> p a n", p=128)
                for hh in range(2):
                    piece(srcv[:, hh * 16:(hh + 1) * 16, :], self.W16["w_mlp2"].h[l, nch, :, hh * 16:(hh + 1) * 16, :], 128, 2048)
        k.barrier()
        es.close()

    def prologue(self):
        k, I, nseq = self.k, self.I, self.nseq
        self.convert_weights()
        es = ExitStack()
        condT = k.sb("condT", [128, 8, nseq], F32, es)
        k.dma(condT[:], I["c_l"][:])
        k.act(condT[:], condT[:], AF.Silu)
        badaT = k.sb("badaT", [128, L, 48], F32, es)
        k.dma(badaT[:], I["b_adaT"][:])
        wp = [k.sb("wada%d" % i, [128, 8, 512], F32, es) for i in range(2)]
        brow = k.sb("brow", [1, 512], F32, es)
        grow_sb = k.sb("growsb", [1, 512], F32, es)
        for l in self.layers:
            for piece in range(12):
                w = wp[piece % 2]
                k.dma(w[:], V(I["w_ada"].h[l, :, piece * 512:(piece + 1) * 512].rearrange("(a p) n -> p a n", p=128), ()))
                for j in range(4):
                    ch = piece * 4 + j
                    ps = self.bank()
                    for kc in range(8):
                        k.mm(ps[:, 0:nseq], w[:, kc, j * 128:(j + 1) * 128], condT[:, kc, :], start=(kc == 0), stop=(kc == 7))
                    k.ts(self.MODT[:, l, ch, :], ps[:, 0:nseq], badaT[:, l, ch:ch + 1], ALU.add)
                if piece in (4, 5, 10, 11):
                    which = 0 if piece < 6 else 1
                    half = piece % 2
                    k.dma(brow[:], V(I["b_ada"].h[l:l + 1, piece * 512:(piece + 1) * 512], ()))
                    for b in range(nseq):
                        ps = self.bank()
                        for kc in range(8):
                            k.mm(ps[0:1, :], condT[:, kc, b:b + 1], w[:, kc, :], start=(kc == 0), stop=(kc == 7))
                        k.tt(grow_sb[:], ps[0:1, :], brow[:], ALU.add)
                        k.ts(grow_sb[:], grow_sb[:], 1.0, ALU.add)
                        k.dma(self.grow.v(self.grow.h[l, b, which:which + 1, half * 512:(half + 1) * 512]), grow_sb[:])
            k.ts(self.MODT[:, l, 8:16, :], self.MODT[:, l, 8:16, :], 1.0, ALU.add)
            k.ts(self.MODT[:, l, 32:40, :], self.MODT[:, l, 32:40, :], 1.0, ALU.add)
        z = k.sb("lbz", [128, L, 4], F32, es)
        e = k.sb("lbe", [128, L, 4], F32, es)
        ssum = k.sb("lbs", [128, 4], F32, es)
        cum = k.sb("lbc", [128, L, 4], F32, es)
        k.dma(z[:], I["lb_l"][:])
        k.act(e[:], z[:], AF.Exp)
        k.tt(ssum[:], e[:, 0, :], e[:, 1, :], ALU.add)
        k.recip(ssum[:], ssum[:])
        for l in range(L):
            k.tt(e[:, l, :], e[:, l, :], ssum[:], ALU.mult)
        k.copy(cum[:, 0, :], e[:, 0, :])
        k.tt(cum[:, 1, :], e[:, 0, :], e[:, 1, :], ALU.add)
        for l in range(L):
            k.tt(self.LB[:, l, :], cum[:, l, :], cum[:, 0, :], ALU.subtract)
        k.ts(self.OML[:], self.LB[:], -1.0, ALU.mult, 1.0, ALU.add)
        k.barrier()
        es.close()

    def layer(self, s, l):
        k, I = self.k, self.I
        src = I["x"] if (self.first and l == self.layers[0]) else self.xres
        self.make_ut(s, l, src, 0)
        self.tapv("ut", self.UT[:], [128, 8, S], BF16)
        if self.stop == "ut":
            return
        if 'nsa' not in self.skip:
            self.nsa(s, l)
        self.tapv("yaT", self.YAB.sub("a", (slice(None), slice(0, 4), slice(None))), [128, 4, S], BF16)
        if self.stop == "nsa":
            return
        if 'hgrn' not in self.skip:
            self.hgrn(s, l)
        self.tapv("ybT", self.YAB.sub("b", (slice(None), slice(4, 8), slice(None))), [128, 4, S], BF16)
        if self.stop in ("hgrn", "hg1", "hg2"):
            return
        self.tail(s, l, src)

    def make_ut(self, s, l, src, sub):
        k = self.k
        es = ExitStack()
        xt = [k.sb("xt%d" % i, [128, 4, D], F32, es) for i in range(2)]
        sh_c, sc_c = (0, 8) if sub == 0 else (24, 32)
        for tc in range(4):
            x4 = xt[tc % 2]
            k.dma(x4[:], V(src.h[s, tc * 512:(tc + 1) * 512, :].rearrange("(a p) d -> p a d", p=128), src[:].res))
            for fc in range(8):
                ps = self.bank()
                for a in range(4):
                    k.tr(ps[:, a * 128:(a + 1) * 128], x4[:, a, fc * 128:(fc + 1) * 128], self.IDF[:])
                k.act(self.UT[:, fc, tc * 512:(tc + 1) * 512], ps[:], AF.Identity,
                      bias=self.MODT[:, l, sh_c + fc, s:s + 1], scale=self.MODT[:, l, sc_c + fc, s:s + 1])
        k.barrier()
        es.close()

    def nsa(self, s, l):
        k, I = self.k, self.I
        es = ExitStack()
        sb = lambda n, sh, dt: k.sb(n, sh, dt, es)
        QT = sb("qt", [128, 4, S], BF16)
        KST = [sb("kst%d" % g, [128, S], BF16) for g in range(2)]
        KWT = [sb("kwt%d" % g, [128, S], BF16) for g in range(2)]
        VS = sb("vs", [128, 16, 2, 65], BF16)
        VW = sb("vw", [128, 16, 2, 65], BF16)
        GATE = sb("gate", [128, 16, 24], F32)
        KCC = [sb("kcc%d" % g, [128, 128], BF16) for g in range(2)]
        VCC = sb("vcc", [127, 2, 65], BF16)
        k.memset(VS[:, :, :, 64:65], 1.0)
        k.memset(VW[:, :, :, 64:65], 1.0)
        win = self.W16["w_in"].h
        WA, wav = self.load_w(win[l, :, 0:512], 8, 512)
        WBb, wbv = self.load_w(win[l, :, 512:1024], 8, 512)
        WC, wcv = self.load_w(win[l, :, 1024:1304], 8, 280)

        es12 = ExitStack()
        KCT = k.sb("kct", [128, S], BF16, es12)
        VCT = k.sb("vct", [128, S], BF16, es12)
        es1 = ExitStack()
        COS = k.sb("cos", [128, 16, 32], F32, es1)
        SIN = k.sb("sin", [128, 16, 32], F32, es1)
        k.dma(COS[:], I["k_cos"][:])
        k.dma(SIN[:], I["k_sin"][:])
        BROW = k.sb("brow", [128, 1304], F32, es1)
        k.dma(BROW[:], V(I["b_in"].h[l, 0:1304].partition_broadcast(128), ()))
        R = [k.sb("r%d" % i, [128, 896], F32, es1) for i in range(2)]
        TA = k.sb("ta", [128, 448], F32, es1)
        TB = k.sb("tb", [128, 448], F32, es1)
        RO = k.sb("ro", [128, 14, 64], BF16, es1)
        RB = [k.sb("rb%d" % i, [128, 1152], BF16, es1) for i in range(4)]
        GL = k.sb("gl", [128, 24], F32, es1)
        for tc in range(4):
            for tl in range(4):
                tt = tc * 4 + tl
                pa, pb_, pc = self.bank(), self.bank(), self.bank()
                for kc in range(8):
                    lhs = self.UT[:, kc, tt * 128:(tt + 1) * 128]
                    k.mm(pa[:], lhs, WA.v(wav[:, kc, :]), start=(kc == 0), stop=(kc == 7))
                for kc in range(8):
                    lhs = self.UT[:, kc, tt * 128:(tt + 1) * 128]
                    k.mm(pb_[:], lhs, WBb.v(wbv[:, kc, :]), start=(kc == 0), stop=(kc == 7))
                for kc in range(8):
                    lhs = self.UT[:, kc, tt * 128:(tt + 1) * 128]
                    k.mm(pc[:, 0:280], lhs, WC.v(wcv[:, kc, :]), start=(kc == 0), stop=(kc == 7))
                r = R[tt % 2]
                k.tt(r[:, 0:512], pa[:], BROW[:, 0:512], ALU.add)
                k.tt(r[:, 512:640], pb_[:, 0:128], BROW[:, 512:640], ALU.add)
                k.tt(r[:, 640:768], pb_[:, 256:384], BROW[:, 768:896], ALU.add)
                k.tt(r[:, 768:896], pc[:, 0:128], BROW[:, 1024:1152], ALU.add)
                k.tt(VS.v(VS.h[:, tt, :, 0:64]), V(pb_.h[:, 384:512].rearrange("p (g d) -> p g d", g=2), pb_[:].res),
                     V(BROW.h[:, 896:1024].rearrange("p (g d) -> p g d", g=2), BROW[:].res), ALU.add)
                k.tt(VW.v(VW.h[:, tt, :, 0:64]), V(pc.h[:, 128:256].rearrange("p (g d) -> p g d", g=2), pc[:].res),
                     V(BROW.h[:, 1152:1280].rearrange("p (g d) -> p g d", g=2), BROW[:].res), ALU.add)
                k.tt(GL[:], pc[:, 256:280], BROW[:, 1280:1304], ALU.add)
                k.act(GATE[:, tt, :], GL[:], AF.Sigmoid)
                rv = r.h[:, :].rearrange("p (h two d) -> p h two d", two=2, d=32)
                t1 = r.v(rv[:, :, 0, :])
                t2 = r.v(rv[:, :, 1, :])
                cosb = COS.v(COS.h[:, tt:tt + 1, :].to_broadcast([128, 14, 32]))
                sinb = SIN.v(SIN.h[:, tt:tt + 1, :].to_broadcast([128, 14, 32]))
                ta = TA.v(TA.h[:, :].rearrange("p (h d) -> p h d", d=32))
                tb = TB.v(TB.h[:, :].rearrange("p (h d) -> p h d", d=32))
                k.tt(ta, t1, cosb, ALU.mult)
                k.tt(tb, t2, sinb, ALU.mult)
                k.tt(RO.v(RO.h[:, :, 0:32]), ta, tb, ALU.subtract)
                k.tt(ta, t2, cosb, ALU.mult)
                k.tt(tb, t1, sinb, ALU.mult)
                k.tt(RO.v(RO.h[:, :, 32:64]), ta, tb, ALU.add)
                rb = RB[tl]
                k.copy(rb.v(rb.h[:, 0:640].rearrange("p (h d) -> p h d", d=64)), RO.v(RO.h[:, 0:10, :]), eng="pool")
                k.copy(rb.v(rb.h[:, 640:1152].rearrange("p (h c d) -> p h c d", c=2, d=64)),
                       RO.v(RO.h[:, 10:14, :].unsqueeze(2).to_broadcast([128, 4, 2, 64])), eng="pool")
            tsl = slice(tc * 512, (tc + 1) * 512)
            dests = [QT.v(QT.h[:, j, tsl]) for j in range(4)] + [KCT[:, tsl], KST[0][:, tsl], KST[1][:, tsl],
                                                                   KWT[0][:, tsl], KWT[1][:, tsl]]
            for j in range(9):
                pbk = self.bbank()
                for tl in range(4):
                    k.tr(pbk[:, tl * 128:(tl + 1) * 128], RB[tl][:, j * 128:(j + 1) * 128], self.IDB[:])
                k.copy(dests[j], pbk[:, 0:512], eng=("act" if j % 2 else "dve"))
            ps = self.bank()
            for kc in range(8):
                k.mm(ps[:], WBb.v(wbv[:, kc, 128:256]), self.UT[:, kc, tsl], start=(kc == 0), stop=(kc == 7))
            k.act(VCT[:, tsl], ps[:], AF.Identity, bias=self.BCOL[:, l, BC_VC:BC_VC + 1])
        k.barrier()
        es1.close()

        es2 = ExitStack()
        W1K = k.sb("w1k", [128, 32, 128], BF16, es2)
        W1V = k.sb("w1v", [128, 32, 128], BF16, es2)
        W2K = k.sb("w2k", [128, 128], BF16, es2)
        W2V = k.sb("w2v", [128, 64], BF16, es2)
        PEK = k.sb("pek", [128, 32], BF16, es2)
        PEV = k.sb("pev", [128, 32], BF16, es2)
        k.memset(VCC[:, :, 64:65], 1.0)
        for half in range(2):
            hs = slice(64 * half, 64 * half + 64)
            k.dma(W1K.v(W1K.h[hs, :, :]), V(self.W16["cmp_wk1"].h[l].rearrange("(i d) h -> d i h", d=64), ()))
            k.dma(W1V.v(W1V.h[hs, :, :]), V(self.W16["cmp_wv1"].h[l].rearrange("(i d) h -> d i h", d=64), ()))
            k.dma(W2K.v(W2K.h[:, hs]), V(self.W16["cmp_wk2"].h[l], ()))
        k.dma(W2V[:], V(self.W16["cmp_wv2"].h[l], ()))
        self.cast_load(PEK, PEK.h[:, :], I["pe_kT"].h[l])
        self.cast_load(PEV, PEV.h[:, :], I["pe_vT"].h[l])
        CB = k.sb("cb", [128, 4], F32, es2)
        HID = k.sb("hid", [128, 4, 128], BF16, es2)
        for g in range(2):
            gs = slice(64 * g, 64 * g + 64)
            for kv, (W1, PE_, SRC) in enumerate(((W1K, PEK, KCT), (W1V, PEV, VCT))):
                idx = g * 2 + kv
                pcb = self.bank()
                for i in range(32):
                    k.mm(pcb[:, 0:1], W1.v(W1.h[gs, i, :]), PE_.v(PE_.h[gs, i:i + 1]), start=(i == 0), stop=(i == 31))
                k.copy(CB[:, idx:idx + 1], pcb[:, 0:1])
                ph = self.bank()
                for i in range(32):
                    k.mm(ph[:, 0:127], W1.v(W1.h[gs, i, :]), SRC.v(SRC.h[gs, i:i + 16 * 126 + 1:16]), start=(i == 0), stop=(i == 31))
                k.act(HID.v(HID.h[:, idx, 0:127]), ph[:, 0:127], AF.Silu, bias=CB[:, idx:idx + 1])
                po = self.bank()
                if kv == 0:
                    k.mm(po[:, 0:127], W2K[:], HID.v(HID.h[:, idx, 0:127]))
                    k.copy(KCC[g][:, 0:127], po[:, 0:127])
                else:
                    k.mm(po[0:127, 0:64], HID.v(HID.h[:, idx, 0:127]), W2V[:])
                    k.copy(VCC.v(VCC.h[:, g, 0:64]), po[0:127, 0:64])
        k.barrier()
        es2.close()
        es12.close()

        es3 = ExitStack()
        CMW = k.sb("cmw", [128, 8, 512], BF16, es3)
        CV = k.sb("cv", [127, S], BF16, es3)
        E = k.sb("e", [32, S], BF16, es3)
        OV = k.sb("ov", [127, 33], BF16, es3)
        VM = k.sb("vm", [128, 16, 32], F32, es3)
        ADDC = k.sb("addc", [128, 16, 32], F32, es3)
        k.dma(CMW[:], V(I["k_cmw"].h[:].rearrange("a p n -> p a n"), ()))
        k.dma(CV[:], I["k_cv"][:])
        k.dma(E[:], I["k_e"][:])
        k.dma(OV[:], I["k_ov"][:])
        k.dma(VM[:], I["k_vm"][:])
        k.dma(ADDC[:], I["k_addc"][:])
        NSELT = [k.sb("nselt%d" % g, [32, S], BF16, es3) for g in range(2)]
        PT = [k.sb("pt%d" % i, [128, 512], BF16, es3) for i in range(4)]
        OE = [k.sb("oe%d" % i, [65, 512], F32, es3) for i in range(4)]
        OEC = [k.sb("oec%d" % i, [65, 512], F32, es3) for i in range(4)]
        PCT = [k.sb("pct%d" % i, [127, 512], BF16, es3) for i in range(4)]
        YAs = [k.sb("ya%d" % i, [128, 4, 256], F32, es3) for i in range(2)]
        YAb = k.sb("yab16", [128, 4, 256], BF16, es3)
        TMP = k.sb("tmp", [128, 256], F32, es3)
        RS = k.sb("rs", [128, 4], F32, es3)
        RSC = k.sb("rsc", [128, 4, 4], F32, es3)
        IMPN = k.sb("impn", [128, 16, 32], F32, es3)
        IMP = k.sb("imp", [128, 4, 32], F32, es3)
        M8 = k.sb("m8", [128, 4, 8], F32, es3)
        LT = k.sb("lt", [128, 4, 32], F32, es3)
        NS = k.sb("ns", [128, 4, 32], BF16, es3)
        pti = [0]

        def next_pt():
            p = PT[pti[0] % 4]
            pti[0] += 1
            return p

        def combine(br, g, tc, first, OEs, YA):
            for tl in range(4):
                tt = tc * 4 + tl
                tp = self.mbank()
                for r in range(4):
                    k.tr(tp[:, r * 65:(r + 1) * 65], OEs[r][0:65, tl * 128:(tl + 1) * 128], self.IDF[0:65, 0:65])
                tpv = tp.h[:, 0:260].rearrange("p (r c) -> p r c", c=65)
                rs = RSC[:, tl, :] if br == 0 else RS[:]
                k.ts(rs, tp.v(tpv[:, :, 64]), 1e-30, ALU.max)
                k.recip(rs, rs)
                gv = GATE.v(GATE.h[:, tt, g * 12:(g + 1) * 12].rearrange("p (r b) -> p r b", b=3)[:, :, br])
                k.tt(RS[:], rs, gv, ALU.mult)
                rsb = RS.v(RS.h[:, :].unsqueeze(2).to_broadcast([128, 4, 64]))
                dst = YA.v(YA.h[:, tl, :].rearrange("p (r d) -> p r d", d=64))
                if first:
                    k.tt(dst, tp.v(tpv[:, :, 0:64]), rsb, ALU.mult)
                else:
                    tmpv = TMP.v(TMP.h[:, :].rearrange("p (r d) -> p r d", d=64))
                    k.tt(tmpv, tp.v(tpv[:, :, 0:64]), rsb, ALU.mult)
                    k.tt(YA[:, tl, :], YA[:, tl, :], TMP[:], ALU.add, eng="pool")

        stages = [(g, tc) for g in range(2) for tc in range(4)]

        def hb(g, r):
            h = 4 * g + r
            return h // 2, slice(64 * (h % 2), 64 * (h % 2) + 64)

        def stage_c1(idx):
            g, tc = stages[idx]
            YA = YAs[idx % 2]
            tsl = slice(tc * 512, (tc + 1) * 512)
            for r in range(4):
                pair, bs = hb(g, r)
                q = QT.v(QT.h[bs, pair, tsl])
                sc = self.bank()
                k.mm(sc[0:127, :], KCC[g].v(KCC[g].h[bs, 0:127]), q, start=True, stop=False)
                k.mm(sc[0:127, :], self.IDB[0:127, 0:127], CV[:, tsl], start=False, stop=True)
                k.act(PCT[r][:], sc[0:127, :], AF.Exp, scale=0.125)
                oa = self.abank()
                k.mm(oa[0:65, :], VCC.v(VCC.h[:, g, :]), PCT[r][:])
                k.copy(OEC[r][:], oa[0:65, :], eng="act")
            combine(0, g, tc, True, OEC, YA)
            pi = self.mbank()
            for tl in range(4):
                for r in range(4):
                    c0 = (tl * 4 + r) * 32
                    k.mm(pi[:, c0:c0 + 32], PCT[r][:, tl * 128:(tl + 1) * 128], OV[:, 0:32])
            k.tt(IMPN.v(IMPN.h[:, :, :]), pi.v(pi.h[:, :].rearrange("p (a j) -> p a j", j=32)),
                 RSC.v(RSC.h[:, :, :].rearrange("p a b -> p (a b)").unsqueeze(2).to_broadcast([128, 16, 32])), ALU.mult)
            iv = IMPN.h[:, :, :].rearrange("p (t r) j -> p t r j", r=4)
            k.tt(IMP[:], IMPN.v(iv[:, :, 0, :]), IMPN.v(iv[:, :, 1, :]), ALU.add)
            k.tt(IMP[:], IMP[:], IMPN.v(iv[:, :, 2, :]), ALU.add)
            k.tt(IMP[:], IMP[:], IMPN.v(iv[:, :, 3, :]), ALU.add)
            k.tt(IMP[:], IMP[:], VM[:, tc * 4:(tc + 1) * 4, :], ALU.mult)
            k.tt(IMP[:], IMP[:], ADDC[:, tc * 4:(tc + 1) * 4, :], ALU.add)
            for tl in range(4):
                k.op("dve", self.nc.vector.max, [IMP[:]], [M8[:]], M8.h[:, tl, :], IMP.h[:, tl, :])
            k.tt(LT[:], IMP[:], M8.v(M8.h[:, :, 7:8].to_broadcast([128, 4, 32])), ALU.is_lt)
            k.ts(NS[:], LT[:], NEG, ALU.mult)

        def stage_c2(idx):
            g, tc = stages[idx]
            pbk = self.bbank()
            for tl in range(4):
                k.tr(pbk[0:32, tl * 128:(tl + 1) * 128], NS[:, tl, :], self.IDB[:])
            k.copy(NSELT[g][:, tc * 512:(tc + 1) * 512], pbk[0:32, 0:512], eng="act")

        def stage_sw(idx):
            g, tc = stages[idx]
            YA = YAs[idx % 2]
            tsl = slice(tc * 512, (tc + 1) * 512)
            jobs = []
            for br in (1, 2):
                if br == 1:
                    kcs = list(range(0, 4 * tc + 4))
                else:
                    kcs = list(range(max(0, 4 * tc - 4), 4 * tc + 4))
                for r in range(4):
                    for kc in kcs:
                        jobs.append((br, r, kc, kc == kcs[0], kc == kcs[-1]))
            state = {}

            def cols(j):
                br, r, kc, first, last = j
                if kc >= 4 * tc:
                    return 128 * (kc - 4 * tc), 512
                if br == 2:
                    return 0, 128 * (kc - (4 * tc - 4) + 1)
                return 0, 512

            def scores(j):
                br, r, kc, first, last = j
                pair, bs = hb(g, r)
                c0, c1 = cols(j)
                qs = slice(tc * 512 + c0, tc * 512 + c1)
                q = QT.v(QT.h[bs, pair, qs])
                ksl = slice(kc * 128, (kc + 1) * 128)
                sc = self.sbank()
                if br == 1:
                    diag = kc >= 4 * tc
                    k.mm(sc[:, c0:c1], KST[g].v(KST[g].h[bs, ksl]), q, start=True, stop=False)
                    k.mm(sc[:, c0:c1], E[:, ksl], NSELT[g][:, qs], start=False, stop=not diag)
                    if diag:
                        k.mm(sc[:, c0:c1], self.IDB[:], CMW[:, kc - 4 * tc, c0:c1], start=False, stop=True)
                else:
                    mi = (kc - 4 * tc) if kc >= 4 * tc else (4 + kc - (4 * tc - 4))
                    k.mm(sc[:, c0:c1], KWT[g].v(KWT[g].h[bs, ksl]), q, start=True, stop=False)
                    k.mm(sc[:, c0:c1], self.IDB[:], CMW[:, mi, c0:c1], start=False, stop=True)
                state[j] = sc

            def rest(j):
                br, r, kc, first, last = j
                c0, c1 = cols(j)
                sc = state.pop(j)
                if first:
                    state[("oa", br, r)] = self.abank()
                oa = state[("oa", br, r)]
                pt = next_pt()
                k.act(pt[:, c0:c1], sc[:, c0:c1], AF.Exp, scale=0.125)
                Vt = VS if br == 1 else VW
                k.mm(oa[0:65, c0:c1], Vt.v(Vt.h[:, kc, g, :]), pt[:, c0:c1], start=first, stop=last, skip_group_check=True)
                if last:
                    k.copy(OE[r][:], oa[0:65, :], eng="act")
                    if r == 3:
                        combine(br, g, tc, False, OE, YA)

            LA = 2
            for i in range(min(LA, len(jobs))):
                scores(jobs[i])
            for i in range(len(jobs)):
                if i + LA < len(jobs):
                    scores(jobs[i + LA])
                rest(jobs[i])
            k.copy(YAb[:], YA[:], eng="act")
            for fcl in range(2):
                pbk = self.bbank()
                for tl in range(4):
                    k.tr(pbk[:, tl * 128:(tl + 1) * 128], YAb[:, tl, fcl * 128:(fcl + 1) * 128], self.IDB[:])
                k.copy(self.YAB.sub("a", (slice(None), 2 * g + fcl, tsl)), pbk[:, 0:512])

        stage_c1(0)
        stage_c2(0)
        for idx in range(len(stages)):
            if idx + 1 < len(stages):
                stage_c1(idx + 1)
            stage_sw(idx)
            if idx + 1 < len(stages):
                stage_c2(idx + 1)
        k.barrier()
        es3.close()
        es.close()

    def hgrn(self, s, l):
        k, I = self.k, self.I
        es = ExitStack()
        sb = lambda n, sh, dt: k.sb(n, sh, dt, es)
        win = self.W16["w_in"].h
        RST = sb("rst", [128, S], F32)
        BDM = sb("bdm", [128, 128], F32)
        k.dma(RST[:], I["k_rst"][:])
        k.dma(BDM[:], I["k_bdm"][:])
        BROWI = sb("browi", [128, 512], F32)
        k.dma(BROWI[:], V(I["b_in"].h[l, 2328:2840].partition_broadcast(128), ()))
        VH = sb("vh", [128, 16, 512], BF16)
        wi, wiv = self.load_w(win[l, :, 2328:2840], 8, 512)
        for tt in range(16):
            ps = self.bank()
            for kc in range(8):
                k.mm(ps[:], self.UT[:, kc, tt * 128:(tt + 1) * 128], wi.v(wiv[:, kc, :]), start=(kc == 0), stop=(kc == 7))
            k.tt(VH[:, tt, :], ps[:], BROWI[:], ALU.add)
        QPs = [sb("qp%d" % i, [128, S], BF16) for i in range(2)]
        KPs = [sb("kp%d" % i, [128, S], BF16) for i in range(2)]
        KPTs = [sb("kpt%d" % i, [128, 16, 128], BF16) for i in range(2)]
        GSs = [sb("gs%d" % i, [128, S], BF16) for i in range(2)]
        EBLs = [sb("ebl%d" % i, [128, 32], F32) for i in range(2)]
        SB16s = [sb("sb16%d" % i, [128, 32, 128], BF16) for i in range(2)]
        F1 = sb("f1", [128, S], F32)
        F2 = sb("f2", [128, S], F32)
        F3 = sb("f3", [128, S], F32)
        EB = sb("eb", [128, S], F32)
        SST = sb("sst", [128, 128], F32)
        STMP = sb("stmp", [128, 128], F32)
        ATM = [sb("atm%d" % i, [128, 128], BF16) for i in range(2)]
        O2 = sb("o2", [128, 512], F32)
        RSTD = sb("rstd", [128, 512], F32)
        T1 = sb("t1", [128, 512], F32)

        def stage_p(hd):
            QP, KP, GS, EBL = QPs[hd % 2], KPs[hd % 2], GSs[hd % 2], EBLs[hd % 2]
            wq, wqv = self.load_w(win[l, :, 1304 + 128 * hd:1304 + 128 * (hd + 1)], 8, 128)
            wf, wfv = self.load_w(win[l, :, 1816 + 128 * hd:1816 + 128 * (hd + 1)], 8, 128)
            wg, wgv = self.load_w(win[l, :, 2840 + 128 * hd:2840 + 128 * (hd + 1)], 8, 128)
            for tc in range(4):
                tsl = slice(tc * 512, (tc + 1) * 512)
                pq, pf_, pg = self.bank(), self.bank(), self.bank()
                for (p_, w_, wv_) in ((pq, wq, wqv), (pf_, wf, wfv), (pg, wg, wgv)):
                    for kc in range(8):
                        k.mm(p_[:], w_.v(wv_[:, kc, :]), self.UT[:, kc, tsl], start=(kc == 0), stop=(kc == 7))
                k.act(F1[:, tsl], pq[:], AF.Silu, bias=self.BCOL[:, l, BC_QB + hd:BC_QB + hd + 1])
                k.act(GS[:, tsl], pg[:], AF.Silu, bias=self.BCOL[:, l, BC_GB + hd:BC_GB + hd + 1])
                k.act(F2[:, tsl], pf_[:], AF.Sigmoid, bias=self.BCOL[:, l, BC_FB + hd:BC_FB + hd + 1])
            k.ts(F2[:], F2[:], self.OML[:, l, hd:hd + 1], ALU.mult, self.LB[:, l, hd:hd + 1], ALU.add)
            k.act(F3[:], F2[:], AF.Ln)
            k.ts(F2[:], F2[:], -1.0, ALU.mult, 1.0, ALU.add)
            k.op("dve", self.nc.vector.tensor_tensor_scan, [RST[:], F3[:]], [EB[:]], EB[:].ap, RST[:].ap, F3[:].ap, 0.0,
                 ALU.mult, ALU.add)
            k.act(F3[:], EB[:], AF.Exp, scale=-1.0)
            k.act(EB[:], EB[:], AF.Exp)
            k.stt(QP[:], F1[:], 128.0 ** -0.5, EB[:], ALU.mult, ALU.mult)
            k.tt(KP[:], F2[:], F3[:], ALU.mult)
            k.copy(EBL[:], EB[:, 63:S:64], eng="pool")

        def stage_r(hd):
            QP, KP, GS, EBL = QPs[hd % 2], KPs[hd % 2], GSs[hd % 2], EBLs[hd % 2]
            KPT, SB16 = KPTs[hd % 2], SB16s[hd % 2]
            for tc in range(4):
                pbk = self.bbank()
                for tl in range(4):
                    tt = tc * 4 + tl
                    k.tr(pbk[:, tl * 128:(tl + 1) * 128], KP[:, tt * 128:(tt + 1) * 128], self.IDB[:])
                k.copy(KPT.v(KPT.h[:, tc * 4:(tc + 1) * 4, :].rearrange("p a b -> p (a b)")), pbk[:, 0:512], eng="act")
            k.memset(SST[:], 0.0, eng="dve")
            k.memset(SB16[:, 0, :], 0.0, eng="dve")
            vcols = slice(hd * 128, (hd + 1) * 128)
            for c4 in range(8):
                pmh = [self.bank(), self.bank()]
                for ci in range(4):
                    c = c4 * 4 + ci
                    tt, half = c // 2, c % 2
                    hs = slice(64 * half, 64 * half + 64)
                    k.mm(pmh[half][:, (ci // 2) * 128:(ci // 2 + 1) * 128], KPT.v(KPT.h[hs, tt, :]), VH.v(VH.h[hs, tt, vcols]))
                for ci in range(4):
                    c = c4 * 4 + ci
                    if c == 31:
                        break
                    half = c % 2
                    k.tt(STMP[:], pmh[half][:, (ci // 2) * 128:(ci // 2 + 1) * 128], SST[:], ALU.add)
                    k.ts(SST[:], STMP[:], EBL[:, c:c + 1], ALU.mult)
                    k.copy(SB16[:, c + 1, :], SST[:], eng="act")
            for tc in range(4):
                tsl = slice(tc * 512, (tc + 1) * 512)
                po = self.abank()
                for tl in range(4):
                    tt = tc * 4 + tl
                    t_sl = slice(tt * 128, (tt + 1) * 128)
                    pa = self.bank()
                    k.mm(pa[:, 0:128], KP[:, t_sl], QP[:, t_sl])
                    atm = ATM[tt % 2]
                    k.tt(atm[:], pa[:, 0:128], BDM[:], ALU.mult)
                    osl = slice(tl * 128, (tl + 1) * 128)
                    k.mm(po[:, osl], VH.v(VH.h[:, tt, vcols]), atm[:], start=True, stop=False)
                    for half in range(2):
                        c = 2 * tt + half
                        k.mm(po[:, tl * 128 + 64 * half: tl * 128 + 64 * half + 64], SB16[:, c, :],
                             QP[:, 64 * c:64 * c + 64], start=False, stop=(half == 1))
                k.act(O2[:], po[:], AF.Square)
                pss = self.bank()
                k.mm(pss[:], self.ONESF[:], O2[:])
                k.act(RSTD[:], pss[:], AF.Ln, scale=1.0 / 128.0, bias=self.EPS[:, 1:2])
                k.act(RSTD[:], RSTD[:], AF.Exp, scale=-0.5)
                k.tt(T1[:], po[:], RSTD[:], ALU.mult)
                k.stt(self.YAB.sub("b", (slice(None), 4 + hd, tsl)), T1[:], self.NORMG[:, l:l + 1], GS[:, tsl], ALU.mult, ALU.mult)

        stage_p(0)
        for hd in range(4):
            if hd + 1 < 4:
                stage_p(hd + 1)
            stage_r(hd)
        k.barrier()
        es.close()

    def layernorm(self, z, gB, bB, es_tmp):
        k = self.k
        st, mv, rstd = es_tmp
        zv = z.res
        for j in range(2):
            k.op("dve", self.nc.vector.bn_stats, [z], [st[:]], st.h[:, j * 6:(j + 1) * 6], z.ap[:, j * 512:(j + 1) * 512])
        k.op("dve", self.nc.vector.bn_aggr, [st[:]], [mv[:]], mv[:].ap, st[:].ap)
        k.act(rstd[:], mv[:, 1:2], AF.Sqrt, bias=self.EPS[:, 0:1])
        k.recip(rstd[:], rstd[:])
        k.ts(z, z, mv[:, 0:1], ALU.subtract, rstd[:], ALU.mult)
        k.tt(z, z, gB[:], ALU.mult, eng="pool")
        k.tt(z, z, bB[:], ALU.add, eng="pool")

    def tail(self, s, l, src):
        k, I = self.k, self.I
        es = ExitStack()
        sb = lambda n, sh, dt: k.sb(n, sh, dt, es)
        win = self.W16["w_in"].h
        MG = sb("mg", [128, 8, S], BF16)
        SIGA = sb("siga", [128, 512], BF16)
        SIGB = sb("sigb", [128, 512], BF16)
        M1 = sb("m1", [128, 512], F32)
        M2 = sb("m2", [128, 512], F32)
        BRB = sb("brb", [128, 4, 128], BF16)
        for nch in range(8):
            csl = slice(nch * 128, (nch + 1) * 128)
            wga, wgav = self.load_w(win[l, :, 3352 + 128 * nch:3352 + 128 * (nch + 1)], 8, 128)
            wgb, wgbv = self.load_w(win[l, :, 4376 + 128 * nch:4376 + 128 * (nch + 1)], 8, 128)
            wbr, wbrv = self.load_w(self.W16["w_branch_a"].h[l, :, csl], 4, 128)
            wbb = BRB
            k.dma(wbb[:], V(self.W16["w_branch_b"].h[l, :, csl].rearrange("(a p) n -> p a n", p=128), ()))
            for tc in range(4):
                tsl = slice(tc * 512, (tc + 1) * 512)
                pga, pgb, pba, pbb = self.bank(), self.bank(), self.bank(), self.bank()
                for kc in range(8):
                    k.mm(pga[:], wga.v(wgav[:, kc, :]), self.UT[:, kc, tsl], start=(kc == 0), stop=(kc == 7))
                for kc in range(8):
                    k.mm(pgb[:], wgb.v(wgbv[:, kc, :]), self.UT[:, kc, tsl], start=(kc == 0), stop=(kc == 7))
                for kc in range(4):
                    k.mm(pba[:], wbr.v(wbrv[:, kc, :]), self.YAB.sub("a", (slice(None), kc, tsl)), start=(kc == 0), stop=(kc == 3))
                for kc in range(4):
                    k.mm(pbb[:], wbb[:, kc, :], self.YAB.sub("b", (slice(None), 4 + kc, tsl)), start=(kc == 0), stop=(kc == 3))
                k.act(SIGA[:], pga[:], AF.Sigmoid, bias=self.BCOL[:, l, BC_GMA + nch:BC_GMA + nch + 1])
                k.act(SIGB[:], pgb[:], AF.Sigmoid, bias=self.BCOL[:, l, BC_GMB + nch:BC_GMB + nch + 1])
                k.tt(M1[:], pba[:], SIGA[:], ALU.mult)
                k.tt(M2[:], pbb[:], SIGB[:], ALU.mult)
                k.tt(MG[:, nch, tsl], M1[:], M2[:], ALU.add, eng="pool")
        self.tapv("mgT", MG[:], [128, 8, S], BF16)
        if self.stop == "merge":
            k.barrier()
            es.close()
            return
        G1 = sb("g1", [128, D], F32)
        G2 = sb("g2", [128, D], F32)
        LG1 = sb("lg1", [128, D], F32)
        LB1 = sb("lb1", [128, D], F32)
        LG2 = sb("lg2", [128, D], F32)
        LB2 = sb("lb2", [128, D], F32)
        k.dma(G1[:], V(self.grow.h[l, s, 0, :].partition_broadcast(128), self.grow[:].res))
        k.dma(G2[:], V(self.grow.h[l, s, 1, :].partition_broadcast(128), self.grow[:].res))
        k.dma(LG1[:], V(I["ln1_g"].h[l, :].partition_broadcast(128), ()))
        k.dma(LB1[:], V(I["ln1_b"].h[l, :].partition_broadcast(128), ()))
        k.dma(LG2[:], V(I["ln2_g"].h[l, :].partition_broadcast(128), ()))
        k.dma(LB2[:], V(I["ln2_b"].h[l, :].partition_broadcast(128), ()))
        XTs = [[sb("xt%d_%d" % (j, i), [128, D], F32) for i in range(4)] for j in range(2)]
        ST = sb("st", [128, 12], F32)
        MV = sb("mv", [128, 2], F32)
        RSD = sb("rsd", [128, 1], F32)
        TMPZ = sb("tmpz", [128, 512], F32)
        Y2 = sb("y2", [128, 512], F32)
        last_layer = (l == self.layers[-1]) and self.last
        dst = self.out if last_layer else self.xres
        x1d = self.xres
        UT2 = self.UT

        wo = []

        def load_wo():
            wo.clear()
            for nh in range(2):
                wo.append(self.load_w(self.W16["w_out"].h[l, :, nh * 512:(nh + 1) * 512], 8, 512))

        def e_chunk(tc):
            XT = XTs[tc % 2]
            for tl in range(4):
                tt = tc * 4 + tl
                x = XT[tl]
                k.dma(x[:], V(src.h[s, tt * 128:(tt + 1) * 128, :], src[:].res))
                for nh in range(2):
                    ps = self.bank()
                    w_, wv_ = wo[nh]
                    for kc in range(8):
                        k.mm(ps[:], MG[:, kc, tt * 128:(tt + 1) * 128], w_.v(wv_[:, kc, :]), start=(kc == 0), stop=(kc == 7))
                    k.tt(TMPZ[:], ps[:], G1[:, nh * 512:(nh + 1) * 512], ALU.mult)
                    k.stt(x[:, nh * 512:(nh + 1) * 512], x[:, nh * 512:(nh + 1) * 512], ALPHA, TMPZ[:], ALU.mult, ALU.add)
                self.layernorm(x[:], LG1, LB1, (ST, MV, RSD))
                k.dma(V(x1d.h[s, tt * 128:(tt + 1) * 128, :], x1d[:].res), x[:])
            for fc in range(8):
                ps = self.bank()
                for tl in range(4):
                    k.tr(ps[:, tl * 128:(tl + 1) * 128], XT[tl][:, fc * 128:(fc + 1) * 128], self.IDF[:])
                k.act(UT2[:, fc, tc * 512:(tc + 1) * 512], ps[:], AF.Identity, bias=self.MODT[:, l, 24 + fc, s:s + 1],
                      scale=self.MODT[:, l, 32 + fc, s:s + 1])

        hA = self.YAB.h[:, :, :].rearrange("p a (b c) -> p (a b) c", c=1024)
        hB = MG.h[:, :, :].rearrange("p a (b c) -> p (a b) c", c=1024)
        resA = ((self.YAB.name, "a"), (self.YAB.name, "b"))
        resB = (MG.name,)

        def h1(kc, cs):
            return V(hA[:, kc, cs], resA) if kc < 16 else V(hB[:, kc - 16, cs], resB)

        XT8 = XTs[0] + XTs[1]

        def f_load(big):
            for i8 in range(8):
                tt = big * 8 + i8
                k.dma(XT8[i8][:], V(x1d.h[s, tt * 128:(tt + 1) * 128, :], x1d[:].res))

        def f1(big, n4s):
            for n4 in n4s:
                w_, wv_ = self.load_w(self.W16["w_mlp1"].h[l, :, n4 * 512:(n4 + 1) * 512], 8, 512)
                for j in range(4):
                    nch = n4 * 4 + j
                    for th in range(2):
                        ps = self.bank()
                        tsl = slice(big * 1024 + th * 512, big * 1024 + (th + 1) * 512)
                        for kc in range(8):
                            k.mm(ps[:], w_.v(wv_[:, kc, j * 128:(j + 1) * 128]), UT2[:, kc, tsl], start=(kc == 0), stop=(kc == 7))
                        k.act(M1[:], ps[:], AF.Relu)
                        k.tt(h1(nch, slice(th * 512, (th + 1) * 512)), M1[:], M1[:], ALU.mult, eng="pool")
        def f2(big):
            for nch2 in range(8):
                w_, wv_ = self.load_w(self.W16["w_mlp2"].h[l, nch2], 32, 128, pretiled=True)
                cs = slice(nch2 * 128, (nch2 + 1) * 128)
                for th in range(2):
                    py = self.abank()
                    for kc in range(32):
                        k.mm(py[:], w_.v(wv_[:, kc, :]), h1(kc, slice(th * 512, (th + 1) * 512)), start=(kc == 0), stop=(kc == 31))
                    k.copy(Y2[:], py[:], eng="act")
                    ptr = self.bank()
                    for tl in range(4):
                        k.tr(ptr[:, tl * 128:(tl + 1) * 128], Y2[:, tl * 128:(tl + 1) * 128], self.IDF[:])
                    for tl in range(4):
                        x = XT8[th * 4 + tl]
                        k.tt(TMPZ[:, 0:128], ptr[:, tl * 128:(tl + 1) * 128], G2[:, cs], ALU.mult)
                        k.stt(x[:, cs], x[:, cs], ALPHA, TMPZ[:, 0:128], ALU.mult, ALU.add)
            for i8 in range(8):
                tt = big * 8 + i8
                x = XT8[i8]
                self.layernorm(x[:], LG2, LB2, (ST, MV, RSD))
                k.dma(V(dst.h[s, tt * 128:(tt + 1) * 128, :], dst[:].res), x[:])

        load_wo()
        e_chunk(0)
        e_chunk(1)
        f1(0, range(0, 4))
        load_wo()
        e_chunk(2)
        e_chunk(3)
        f1(0, range(4, 8))
        f_load(0)
        f2(0)
        f1(1, range(0, 8))
        f_load(1)
        f2(1)
        k.barrier()
        es.close()


def _consts():
    cmw = np.zeros((8, 128, 512), np.float32)
    kl = np.arange(128)[:, None]
    tl = np.arange(512)[None, :]
    for i in range(4):
        cmw[i] = np.where(128 * i + kl <= tl, 0.0, NEG)
        cmw[4 + i] = np.where(tl <= 128 * i + kl - 1, 0.0, NEG)
    n = np.arange(127)[:, None]
    t = np.arange(S)[None, :]
    cv = np.where(16 * n + 31 <= t, 0.0, NEG).astype(np.float32)
    j = np.arange(32)[:, None]
    e = (t // 64 == j).astype(np.float32)
    jj = np.arange(32)[None, :]
    ov = np.zeros((127, 33), np.float32)
    ov[:, :32] = ((16 * n < 64 * jj + 64) & (16 * n + 32 > 64 * jj)).astype(np.float32)
    ov[:, 32] = 1.0
    rst = np.ones((128, S), np.float32)
    rst[:, ::64] = 0.0
    s_ = np.arange(128)[:, None]
    t_ = np.arange(128)[None, :]
    bdm = ((s_ <= t_) & (s_ // 64 == t_ // 64)).astype(np.float32)
    ident = np.eye(128, dtype=np.float32)
    inv = (1.0 / (np.float32(10000.0) ** (np.arange(0, 64, 2, dtype=np.float32) / np.float32(64)))).astype(np.float32)
    ang = np.arange(S, dtype=np.float32)[:, None] * inv[None, :]
    cos = np.cos(ang).astype(np.float32).reshape(16, 128, 32).transpose(1, 0, 2)
    sin = np.sin(ang).astype(np.float32).reshape(16, 128, 32).transpose(1, 0, 2)
    pos = np.arange(S)
    tb = pos // 64
    jb = np.arange(32)
    valid = jb[None, :] <= tb[:, None]
    forced = valid & ((jb[None, :] == 0) | (jb[None, :] == tb[:, None]) | (jb[None, :] == tb[:, None] - 1))
    vm = (valid & ~forced).astype(np.float32).reshape(16, 128, 32).transpose(1, 0, 2)
    addc = np.where(forced, 1.0e4, np.where(valid, 0.0, -1.0)).astype(np.float32).reshape(16, 128, 32).transpose(1, 0, 2)
    import ml_dtypes
    bf = ml_dtypes.bfloat16
    c = dict(k_cmw=cmw.astype(bf), k_cv=cv.astype(bf), k_e=e.astype(bf), k_ov=ov.astype(bf), k_rst=rst, k_bdm=bdm,
             k_ident=ident, k_identb=ident.astype(bf), k_cos=cos, k_sin=sin, k_vm=vm, k_addc=addc)
    return {k_: np.ascontiguousarray(v) for k_, v in c.items()}


def prep_inputs(inputs, seqs):
    f = lambda a: np.ascontiguousarray(np.asarray(a, dtype=np.float32))
    m = {}
    m["x"] = f(inputs["x"][seqs])
    c = np.asarray(inputs["c"], np.float32)[seqs]
    m["c_l"] = f(c.reshape(len(seqs), 8, 128).transpose(2, 1, 0))
    for nme in ("w_in", "b_in", "cmp_wk1", "cmp_wk2", "cmp_wv1", "cmp_wv2", "w_branch_a", "w_branch_b", "w_out",
                "w_ada", "b_ada", "ln1_g", "ln1_b", "w_mlp1", "w_mlp2", "ln2_g", "ln2_b"):
        m[nme] = f(inputs[nme])
    b_in = np.asarray(inputs["b_in"], np.float32)
    m["bcols"] = f(np.stack([b_in[:, o:o + 128] for o in BCOL_OFFS], axis=-1).transpose(1, 0, 2))
    pek = np.asarray(inputs["cmp_pe_k"], np.float32).transpose(0, 2, 1)
    pev = np.asarray(inputs["cmp_pe_v"], np.float32).transpose(0, 2, 1)
    m["pe_kT"] = f(np.concatenate([pek, pek], axis=1))
    m["pe_vT"] = f(np.concatenate([pev, pev], axis=1))
    m["lb_l"] = f(np.asarray(inputs["hgrn_lb_logits"], np.float32).reshape(L, 4, 128).transpose(2, 0, 1))
    m["normg_l"] = f(np.asarray(inputs["hgrn_norm_g"], np.float32).T)
    m["b_adaT"] = f(np.asarray(inputs["b_ada"], np.float32).reshape(L, 48, 128).transpose(2, 0, 1))
    m.update(_consts())
    return m


def kernel(**inputs):
    n = 8
    prog = Prog()
    in_maps = [prep_inputs(inputs, list(range(c * NSEQ, (c + 1) * NSEQ))) for c in range(n)]
    res = run_bass_kernel_spmd(prog.nc, in_maps, core_ids=list(range(n)))
    out = np.concatenate([np.asarray(r["out"], np.float32) for r in res.results], axis=0)
    return out
```

```python
from contextlib import ExitStack
import numpy as np
import concourse.bass as bass
import concourse.mybir as mybir
from concourse.bass_utils import run_bass_kernel_spmd

F32 = mybir.dt.float32
BF16 = mybir.dt.bfloat16
AF = mybir.ActivationFunctionType
ALU = mybir.AluOpType

D = 1024
S = 2048
L = 2
NSEQ = 4
N_IN = 5400
NEG = -30000.0
ALPHA = (2 * L) ** 0.25
LN_EPS = 1e-5
RMS_EPS = 1e-6
BCOL_OFFS = [640] + [1304 + 128 * i for i in range(4)] + [1816 + 128 * i for i in range(4)] + \
    [2840 + 128 * i for i in range(4)] + [3352 + 128 * i for i in range(8)] + [4376 + 128 * i for i in range(8)]
BC_VC, BC_QB, BC_FB, BC_GB, BC_GMA, BC_GMB = 0, 1, 5, 9, 13, 21


class V:
    __slots__ = ("ap", "res")

    def __init__(self, ap, res):
        self.ap = ap
        self.res = res


class T:
    def __init__(self, name, handle):
        self.name = name
        self.h = handle

    def __getitem__(self, key):
        return V(self.h[key], (self.name,))

    def sub(self, skey, key=slice(None)):
        return V(self.h[key], ((self.name, skey),))

    def v(self, ap, skey=None):
        return V(ap, ((self.name,) if skey is None else ((self.name, skey),)))


class HalfBank:
    def __init__(self, t, i):
        self.t = t
        self.i = i

    def __getitem__(self, key):
        rows, cols = key
        c0 = (cols.start or 0) + 512 * self.i
        c1 = (cols.stop if cols.stop is not None else 512) + 512 * self.i
        return V(self.t.h[rows, c0:c1], ((self.t.name, self.i),))


class K:
    def __init__(self, nc):
        self.nc = nc
        self.es = ExitStack()
        self.engs = {"pe": nc.tensor, "act": nc.scalar, "dve": nc.vector, "pool": nc.gpsimd, "sp": nc.sync}
        self.sem = {}
        self.cnt = {}
        for n in ("pe", "act", "dve", "pool"):
            self.sem[n] = self.es.enter_context(nc.semaphore("s_" + n))
            self.cnt[n] = 0
        self.ndma = 32
        self.dsem = [self.es.enter_context(nc.semaphore("s_dma%d" % i)) for i in range(self.ndma)]
        self.dcnt = [0] * self.ndma
        self.dnext = 0
        self.seen = {n: {} for n in self.engs}
        self.writers = {}
        self.readers = {}
        self.n_inst = 0
        self.n_wait = 0
        self.taps = {}
        self.uid = 0

    def sb(self, name, shape, dtype, es=None):
        self.uid += 1
        name = "%s_%d" % (name, self.uid)
        h = (es or self.es).enter_context(self.nc.sbuf_tensor(name, list(shape), dtype))
        return T(name, h)

    def ps(self, name, shape, dtype=F32):
        h = self.es.enter_context(self.nc.psum_tensor(name, list(shape), dtype))
        return T(name, h)

    def dram(self, name, shape, dtype, kind):
        h = self.nc.dram_tensor(name, list(shape), dtype, kind=kind)
        return T(name, h)

    def _semobj(self, semkey):
        return self.sem[semkey] if isinstance(semkey, str) else self.dsem[semkey]

    def _deps(self, eng, reads, writes):
        deps = {}

        def add(d, same_ok):
            semkey, val, e = d
            if e == eng and same_ok:
                return
            if deps.get(semkey, 0) < val:
                deps[semkey] = val

        for r in reads:
            w = self.writers.get(r)
            if w is not None:
                add(w, eng == "pe")
        for r in writes:
            w = self.writers.get(r)
            if w is not None:
                add(w, eng == "pe")
            for rd in self.readers.get(r, ()):
                add(rd, eng == "pe")
        seen = self.seen[eng]
        e = self.engs[eng]
        for semkey, val in deps.items():
            if seen.get(semkey, 0) >= val:
                continue
            e.wait_ge(self._semobj(semkey), val)
            seen[semkey] = val
            self.n_wait += 1

    def _commit(self, tok, reads, writes):
        for r in reads:
            lst = self.readers.setdefault(r, [])
            lst[:] = [x for x in lst if x[0] != tok[0]]
            lst.append(tok)
        for r in writes:
            self.writers[r] = tok
            self.readers[r] = []

    @staticmethod
    def _res(vs):
        out = []
        for v in vs:
            if v is None:
                continue
            out.extend(v.res)
        return out

    def op(self, eng, fn, reads, writes, *args, signal=True, **kw):
        rr, ww = self._res(reads), self._res(writes)
        self._deps(eng, rr, ww)
        ins = fn(*args, **kw)
        if signal:
            self.cnt[eng] += 1
            ins.then_inc(self.sem[eng], 1)
            tok = (eng, self.cnt[eng], eng)
        else:
            tok = (eng, self.cnt[eng] + 1, eng)
        self._commit(tok, rr, ww)
        self.n_inst += 1
        return ins

    def dma(self, out, in_, queue="sp", **kw):
        rr, ww = self._res([in_]), self._res([out])
        i = self.dnext
        self.dnext = (self.dnext + 1) % self.ndma
        if self.dcnt[i] > 0 and self.seen[queue].get(i, 0) < self.dcnt[i]:
            self.engs[queue].wait_ge(self.dsem[i], self.dcnt[i])
            self.seen[queue][i] = self.dcnt[i]
        self._deps(queue, rr, ww)
        ins = self.engs[queue].dma_start(out=out.ap, in_=in_.ap, **kw)
        self.dcnt[i] += 16
        ins.then_inc(self.dsem[i], 16)
        self._commit((i, self.dcnt[i], "dma"), rr, ww)
        self.n_inst += 1
        return ins

    def barrier(self):
        for en in ("pe", "act", "dve", "pool", "sp"):
            e = self.engs[en]
            for sk in ("pe", "act", "dve", "pool"):
                if sk != en and self.cnt[sk] > self.seen[en].get(sk, 0):
                    e.wait_ge(self.sem[sk], self.cnt[sk])
                    self.seen[en][sk] = self.cnt[sk]
            for i in range(self.ndma):
                if self.dcnt[i] > self.seen[en].get(i, 0):
                    e.wait_ge(self.dsem[i], self.dcnt[i])
                    self.seen[en][i] = self.dcnt[i]
        self.writers = {}
        self.readers = {}

    def mm(self, out, lhsT, rhs, start=True, stop=True, **kw):
        return self.op("pe", self.nc.tensor.matmul, [lhsT, rhs], [out], out.ap, lhsT.ap, rhs.ap,
                       signal=bool(stop), start=start, stop=stop, **kw)

    def tr(self, out, in_, ident):
        return self.op("pe", self.nc.tensor.transpose, [in_, ident], [out], out.ap, in_.ap, ident.ap)

    def act(self, out, in_, func, bias=None, scale=None):
        kw = {}
        rd = [in_]
        if bias is not None:
            if isinstance(bias, V):
                kw["bias"] = bias.ap
                rd.append(bias)
            else:
                kw["bias"] = bias
        if scale is not None:
            if isinstance(scale, V):
                kw["scale"] = scale.ap
                rd.append(scale)
            else:
                kw["scale"] = scale
        return self.op("act", self.nc.scalar.activation, rd, [out], out.ap, in_.ap, func, **kw)

    def tt(self, out, a, b, op, eng="dve"):
        e = self.engs[eng]
        return self.op(eng, e.tensor_tensor, [a, b], [out], out.ap, a.ap, b.ap, op)

    def ts(self, out, a, s1, op0, s2=None, op1=None, eng="dve"):
        e = self.engs[eng]
        rd = [a]
        a1, a2 = s1, s2
        if isinstance(s1, V):
            rd.append(s1)
            a1 = s1.ap
        if isinstance(s2, V):
            rd.append(s2)
            a2 = s2.ap
        kw = {}
        if op1 is not None:
            kw["op1"] = op1
        return self.op(eng, e.tensor_scalar, rd, [out], out.ap, a.ap, a1, a2, op0, **kw)

    def stt(self, out, a, s, b, op0, op1):
        rd = [a, b]
        sv = s
        if isinstance(s, V):
            rd.append(s)
            sv = s.ap
        return self.op("dve", self.nc.vector.scalar_tensor_tensor, rd, [out], out.ap, a.ap, sv, b.ap, op0, op1)

    def copy(self, out, in_, eng="dve"):
        if eng == "act":
            return self.op("act", self.nc.scalar.copy, [in_], [out], out.ap, in_.ap)
        e = self.engs[eng]
        return self.op(eng, e.tensor_copy, [in_], [out], out.ap, in_.ap)

    def memset(self, out, val, eng="pool"):
        e = self.engs[eng]
        return self.op(eng, e.memset, [], [out], out.ap, val)

    def recip(self, out, in_):
        return self.op("dve", self.nc.vector.reciprocal, [in_], [out], out.ap, in_.ap)

    def tap(self, name, view, shape, dtype=F32):
        t = self.dram("tap_" + name, shape, dtype, "ExternalOutput")
        self.dma(t[:], view, queue="sp")
        self.taps[name] = t

    def finish(self):
        for i in range(self.ndma):
            if self.dcnt[i] > 0:
                self.nc.sync.wait_ge(self.dsem[i], self.dcnt[i])

    def close(self):
        self.es.close()


class Prog:
    def __init__(self, nseq=NSEQ, layers=(0, 1), first=True, last=True, taps=(), stop=None, skip=()):
        self.nseq = nseq
        self.layers = layers
        self.first = first
        self.last = last
        self.want = set(taps)
        self.stop = stop
        self.skip = set(skip)
        nc = bass.Bass("TRN2", target_bir_lowering=False)
        self.nc = nc
        k = K(nc)
        self.k = k
        I = {}
        self.I = I

        def inp(name, shape, dt=F32):
            I[name] = k.dram(name, shape, dt, "ExternalInput")

        inp("x", [nseq, S, D])
        inp("c_l", [128, 8, nseq])
        inp("w_in", [L, D, N_IN])
        inp("b_in", [L, N_IN])
        inp("bcols", [128, L, len(BCOL_OFFS)])
        inp("pe_kT", [L, 128, 32])
        inp("pe_vT", [L, 128, 32])
        inp("cmp_wk1", [L, 2048, 128])
        inp("cmp_wk2", [L, 128, 64])
        inp("cmp_wv1", [L, 2048, 128])
        inp("cmp_wv2", [L, 128, 64])
        inp("lb_l", [128, L, 4])
        inp("normg_l", [128, L])
        inp("w_branch_a", [L, 512, D])
        inp("w_branch_b", [L, 512, D])
        inp("w_out", [L, D, D])
        inp("w_ada", [L, D, 6 * D])
        inp("b_ada", [L, 6 * D])
        inp("b_adaT", [128, L, 48])
        inp("ln1_g", [L, D])
        inp("ln1_b", [L, D])
        inp("w_mlp1", [L, D, 4 * D])
        inp("w_mlp2", [L, 4 * D, D])
        inp("ln2_g", [L, D])
        inp("ln2_b", [L, D])
        inp("k_cmw", [8, 128, 512], BF16)
        inp("k_cv", [127, S], BF16)
        inp("k_e", [32, S], BF16)
        inp("k_ov", [127, 33], BF16)
        inp("k_rst", [128, S])
        inp("k_bdm", [128, 128])
        inp("k_ident", [128, 128])
        inp("k_identb", [128, 128], BF16)
        inp("k_cos", [128, 16, 32])
        inp("k_sin", [128, 16, 32])
        inp("k_vm", [128, 16, 32])
        inp("k_addc", [128, 16, 32])
        self.W16 = {}
        for nme, shp in (("w_in", [L, D, N_IN]), ("w_branch_a", [L, 512, D]), ("w_branch_b", [L, 512, D]),
                         ("w_out", [L, D, D]), ("w_mlp1", [L, D, 4 * D]), ("cmp_wk1", [L, 2048, 128]),
                         ("cmp_wv1", [L, 2048, 128]), ("cmp_wk2", [L, 128, 64]), ("cmp_wv2", [L, 128, 64])):
            self.W16[nme] = k.dram("s16_" + nme, shp, BF16, "Internal")
        self.W16["w_mlp2"] = k.dram("s16_w_mlp2", [L, 8, 128, 32, 128], BF16, "Internal")
        self.out = k.dram("out", [nseq, S, D], F32, "ExternalOutput")
        self.xres = k.dram("xres", [nseq, S, D], F32, "Internal")
        self.grow = k.dram("grow", [L, nseq, 2, D], F32, "Internal")

        self.pf = [k.ps("pf%d" % i, [128, 512], F32) for i in range(6)]
        self.pb = [k.ps("pb%d" % i, [128, 1024], BF16) for i in range(2)]
        self.psi = 0
        self.pmi = 0
        self.pfi = 0
        self.pai = 0
        self.pbi = 0

        self.IDF = k.sb("idf", [128, 128], F32)
        self.IDB = k.sb("idb", [128, 128], BF16)
        self.ONESF = k.sb("onesf", [128, 128], F32)
        self.MODT = k.sb("modt", [128, L, 48, nseq], F32)
        self.LB = k.sb("lb", [128, L, 4], F32)
        self.OML = k.sb("oml", [128, L, 4], F32)
        self.BCOL = k.sb("bcol", [128, L, len(BCOL_OFFS)], F32)
        self.NORMG = k.sb("normg", [128, L], F32)
        self.EPS = k.sb("eps", [128, 2], F32)
        self.UT = k.sb("ut", [128, 8, S], BF16)
        self.YAB = k.sb("yab", [128, 8, S], BF16)
        self.WB = [k.sb("wb%d" % i, [128, 4096], BF16) for i in range(3)]
        self.wbi = 0
        self.STG = [k.sb("stg%d" % i, [128, 256], F32) for i in range(2)]
        self.stgi = 0

        k.dma(self.IDF[:], I["k_ident"][:])
        k.dma(self.IDB[:], I["k_identb"][:])
        k.dma(self.BCOL[:], I["bcols"][:])
        k.dma(self.NORMG[:], I["normg_l"][:])
        k.memset(self.ONESF[:], 1.0)
        k.memset(self.EPS[:, 0:1], LN_EPS)
        k.memset(self.EPS[:, 1:2], RMS_EPS)

        self.prologue()
        for s in range(nseq):
            for l in layers:
                self.layer(s, l)
        k.finish()
        k.close()

    def bank(self):
        p = self.pf[self.pfi]
        self.pfi = (self.pfi + 1) % 4
        return p

    def sbank(self):
        p = self.pf[self.psi]
        self.psi = (self.psi + 1) % 3
        return p

    def mbank(self):
        return self.abank()

    def abank(self):
        p = self.pf[4 + self.pai]
        self.pai = (self.pai + 1) % 2
        return p

    def bbank(self):
        p = self.pb[self.pbi]
        self.pbi = (self.pbi + 1) % 2
        return p

    def wbuf(self):
        w = self.WB[self.wbi]
        self.wbi = (self.wbi + 1) % len(self.WB)
        return w

    def stg(self):
        t = self.STG[self.stgi]
        self.stgi = (self.stgi + 1) % len(self.STG)
        return t

    def cast_load(self, dst_t, dst_ap, src_ap):
        k = self.k
        shp = list(dst_ap.shape)
        p0 = dst_ap.base_partition()
        P = shp[0]
        if len(shp) == 2:
            B = shp[1]
            for b0 in range(0, B, 2048):
                b1 = min(B, b0 + 2048)
                st = self.stg()
                sv = st.h[p0:p0 + P, 0:b1 - b0]
                k.dma(st.v(sv), V(src_ap[:, b0:b1], ()))
                k.copy(dst_t.v(dst_ap[:, b0:b1]), st.v(sv), eng="pool")
            return
        A, B = shp[1], shp[2]
        per = max(1, 2048 // B)
        for a0 in range(0, A, per):
            a1 = min(A, a0 + per)
            st = self.stg()
            sv = st.h[p0:p0 + P, 0:(a1 - a0) * B].rearrange("p (a b) -> p a b", a=a1 - a0)
            k.dma(st.v(sv), V(src_ap[:, a0:a1, :], ()))
            k.copy(dst_t.v(dst_ap[:, a0:a1, :]), st.v(sv), eng="pool")

    def load_w(self, src_ap, kch, ncols, pretiled=False):
        w = self.wbuf()
        view = w.h[:, 0:kch * ncols].rearrange("p (a b) -> p a b", a=kch)
        src = src_ap if pretiled else src_ap.rearrange("(a p) n -> p a n", p=128)
        self.k.dma(w.v(view), V(src, ()))
        return w, view

    def tapv(self, name, view, shape, dtype=F32):
        if name in self.want:
            self.k.tap(name, view, shape, dtype)

    def convert_weights(self):
        k, I = self.k, self.I
        es = ExitStack()
        SF = [k.sb("cvf%d" % i, [128, 2048], F32, es) for i in range(4)]
        SH = [k.sb("cvh%d" % i, [128, 2048], BF16, es) for i in range(4)]
        cnt = [0]
        engs = ("pool", "dve", "act")

        def piece(src_ap, dst_ap, P, n):
            i = cnt[0]
            cnt[0] += 1
            f, h = SF[i % 4], SH[i % 4]
            shp = list(src_ap.shape)
            if len(shp) == 2:
                fv, hv = f.h[0:P, 0:n], h.h[0:P, 0:n]
            else:
                fv = f.h[0:P, 0:n].rearrange("p (a b) -> p a b", a=shp[1])
                hv = h.h[0:P, 0:n].rearrange("p (a b) -> p a b", a=shp[1])
            k.dma(f.v(fv), V(src_ap, ()))
            k.copy(h.v(hv), f.v(fv), eng=engs[i % 3])
            k.dma(V(dst_ap, (("w16", i),)), h.v(hv), queue="act")

        for l in self.layers:
            for nme in ("w_in", "w_branch_a", "w_branch_b", "w_out", "w_mlp1", "cmp_wk1", "cmp_wv1", "cmp_wk2", "cmp_wv2"):
                src = I[nme].h[l].rearrange("(p a) n -> p (a n)", p=128)
                dst = self.W16[nme].h[l].rearrange("(p a) n -> p (a n)", p=128)
                tot = src.shape[1]
                for j0 in range(0, tot, 2048):
                    j1 = min(tot, j0 + 2048)
                    piece(src[:, j0:j1], dst[:, j0:j1], 128, j1 - j0)
            for nch in range(8):
                srcv = I["w_mlp2"].h[l, :, nch * 128:(nch + 1) * 128].rearrange("(a p) n -> p a n", p=128)
                for hh in range(2):
                    piece(srcv[:, hh * 16:(hh + 1) * 16, :], self.W16["w_mlp2"].h[l, nch, :, hh * 16:(hh + 1) * 16, :], 128, 2048)
        k.barrier()
        es.close()

    def prologue(self):
        k, I, nseq = self.k, self.I, self.nseq
        self.convert_weights()
        es = ExitStack()
        condT = k.sb("condT", [128, 8, nseq], F32, es)
        k.dma(condT[:], I["c_l"][:])
        k.act(condT[:], condT[:], AF.Silu)
        badaT = k.sb("badaT", [128, L, 48], F32, es)
        k.dma(badaT[:], I["b_adaT"][:])
        wp = [k.sb("wada%d" % i, [128, 8, 512], F32, es) for i in range(2)]
        brow = k.sb("brow", [1, 512], F32, es)
        grow_sb = k.sb("growsb", [1, 512], F32, es)
        for l in self.layers:
            for piece in range(12):
                w = wp[piece % 2]
                k.dma(w[:], V(I["w_ada"].h[l, :, piece * 512:(piece + 1) * 512].rearrange("(a p) n -> p a n", p=128), ()))
                for j in range(4):
                    ch = piece * 4 + j
                    ps = self.bank()
                    for kc in range(8):
                        k.mm(ps[:, 0:nseq], w[:, kc, j * 128:(j + 1) * 128], condT[:, kc, :], start=(kc == 0), stop=(kc == 7))
                    k.ts(self.MODT[:, l, ch, :], ps[:, 0:nseq], badaT[:, l, ch:ch + 1], ALU.add)
                if piece in (4, 5, 10, 11):
                    which = 0 if piece < 6 else 1
                    half = piece % 2
                    k.dma(brow[:], V(I["b_ada"].h[l:l + 1, piece * 512:(piece + 1) * 512], ()))
                    for b in range(nseq):
                        ps = self.bank()
                        for kc in range(8):
                            k.mm(ps[0:1, :], condT[:, kc, b:b + 1], w[:, kc, :], start=(kc == 0), stop=(kc == 7))
                        k.tt(grow_sb[:], ps[0:1, :], brow[:], ALU.add)
                        k.ts(grow_sb[:], grow_sb[:], 1.0, ALU.add)
                        k.dma(self.grow.v(self.grow.h[l, b, which:which + 1, half * 512:(half + 1) * 512]), grow_sb[:])
            k.ts(self.MODT[:, l, 8:16, :], self.MODT[:, l, 8:16, :], 1.0, ALU.add)
            k.ts(self.MODT[:, l, 32:40, :], self.MODT[:, l, 32:40, :], 1.0, ALU.add)
        z = k.sb("lbz", [128, L, 4], F32, es)
        e = k.sb("lbe", [128, L, 4], F32, es)
        ssum = k.sb("lbs", [128, 4], F32, es)
        cum = k.sb("lbc", [128, L, 4], F32, es)
        k.dma(z[:], I["lb_l"][:])
        k.act(e[:], z[:], AF.Exp)
        k.tt(ssum[:], e[:, 0, :], e[:, 1, :], ALU.add)
        k.recip(ssum[:], ssum[:])
        for l in range(L):
            k.tt(e[:, l, :], e[:, l, :], ssum[:], ALU.mult)
        k.copy(cum[:, 0, :], e[:, 0, :])
        k.tt(cum[:, 1, :], e[:, 0, :], e[:, 1, :], ALU.add)
        for l in range(L):
            k.tt(self.LB[:, l, :], cum[:, l, :], cum[:, 0, :], ALU.subtract)
        k.ts(self.OML[:], self.LB[:], -1.0, ALU.mult, 1.0, ALU.add)
        k.barrier()
        es.close()

    def layer(self, s, l):
        k, I = self.k, self.I
        src = I["x"] if (self.first and l == self.layers[0]) else self.xres
        self.make_ut(s, l, src, 0)
        self.tapv("ut", self.UT[:], [128, 8, S], BF16)
        if self.stop == "ut":
            return
        if 'nsa' not in self.skip:
            self.nsa(s, l)
        self.tapv("yaT", self.YAB.sub("a", (slice(None), slice(0, 4), slice(None))), [128, 4, S], BF16)
        if self.stop == "nsa":
            return
        if 'hgrn' not in self.skip:
            self.hgrn(s, l)
        self.tapv("ybT", self.YAB.sub("b", (slice(None), slice(4, 8), slice(None))), [128, 4, S], BF16)
        if self.stop in ("hgrn", "hg1", "hg2"):
            return
        self.tail(s, l, src)

    def make_ut(self, s, l, src, sub):
        k = self.k
        es = ExitStack()
        xt = [k.sb("xt%d" % i, [128, 4, D], F32, es) for i in range(2)]
        sh_c, sc_c = (0, 8) if sub == 0 else (24, 32)
        for tc in range(4):
            x4 = xt[tc % 2]
            k.dma(x4[:], V(src.h[s, tc * 512:(tc + 1) * 512, :].rearrange("(a p) d -> p a d", p=128), src[:].res))
            for fc in range(8):
                ps = self.bank()
                for a in range(4):
                    k.tr(ps[:, a * 128:(a + 1) * 128], x4[:, a, fc * 128:(fc + 1) * 128], self.IDF[:])
                k.act(self.UT[:, fc, tc * 512:(tc + 1) * 512], ps[:], AF.Identity,
                      bias=self.MODT[:, l, sh_c + fc, s:s + 1], scale=self.MODT[:, l, sc_c + fc, s:s + 1])
        k.barrier()
        es.close()

    def nsa(self, s, l):
        k, I = self.k, self.I
        es = ExitStack()
        sb = lambda n, sh, dt: k.sb(n, sh, dt, es)
        QT = sb("qt", [128, 4, S], BF16)
        KST = [sb("kst%d" % g, [128, S], BF16) for g in range(2)]
        KWT = [sb("kwt%d" % g, [128, S], BF16) for g in range(2)]
        VS = sb("vs", [128, 16, 2, 65], BF16)
        VW = sb("vw", [128, 16, 2, 65], BF16)
        GATE = sb("gate", [128, 16, 24], F32)
        KCC = [sb("kcc%d" % g, [128, 128], BF16) for g in range(2)]
        VCC = sb("vcc", [127, 2, 65], BF16)
        k.memset(VS[:, :, :, 64:65], 1.0)
        k.memset(VW[:, :, :, 64:65], 1.0)
        win = self.W16["w_in"].h
        WA, wav = self.load_w(win[l, :, 0:512], 8, 512)
        WBb, wbv = self.load_w(win[l, :, 512:1024], 8, 512)
        WC, wcv = self.load_w(win[l, :, 1024:1304], 8, 280)

        es12 = ExitStack()
        KCT = k.sb("kct", [128, S], BF16, es12)
        VCT = k.sb("vct", [128, S], BF16, es12)
        es1 = ExitStack()
        COS = k.sb("cos", [128, 16, 32], F32, es1)
        SIN = k.sb("sin", [128, 16, 32], F32, es1)
        k.dma(COS[:], I["k_cos"][:])
        k.dma(SIN[:], I["k_sin"][:])
        BROW = k.sb("brow", [128, 1304], F32, es1)
        k.dma(BROW[:], V(I["b_in"].h[l, 0:1304].partition_broadcast(128), ()))
        R = [k.sb("r%d" % i, [128, 896], F32, es1) for i in range(2)]
        TA = k.sb("ta", [128, 448], F32, es1)
        TB = k.sb("tb", [128, 448], F32, es1)
        RO = k.sb("ro", [128, 14, 64], BF16, es1)
        RB = [k.sb("rb%d" % i, [128, 1152], BF16, es1) for i in range(4)]
        GL = k.sb("gl", [128, 24], F32, es1)
        for tc in range(4):
            for tl in range(4):
                tt = tc * 4 + tl
                pa, pb_, pc = self.bank(), self.bank(), self.bank()
                for kc in range(8):
                    lhs = self.UT[:, kc, tt * 128:(tt + 1) * 128]
                    k.mm(pa[:], lhs, WA.v(wav[:, kc, :]), start=(kc == 0), stop=(kc == 7))
                for kc in range(8):
                    lhs = self.UT[:, kc, tt * 128:(tt + 1) * 128]
                    k.mm(pb_[:], lhs, WBb.v(wbv[:, kc, :]), start=(kc == 0), stop=(kc == 7))
                for kc in range(8):
                    lhs = self.UT[:, kc, tt * 128:(tt + 1) * 128]
                    k.mm(pc[:, 0:280], lhs, WC.v(wcv[:, kc, :]), start=(kc == 0), stop=(kc == 7))
                r = R[tt % 2]
                k.tt(r[:, 0:512], pa[:], BROW[:, 0:512], ALU.add)
                k.tt(r[:, 512:640], pb_[:, 0:128], BROW[:, 512:640], ALU.add)
                k.tt(r[:, 640:768], pb_[:, 256:384], BROW[:, 768:896], ALU.add)
                k.tt(r[:, 768:896], pc[:, 0:128], BROW[:, 1024:1152], ALU.add)
                k.tt(VS.v(VS.h[:, tt, :, 0:64]), V(pb_.h[:, 384:512].rearrange("p (g d) -> p g d", g=2), pb_[:].res),
                     V(BROW.h[:, 896:1024].rearrange("p (g d) -> p g d", g=2), BROW[:].res), ALU.add)
                k.tt(VW.v(VW.h[:, tt, :, 0:64]), V(pc.h[:, 128:256].rearrange("p (g d) -> p g d", g=2), pc[:].res),
                     V(BROW.h[:, 1152:1280].rearrange("p (g d) -> p g d", g=2), BROW[:].res), ALU.add)
                k.tt(GL[:], pc[:, 256:280], BROW[:, 1280:1304], ALU.add)
                k.act(GATE[:, tt, :], GL[:], AF.Sigmoid)
                rv = r.h[:, :].rearrange("p (h two d) -> p h two d", two=2, d=32)
                t1 = r.v(rv[:, :, 0, :])
                t2 = r.v(rv[:, :, 1, :])
                cosb = COS.v(COS.h[:, tt:tt + 1, :].to_broadcast([128, 14, 32]))
                sinb = SIN.v(SIN.h[:, tt:tt + 1, :].to_broadcast([128, 14, 32]))
                ta = TA.v(TA.h[:, :].rearrange("p (h d) -> p h d", d=32))
                tb = TB.v(TB.h[:, :].rearrange("p (h d) -> p h d", d=32))
                k.tt(ta, t1, cosb, ALU.mult)
                k.tt(tb, t2, sinb, ALU.mult)
                k.tt(RO.v(RO.h[:, :, 0:32]), ta, tb, ALU.subtract)
                k.tt(ta, t2, cosb, ALU.mult)
                k.tt(tb, t1, sinb, ALU.mult)
                k.tt(RO.v(RO.h[:, :, 32:64]), ta, tb, ALU.add)
                rb = RB[tl]
                k.copy(rb.v(rb.h[:, 0:640].rearrange("p (h d) -> p h d", d=64)), RO.v(RO.h[:, 0:10, :]), eng="pool")
                k.copy(rb.v(rb.h[:, 640:1152].rearrange("p (h c d) -> p h c d", c=2, d=64)),
                       RO.v(RO.h[:, 10:14, :].unsqueeze(2).to_broadcast([128, 4, 2, 64])), eng="pool")
            tsl = slice(tc * 512, (tc + 1) * 512)
            dests = [QT.v(QT.h[:, j, tsl]) for j in range(4)] + [KCT[:, tsl], KST[0][:, tsl], KST[1][:, tsl],
                                                                   KWT[0][:, tsl], KWT[1][:, tsl]]
            for j in range(9):
                pbk = self.bbank()
                for tl in range(4):
                    k.tr(pbk[:, tl * 128:(tl + 1) * 128], RB[tl][:, j * 128:(j + 1) * 128], self.IDB[:])
                k.copy(dests[j], pbk[:, 0:512], eng=("act" if j % 2 else "dve"))
            ps = self.bank()
            for kc in range(8):
                k.mm(ps[:], WBb.v(wbv[:, kc, 128:256]), self.UT[:, kc, tsl], start=(kc == 0), stop=(kc == 7))
            k.act(VCT[:, tsl], ps[:], AF.Identity, bias=self.BCOL[:, l, BC_VC:BC_VC + 1])
        k.barrier()
        es1.close()

        es2 = ExitStack()
        W1K = k.sb("w1k", [128, 32, 128], BF16, es2)
        W1V = k.sb("w1v", [128, 32, 128], BF16, es2)
        W2K = k.sb("w2k", [128, 128], BF16, es2)
        W2V = k.sb("w2v", [128, 64], BF16, es2)
        PEK = k.sb("pek", [128, 32], BF16, es2)
        PEV = k.sb("pev", [128, 32], BF16, es2)
        k.memset(VCC[:, :, 64:65], 1.0)
        for half in range(2):
            hs = slice(64 * half, 64 * half + 64)
            k.dma(W1K.v(W1K.h[hs, :, :]), V(self.W16["cmp_wk1"].h[l].rearrange("(i d) h -> d i h", d=64), ()))
            k.dma(W1V.v(W1V.h[hs, :, :]), V(self.W16["cmp_wv1"].h[l].rearrange("(i d) h -> d i h", d=64), ()))
            k.dma(W2K.v(W2K.h[:, hs]), V(self.W16["cmp_wk2"].h[l], ()))
        k.dma(W2V[:], V(self.W16["cmp_wv2"].h[l], ()))
        self.cast_load(PEK, PEK.h[:, :], I["pe_kT"].h[l])
        self.cast_load(PEV, PEV.h[:, :], I["pe_vT"].h[l])
        CB = k.sb("cb", [128, 4], F32, es2)
        HID = k.sb("hid", [128, 4, 128], BF16, es2)
        for g in range(2):
            gs = slice(64 * g, 64 * g + 64)
            for kv, (W1, PE_, SRC) in enumerate(((W1K, PEK, KCT), (W1V, PEV, VCT))):
                idx = g * 2 + kv
                pcb = self.bank()
                for i in range(32):
                    k.mm(pcb[:, 0:1], W1.v(W1.h[gs, i, :]), PE_.v(PE_.h[gs, i:i + 1]), start=(i == 0), stop=(i == 31))
                k.copy(CB[:, idx:idx + 1], pcb[:, 0:1])
                ph = self.bank()
                for i in range(32):
                    k.mm(ph[:, 0:127], W1.v(W1.h[gs, i, :]), SRC.v(SRC.h[gs, i:i + 16 * 126 + 1:16]), start=(i == 0), stop=(i == 31))
                k.act(HID.v(HID.h[:, idx, 0:127]), ph[:, 0:127], AF.Silu, bias=CB[:, idx:idx + 1])
                po = self.bank()
                if kv == 0:
                    k.mm(po[:, 0:127], W2K[:], HID.v(HID.h[:, idx, 0:127]))
                    k.copy(KCC[g][:, 0:127], po[:, 0:127])
                else:
                    k.mm(po[0:127, 0:64], HID.v(HID.h[:, idx, 0:127]), W2V[:])
                    k.copy(VCC.v(VCC.h[:, g, 0:64]), po[0:127, 0:64])
        k.barrier()
        es2.close()
        es12.close()

        es3 = ExitStack()
        CMW = k.sb("cmw", [128, 8, 512], BF16, es3)
        CV = k.sb("cv", [127, S], BF16, es3)
        E = k.sb("e", [32, S], BF16, es3)
        OV = k.sb("ov", [127, 33], BF16, es3)
        VM = k.sb("vm", [128, 16, 32], F32, es3)
        ADDC = k.sb("addc", [128, 16, 32], F32, es3)
        k.dma(CMW[:], V(I["k_cmw"].h[:].rearrange("a p n -> p a n"), ()))
        k.dma(CV[:], I["k_cv"][:])
        k.dma(E[:], I["k_e"][:])
        k.dma(OV[:], I["k_ov"][:])
        k.dma(VM[:], I["k_vm"][:])
        k.dma(ADDC[:], I["k_addc"][:])
        NSELT = [k.sb("nselt%d" % g, [32, S], BF16, es3) for g in range(2)]
        PT = [k.sb("pt%d" % i, [128, 512], BF16, es3) for i in range(4)]
        OE = [k.sb("oe%d" % i, [65, 512], F32, es3) for i in range(4)]
        OEC = [k.sb("oec%d" % i, [65, 512], F32, es3) for i in range(4)]
        PCT = [k.sb("pct%d" % i, [127, 512], BF16, es3) for i in range(4)]
        YAs = [k.sb("ya%d" % i, [128, 4, 256], F32, es3) for i in range(2)]
        YAb = k.sb("yab16", [128, 4, 256], BF16, es3)
        TMP = k.sb("tmp", [128, 256], F32, es3)
        RS = k.sb("rs", [128, 4], F32, es3)
        RSC = k.sb("rsc", [128, 4, 4], F32, es3)
        IMPN = k.sb("impn", [128, 16, 32], F32, es3)
        IMP = k.sb("imp", [128, 4, 32], F32, es3)
        M8 = k.sb("m8", [128, 4, 8], F32, es3)
        LT = k.sb("lt", [128, 4, 32], F32, es3)
        NS = k.sb("ns", [128, 4, 32], BF16, es3)
        pti = [0]

        def next_pt():
            p = PT[pti[0] % 4]
            pti[0] += 1
            return p

        def combine(br, g, tc, first, OEs, YA):
            for tl in range(4):
                tt = tc * 4 + tl
                tp = self.mbank()
                for r in range(4):
                    k.tr(tp[:, r * 65:(r + 1) * 65], OEs[r][0:65, tl * 128:(tl + 1) * 128], self.IDF[0:65, 0:65])
                tpv = tp.h[:, 0:260].rearrange("p (r c) -> p r c", c=65)
                rs = RSC[:, tl, :] if br == 0 else RS[:]
                k.ts(rs, tp.v(tpv[:, :, 64]), 1e-30, ALU.max)
                k.recip(rs, rs)
                gv = GATE.v(GATE.h[:, tt, g * 12:(g + 1) * 12].rearrange("p (r b) -> p r b", b=3)[:, :, br])
                k.tt(RS[:], rs, gv, ALU.mult)
                rsb = RS.v(RS.h[:, :].unsqueeze(2).to_broadcast([128, 4, 64]))
                dst = YA.v(YA.h[:, tl, :].rearrange("p (r d) -> p r d", d=64))
                if first:
                    k.tt(dst, tp.v(tpv[:, :, 0:64]), rsb, ALU.mult)
                else:
                    tmpv = TMP.v(TMP.h[:, :].rearrange("p (r d) -> p r d", d=64))
                    k.tt(tmpv, tp.v(tpv[:, :, 0:64]), rsb, ALU.mult)
                    k.tt(YA[:, tl, :], YA[:, tl, :], TMP[:], ALU.add, eng="pool")

        stages = [(g, tc) for g in range(2) for tc in range(4)]

        def hb(g, r):
            h = 4 * g + r
            return h // 2, slice(64 * (h % 2), 64 * (h % 2) + 64)

        def stage_c1(idx):
            g, tc = stages[idx]
            YA = YAs[idx % 2]
            tsl = slice(tc * 512, (tc + 1) * 512)
            for r in range(4):
                pair, bs = hb(g, r)
                q = QT.v(QT.h[bs, pair, tsl])
                sc = self.bank()
                k.mm(sc[0:127, :], KCC[g].v(KCC[g].h[bs, 0:127]), q, start=True, stop=False)
                k.mm(sc[0:127, :], self.IDB[0:127, 0:127], CV[:, tsl], start=False, stop=True)
                k.act(PCT[r][:], sc[0:127, :], AF.Exp, scale=0.125)
                oa = self.abank()
                k.mm(oa[0:65, :], VCC.v(VCC.h[:, g, :]), PCT[r][:])
                k.copy(OEC[r][:], oa[0:65, :], eng="act")
            combine(0, g, tc, True, OEC, YA)
            pi = self.mbank()
            for tl in range(4):
                for r in range(4):
                    c0 = (tl * 4 + r) * 32
                    k.mm(pi[:, c0:c0 + 32], PCT[r][:, tl * 128:(tl + 1) * 128], OV[:, 0:32])
            k.tt(IMPN.v(IMPN.h[:, :, :]), pi.v(pi.h[:, :].rearrange("p (a j) -> p a j", j=32)),
                 RSC.v(RSC.h[:, :, :].rearrange("p a b -> p (a b)").unsqueeze(2).to_broadcast([128, 16, 32])), ALU.mult)
            iv = IMPN.h[:, :, :].rearrange("p (t r) j -> p t r j", r=4)
            k.tt(IMP[:], IMPN.v(iv[:, :, 0, :]), IMPN.v(iv[:, :, 1, :]), ALU.add)
            k.tt(IMP[:], IMP[:], IMPN.v(iv[:, :, 2, :]), ALU.add)
            k.tt(IMP[:], IMP[:], IMPN.v(iv[:, :, 3, :]), ALU.add)
            k.tt(IMP[:], IMP[:], VM[:, tc * 4:(tc + 1) * 4, :], ALU.mult)
            k.tt(IMP[:], IMP[:], ADDC[:, tc * 4:(tc + 1) * 4, :], ALU.add)
            for tl in range(4):
                k.op("dve", self.nc.vector.max, [IMP[:]], [M8[:]], M8.h[:, tl, :], IMP.h[:, tl, :])
            k.tt(LT[:], IMP[:], M8.v(M8.h[:, :, 7:8].to_broadcast([128, 4, 32])), ALU.is_lt)
            k.ts(NS[:], LT[:], NEG, ALU.mult)

        def stage_c2(idx):
            g, tc = stages[idx]
            pbk = self.bbank()
            for tl in range(4):
                k.tr(pbk[0:32, tl * 128:(tl + 1) * 128], NS[:, tl, :], self.IDB[:])
            k.copy(NSELT[g][:, tc * 512:(tc + 1) * 512], pbk[0:32, 0:512], eng="act")

        def stage_sw(idx):
            g, tc = stages[idx]
            YA = YAs[idx % 2]
            tsl = slice(tc * 512, (tc + 1) * 512)
            jobs = []
            for br in (1, 2):
                if br == 1:
                    kcs = list(range(0, 4 * tc + 4))
                else:
                    kcs = list(range(max(0, 4 * tc - 4), 4 * tc + 4))
                for r in range(4):
                    for kc in kcs:
                        jobs.append((br, r, kc, kc == kcs[0], kc == kcs[-1]))
            state = {}

            def cols(j):
                br, r, kc, first, last = j
                if kc >= 4 * tc:
                    return 128 * (kc - 4 * tc), 512
                if br == 2:
                    return 0, 128 * (kc - (4 * tc - 4) + 1)
                return 0, 512

            def scores(j):
                br, r, kc, first, last = j
                pair, bs = hb(g, r)
                c0, c1 = cols(j)
                qs = slice(tc * 512 + c0, tc * 512 + c1)
                q = QT.v(QT.h[bs, pair, qs])
                ksl = slice(kc * 128, (kc + 1) * 128)
                sc = self.sbank()
                if br == 1:
                    diag = kc >= 4 * tc
                    k.mm(sc[:, c0:c1], KST[g].v(KST[g].h[bs, ksl]), q, start=True, stop=False)
                    k.mm(sc[:, c0:c1], E[:, ksl], NSELT[g][:, qs], start=False, stop=not diag)
                    if diag:
                        k.mm(sc[:, c0:c1], self.IDB[:], CMW[:, kc - 4 * tc, c0:c1], start=False, stop=True)
                else:
                    mi = (kc - 4 * tc) if kc >= 4 * tc else (4 + kc - (4 * tc - 4))
                    k.mm(sc[:, c0:c1], KWT[g].v(KWT[g].h[bs, ksl]), q, start=True, stop=False)
                    k.mm(sc[:, c0:c1], self.IDB[:], CMW[:, mi, c0:c1], start=False, stop=True)
                state[j] = sc

            def rest(j):
                br, r, kc, first, last = j
                c0, c1 = cols(j)
                sc = state.pop(j)
                if first:
                    state[("oa", br, r)] = self.abank()
                oa = state[("oa", br, r)]
                pt = next_pt()
                k.act(pt[:, c0:c1], sc[:, c0:c1], AF.Exp, scale=0.125)
                Vt = VS if br == 1 else VW
                k.mm(oa[0:65, c0:c1], Vt.v(Vt.h[:, kc, g, :]), pt[:, c0:c1], start=first, stop=last, skip_group_check=True)
                if last:
                    k.copy(OE[r][:], oa[0:65, :], eng="act")
                    if r == 3:
                        combine(br, g, tc, False, OE, YA)

            LA = 2
            for i in range(min(LA, len(jobs))):
                scores(jobs[i])
            for i in range(len(jobs)):
                if i + LA < len(jobs):
                    scores(jobs[i + LA])
                rest(jobs[i])
            k.copy(YAb[:], YA[:], eng="act")
            for fcl in range(2):
                pbk = self.bbank()
                for tl in range(4):
                    k.tr(pbk[:, tl * 128:(tl + 1) * 128], YAb[:, tl, fcl * 128:(fcl + 1) * 128], self.IDB[:])
                k.copy(self.YAB.sub("a", (slice(None), 2 * g + fcl, tsl)), pbk[:, 0:512])

        stage_c1(0)
        stage_c2(0)
        for idx in range(len(stages)):
            if idx + 1 < len(stages):
                stage_c1(idx + 1)
            stage_sw(idx)
            if idx + 1 < len(stages):
                stage_c2(idx + 1)
        k.barrier()
        es3.close()
        es.close()

    def hgrn(self, s, l):
        k, I = self.k, self.I
        es = ExitStack()
        sb = lambda n, sh, dt: k.sb(n, sh, dt, es)
        win = self.W16["w_in"].h
        RST = sb("rst", [128, S], F32)
        BDM = sb("bdm", [128, 128], F32)
        k.dma(RST[:], I["k_rst"][:])
        k.dma(BDM[:], I["k_bdm"][:])
        BROWI = sb("browi", [128, 512], F32)
        k.dma(BROWI[:], V(I["b_in"].h[l, 2328:2840].partition_broadcast(128), ()))
        VH = sb("vh", [128, 16, 512], BF16)
        wi, wiv = self.load_w(win[l, :, 2328:2840], 8, 512)
        for tt in range(16):
            ps = self.bank()
            for kc in range(8):
                k.mm(ps[:], self.UT[:, kc, tt * 128:(tt + 1) * 128], wi.v(wiv[:, kc, :]), start=(kc == 0), stop=(kc == 7))
            k.tt(VH[:, tt, :], ps[:], BROWI[:], ALU.add)
        QPs = [sb("qp%d" % i, [128, S], BF16) for i in range(2)]
        KPs = [sb("kp%d" % i, [128, S], BF16) for i in range(2)]
        KPTs = [sb("kpt%d" % i, [128, 16, 128], BF16) for i in range(2)]
        GSs = [sb("gs%d" % i, [128, S], BF16) for i in range(2)]
        EBLs = [sb("ebl%d" % i, [128, 32], F32) for i in range(2)]
        SB16s = [sb("sb16%d" % i, [128, 32, 128], BF16) for i in range(2)]
        F1 = sb("f1", [128, S], F32)
        F2 = sb("f2", [128, S], F32)
        F3 = sb("f3", [128, S], F32)
        EB = sb("eb", [128, S], F32)
        SST = sb("sst", [128, 128], F32)
        STMP = sb("stmp", [128, 128], F32)
        ATM = [sb("atm%d" % i, [128, 128], BF16) for i in range(2)]
        O2 = sb("o2", [128, 512], F32)
        RSTD = sb("rstd", [128, 512], F32)
        T1 = sb("t1", [128, 512], F32)

        def stage_p(hd):
            QP, KP, GS, EBL = QPs[hd % 2], KPs[hd % 2], GSs[hd % 2], EBLs[hd % 2]
            wq, wqv = self.load_w(win[l, :, 1304 + 128 * hd:1304 + 128 * (hd + 1)], 8, 128)
            wf, wfv = self.load_w(win[l, :, 1816 + 128 * hd:1816 + 128 * (hd + 1)], 8, 128)
            wg, wgv = self.load_w(win[l, :, 2840 + 128 * hd:2840 + 128 * (hd + 1)], 8, 128)
            for tc in range(4):
                tsl = slice(tc * 512, (tc + 1) * 512)
                pq, pf_, pg = self.bank(), self.bank(), self.bank()
                for (p_, w_, wv_) in ((pq, wq, wqv), (pf_, wf, wfv), (pg, wg, wgv)):
                    for kc in range(8):
                        k.mm(p_[:], w_.v(wv_[:, kc, :]), self.UT[:, kc, tsl], start=(kc == 0), stop=(kc == 7))
                k.act(F1[:, tsl], pq[:], AF.Silu, bias=self.BCOL[:, l, BC_QB + hd:BC_QB + hd + 1])
                k.act(GS[:, tsl], pg[:], AF.Silu, bias=self.BCOL[:, l, BC_GB + hd:BC_GB + hd + 1])
                k.act(F2[:, tsl], pf_[:], AF.Sigmoid, bias=self.BCOL[:, l, BC_FB + hd:BC_FB + hd + 1])
            k.ts(F2[:], F2[:], self.OML[:, l, hd:hd + 1], ALU.mult, self.LB[:, l, hd:hd + 1], ALU.add)
            k.act(F3[:], F2[:], AF.Ln)
            k.ts(F2[:], F2[:], -1.0, ALU.mult, 1.0, ALU.add, eng="pool")
            k.op("dve", self.nc.vector.tensor_tensor_scan, [RST[:], F3[:]], [EB[:]], EB[:].ap, RST[:].ap, F3[:].ap, 0.0,
                 ALU.mult, ALU.add)
            k.act(F3[:], EB[:], AF.Exp, scale=-1.0)
            k.act(EB[:], EB[:], AF.Exp)
            k.stt(QP[:], F1[:], 128.0 ** -0.5, EB[:], ALU.mult, ALU.mult)
            k.tt(KP[:], F2[:], F3[:], ALU.mult, eng="pool")
            k.copy(EBL[:], EB[:, 63:S:64], eng="pool")

        def stage_r(hd):
            QP, KP, GS, EBL = QPs[hd % 2], KPs[hd % 2], GSs[hd % 2], EBLs[hd % 2]
            KPT, SB16 = KPTs[hd % 2], SB16s[hd % 2]
            for tc in range(4):
                pbk = self.bbank()
                for tl in range(4):
                    tt = tc * 4 + tl
                    k.tr(pbk[:, tl * 128:(tl + 1) * 128], KP[:, tt * 128:(tt + 1) * 128], self.IDB[:])
                k.copy(KPT.v(KPT.h[:, tc * 4:(tc + 1) * 4, :].rearrange("p a b -> p (a b)")), pbk[:, 0:512], eng="act")
            k.memset(SST[:], 0.0, eng="dve")
            k.memset(SB16[:, 0, :], 0.0, eng="dve")
            vcols = slice(hd * 128, (hd + 1) * 128)
            for c4 in range(8):
                pmh = [self.bank(), self.bank()]
                for ci in range(4):
                    c = c4 * 4 + ci
                    tt, half = c // 2, c % 2
                    hs = slice(64 * half, 64 * half + 64)
                    k.mm(pmh[half][:, (ci // 2) * 128:(ci // 2 + 1) * 128], KPT.v(KPT.h[hs, tt, :]), VH.v(VH.h[hs, tt, vcols]))
                for ci in range(4):
                    c = c4 * 4 + ci
                    if c == 31:
                        break
                    half = c % 2
                    k.tt(STMP[:], pmh[half][:, (ci // 2) * 128:(ci // 2 + 1) * 128], SST[:], ALU.add)
                    k.ts(SST[:], STMP[:], EBL[:, c:c + 1], ALU.mult)
                    k.copy(SB16[:, c + 1, :], SST[:], eng="act")
            for tc in range(4):
                tsl = slice(tc * 512, (tc + 1) * 512)
                po = self.abank()
                for tl in range(4):
                    tt = tc * 4 + tl
                    t_sl = slice(tt * 128, (tt + 1) * 128)
                    pa = self.bank()
                    k.mm(pa[:, 0:128], KP[:, t_sl], QP[:, t_sl])
                    atm = ATM[tt % 2]
                    k.tt(atm[:], pa[:, 0:128], BDM[:], ALU.mult)
                    osl = slice(tl * 128, (tl + 1) * 128)
                    k.mm(po[:, osl], VH.v(VH.h[:, tt, vcols]), atm[:], start=True, stop=False)
                    for half in range(2):
                        c = 2 * tt + half
                        k.mm(po[:, tl * 128 + 64 * half: tl * 128 + 64 * half + 64], SB16[:, c, :],
                             QP[:, 64 * c:64 * c + 64], start=False, stop=(half == 1))
                k.act(O2[:], po[:], AF.Square)
                pss = self.bank()
                k.mm(pss[:], self.ONESF[:], O2[:])
                k.act(RSTD[:], pss[:], AF.Ln, scale=1.0 / 128.0, bias=self.EPS[:, 1:2])
                k.act(RSTD[:], RSTD[:], AF.Exp, scale=-0.5)
                k.tt(T1[:], po[:], RSTD[:], ALU.mult)
                k.stt(self.YAB.sub("b", (slice(None), 4 + hd, tsl)), T1[:], self.NORMG[:, l:l + 1], GS[:, tsl], ALU.mult, ALU.mult)

        stage_p(0)
        for hd in range(4):
            if hd + 1 < 4:
                stage_p(hd + 1)
            stage_r(hd)
        k.barrier()
        es.close()

    def layernorm(self, z, gB, bB, es_tmp):
        k = self.k
        st, mv, rstd = es_tmp
        zv = z.res
        for j in range(2):
            k.op("dve", self.nc.vector.bn_stats, [z], [st[:]], st.h[:, j * 6:(j + 1) * 6], z.ap[:, j * 512:(j + 1) * 512])
        k.op("dve", self.nc.vector.bn_aggr, [st[:]], [mv[:]], mv[:].ap, st[:].ap)
        k.act(rstd[:], mv[:, 1:2], AF.Sqrt, bias=self.EPS[:, 0:1])
        k.recip(rstd[:], rstd[:])
        k.ts(z, z, mv[:, 0:1], ALU.subtract, rstd[:], ALU.mult)
        k.tt(z, z, gB[:], ALU.mult, eng="pool")
        k.tt(z, z, bB[:], ALU.add, eng="pool")

    def tail(self, s, l, src):
        k, I = self.k, self.I
        es = ExitStack()
        sb = lambda n, sh, dt: k.sb(n, sh, dt, es)
        win = self.W16["w_in"].h
        MG = sb("mg", [128, 8, S], BF16)
        SIGA = sb("siga", [128, 512], BF16)
        SIGB = sb("sigb", [128, 512], BF16)
        M1 = sb("m1", [128, 512], F32)
        M2 = sb("m2", [128, 512], F32)
        BRB = sb("brb", [128, 4, 128], BF16)
        for nch in range(8):
            csl = slice(nch * 128, (nch + 1) * 128)
            wga, wgav = self.load_w(win[l, :, 3352 + 128 * nch:3352 + 128 * (nch + 1)], 8, 128)
            wgb, wgbv = self.load_w(win[l, :, 4376 + 128 * nch:4376 + 128 * (nch + 1)], 8, 128)
            wbr, wbrv = self.load_w(self.W16["w_branch_a"].h[l, :, csl], 4, 128)
            wbb = BRB
            k.dma(wbb[:], V(self.W16["w_branch_b"].h[l, :, csl].rearrange("(a p) n -> p a n", p=128), ()))
            for tc in range(4):
                tsl = slice(tc * 512, (tc + 1) * 512)
                pga, pgb, pba, pbb = self.bank(), self.bank(), self.bank(), self.bank()
                for kc in range(8):
                    k.mm(pga[:], wga.v(wgav[:, kc, :]), self.UT[:, kc, tsl], start=(kc == 0), stop=(kc == 7))
                for kc in range(8):
                    k.mm(pgb[:], wgb.v(wgbv[:, kc, :]), self.UT[:, kc, tsl], start=(kc == 0), stop=(kc == 7))
                for kc in range(4):
                    k.mm(pba[:], wbr.v(wbrv[:, kc, :]), self.YAB.sub("a", (slice(None), kc, tsl)), start=(kc == 0), stop=(kc == 3))
                for kc in range(4):
                    k.mm(pbb[:], wbb[:, kc, :], self.YAB.sub("b", (slice(None), 4 + kc, tsl)), start=(kc == 0), stop=(kc == 3))
                k.act(SIGA[:], pga[:], AF.Sigmoid, bias=self.BCOL[:, l, BC_GMA + nch:BC_GMA + nch + 1])
                k.act(SIGB[:], pgb[:], AF.Sigmoid, bias=self.BCOL[:, l, BC_GMB + nch:BC_GMB + nch + 1])
                k.tt(M1[:], pba[:], SIGA[:], ALU.mult)
                k.tt(M2[:], pbb[:], SIGB[:], ALU.mult)
                k.tt(MG[:, nch, tsl], M1[:], M2[:], ALU.add, eng="pool")
        self.tapv("mgT", MG[:], [128, 8, S], BF16)
        if self.stop == "merge":
            k.barrier()
            es.close()
            return
        G1 = sb("g1", [128, D], F32)
        G2 = sb("g2", [128, D], F32)
        LG1 = sb("lg1", [128, D], F32)
        LB1 = sb("lb1", [128, D], F32)
        LG2 = sb("lg2", [128, D], F32)
        LB2 = sb("lb2", [128, D], F32)
        k.dma(G1[:], V(self.grow.h[l, s, 0, :].partition_broadcast(128), self.grow[:].res))
        k.dma(G2[:], V(self.grow.h[l, s, 1, :].partition_broadcast(128), self.grow[:].res))
        k.dma(LG1[:], V(I["ln1_g"].h[l, :].partition_broadcast(128), ()))
        k.dma(LB1[:], V(I["ln1_b"].h[l, :].partition_broadcast(128), ()))
        k.dma(LG2[:], V(I["ln2_g"].h[l, :].partition_broadcast(128), ()))
        k.dma(LB2[:], V(I["ln2_b"].h[l, :].partition_broadcast(128), ()))
        XTs = [[sb("xt%d_%d" % (j, i), [128, D], F32) for i in range(4)] for j in range(2)]
        ST = sb("st", [128, 12], F32)
        MV = sb("mv", [128, 2], F32)
        RSD = sb("rsd", [128, 1], F32)
        TMPZ = sb("tmpz", [128, 512], F32)
        Y2 = sb("y2", [128, 512], F32)
        H1 = self.YAB
        h1v = H1.h[:, :, :].rearrange("p a (b c) -> p (a b) c", c=512)
        h1res = ((H1.name, "a"), (H1.name, "b"))
        UT2 = sb("ut2", [128, 8, 512], BF16)
        last_layer = (l == self.layers[-1]) and self.last
        dst = self.out if last_layer else self.xres

        def stage_ea(tc):
            XT = XTs[tc % 2]
            wo = []
            for nh in range(2):
                wo.append(self.load_w(self.W16["w_out"].h[l, :, nh * 512:(nh + 1) * 512], 8, 512))
            for tl in range(4):
                tt = tc * 4 + tl
                x = XT[tl]
                k.dma(x[:], V(src.h[s, tt * 128:(tt + 1) * 128, :], src[:].res))
                for nh in range(2):
                    ps = self.bank()
                    w_, wv_ = wo[nh]
                    for kc in range(8):
                        k.mm(ps[:], MG[:, kc, tt * 128:(tt + 1) * 128], w_.v(wv_[:, kc, :]), start=(kc == 0), stop=(kc == 7))
                    k.tt(TMPZ[:], ps[:], G1[:, nh * 512:(nh + 1) * 512], ALU.mult)
                    k.stt(x[:, nh * 512:(nh + 1) * 512], x[:, nh * 512:(nh + 1) * 512], ALPHA, TMPZ[:], ALU.mult, ALU.add)
                self.layernorm(x[:], LG1, LB1, (ST, MV, RSD))

        def stage_eb(tc):
            XT = XTs[tc % 2]
            for fc in range(8):
                ps = self.bank()
                for tl in range(4):
                    k.tr(ps[:, tl * 128:(tl + 1) * 128], XT[tl][:, fc * 128:(fc + 1) * 128], self.IDF[:])
                k.act(UT2[:, fc, :], ps[:], AF.Identity, bias=self.MODT[:, l, 24 + fc, s:s + 1],
                      scale=self.MODT[:, l, 32 + fc, s:s + 1])

        def stage_f1(tc):
            for n4 in range(8):
                w_, wv_ = self.load_w(self.W16["w_mlp1"].h[l, :, n4 * 512:(n4 + 1) * 512], 8, 512)
                for j in range(4):
                    nch = n4 * 4 + j
                    ps = self.bank()
                    for kc in range(8):
                        k.mm(ps[:], w_.v(wv_[:, kc, j * 128:(j + 1) * 128]), UT2[:, kc, :], start=(kc == 0), stop=(kc == 7))
                    k.act(M1[:], ps[:], AF.Relu)
                    k.tt(V(h1v[:, nch, :], h1res), M1[:], M1[:], ALU.mult, eng="pool")

        def stage_f2(tc):
            XT = XTs[tc % 2]
            for nch2 in range(8):
                w_, wv_ = self.load_w(self.W16["w_mlp2"].h[l, nch2], 32, 128, pretiled=True)
                py = self.abank()
                for kc in range(32):
                    k.mm(py[:], w_.v(wv_[:, kc, :]), V(h1v[:, kc, :], h1res), start=(kc == 0), stop=(kc == 31))
                k.copy(Y2[:], py[:], eng="act")
                ptr = self.bank()
                for tl in range(4):
                    k.tr(ptr[:, tl * 128:(tl + 1) * 128], Y2[:, tl * 128:(tl + 1) * 128], self.IDF[:])
                cs = slice(nch2 * 128, (nch2 + 1) * 128)
                for tl in range(4):
                    x = XT[tl]
                    k.tt(TMPZ[:, 0:128], ptr[:, tl * 128:(tl + 1) * 128], G2[:, cs], ALU.mult)
                    k.stt(x[:, cs], x[:, cs], ALPHA, TMPZ[:, 0:128], ALU.mult, ALU.add)
            for tl in range(4):
                tt = tc * 4 + tl
                x = XT[tl]
                self.layernorm(x[:], LG2, LB2, (ST, MV, RSD))
                k.dma(V(dst.h[s, tt * 128:(tt + 1) * 128, :], dst[:].res), x[:])

        stage_ea(0)
        stage_eb(0)
        for tc in range(4):
            stage_f1(tc)
            if tc + 1 < 4:
                stage_ea(tc + 1)
            stage_f2(tc)
            if tc + 1 < 4:
                stage_eb(tc + 1)
        k.barrier()
        es.close()


def _consts():
    cmw = np.zeros((8, 128, 512), np.float32)
    kl = np.arange(128)[:, None]
    tl = np.arange(512)[None, :]
    for i in range(4):
        cmw[i] = np.where(128 * i + kl <= tl, 0.0, NEG)
        cmw[4 + i] = np.where(tl <= 128 * i + kl - 1, 0.0, NEG)
    n = np.arange(127)[:, None]
    t = np.arange(S)[None, :]
    cv = np.where(16 * n + 31 <= t, 0.0, NEG).astype(np.float32)
    j = np.arange(32)[:, None]
    e = (t // 64 == j).astype(np.float32)
    jj = np.arange(32)[None, :]
    ov = np.zeros((127, 33), np.float32)
    ov[:, :32] = ((16 * n < 64 * jj + 64) & (16 * n + 32 > 64 * jj)).astype(np.float32)
    ov[:, 32] = 1.0
    rst = np.ones((128, S), np.float32)
    rst[:, ::64] = 0.0
    s_ = np.arange(128)[:, None]
    t_ = np.arange(128)[None, :]
    bdm = ((s_ <= t_) & (s_ // 64 == t_ // 64)).astype(np.float32)
    ident = np.eye(128, dtype=np.float32)
    inv = (1.0 / (np.float32(10000.0) ** (np.arange(0, 64, 2, dtype=np.float32) / np.float32(64)))).astype(np.float32)
    ang = np.arange(S, dtype=np.float32)[:, None] * inv[None, :]
    cos = np.cos(ang).astype(np.float32).reshape(16, 128, 32).transpose(1, 0, 2)
    sin = np.sin(ang).astype(np.float32).reshape(16, 128, 32).transpose(1, 0, 2)
    pos = np.arange(S)
    tb = pos // 64
    jb = np.arange(32)
    valid = jb[None, :] <= tb[:, None]
    forced = valid & ((jb[None, :] == 0) | (jb[None, :] == tb[:, None]) | (jb[None, :] == tb[:, None] - 1))
    vm = (valid & ~forced).astype(np.float32).reshape(16, 128, 32).transpose(1, 0, 2)
    addc = np.where(forced, 1.0e4, np.where(valid, 0.0, -1.0)).astype(np.float32).reshape(16, 128, 32).transpose(1, 0, 2)
    import ml_dtypes
    bf = ml_dtypes.bfloat16
    c = dict(k_cmw=cmw.astype(bf), k_cv=cv.astype(bf), k_e=e.astype(bf), k_ov=ov.astype(bf), k_rst=rst, k_bdm=bdm,
             k_ident=ident, k_identb=ident.astype(bf), k_cos=cos, k_sin=sin, k_vm=vm, k_addc=addc)
    return {k_: np.ascontiguousarray(v) for k_, v in c.items()}


def prep_inputs(inputs, seqs):
    f = lambda a: np.ascontiguousarray(np.asarray(a, dtype=np.float32))
    m = {}
    m["x"] = f(inputs["x"][seqs])
    c = np.asarray(inputs["c"], np.float32)[seqs]
    m["c_l"] = f(c.reshape(len(seqs), 8, 128).transpose(2, 1, 0))
    for nme in ("w_in", "b_in", "cmp_wk1", "cmp_wk2", "cmp_wv1", "cmp_wv2", "w_branch_a", "w_branch_b", "w_out",
                "w_ada", "b_ada", "ln1_g", "ln1_b", "w_mlp1", "w_mlp2", "ln2_g", "ln2_b"):
        m[nme] = f(inputs[nme])
    b_in = np.asarray(inputs["b_in"], np.float32)
    m["bcols"] = f(np.stack([b_in[:, o:o + 128] for o in BCOL_OFFS], axis=-1).transpose(1, 0, 2))
    pek = np.asarray(inputs["cmp_pe_k"], np.float32).transpose(0, 2, 1)
    pev = np.asarray(inputs["cmp_pe_v"], np.float32).transpose(0, 2, 1)
    m["pe_kT"] = f(np.concatenate([pek, pek], axis=1))
    m["pe_vT"] = f(np.concatenate([pev, pev], axis=1))
    m["lb_l"] = f(np.asarray(inputs["hgrn_lb_logits"], np.float32).reshape(L, 4, 128).transpose(2, 0, 1))
    m["normg_l"] = f(np.asarray(inputs["hgrn_norm_g"], np.float32).T)
    m["b_adaT"] = f(np.asarray(inputs["b_ada"], np.float32).reshape(L, 48, 128).transpose(2, 0, 1))
    m.update(_consts())
    return m


def kernel(**inputs):
    n = 8
    prog = Prog()
    in_maps = [prep_inputs(inputs, list(range(c * NSEQ, (c + 1) * NSEQ))) for c in range(n)]
    res = run_bass_kernel_spmd(prog.nc, in_maps, core_ids=list(range(n)))
    out = np.concatenate([np.asarray(r["out"], np.float32) for r in res.results], axis=0)
    return out
```

```python
from contextlib import ExitStack
import numpy as np
import concourse.bass as bass
import concourse.mybir as mybir
from concourse.bass_utils import run_bass_kernel_spmd

F32 = mybir.dt.float32
BF16 = mybir.dt.bfloat16
AF = mybir.ActivationFunctionType
ALU = mybir.AluOpType

D = 1024
S = 2048
L = 2
NSEQ = 4
N_IN = 5400
NEG = -30000.0
ALPHA = (2 * L) ** 0.25
LN_EPS = 1e-5
RMS_EPS = 1e-6
BCOL_OFFS = [640] + [1304 + 128 * i for i in range(4)] + [1816 + 128 * i for i in range(4)] + \
    [2840 + 128 * i for i in range(4)] + [3352 + 128 * i for i in range(8)] + [4376 + 128 * i for i in range(8)]
BC_VC, BC_QB, BC_FB, BC_GB, BC_GMA, BC_GMB = 0, 1, 5, 9, 13, 21


class V:
    __slots__ = ("ap", "res")

    def __init__(self, ap, res):
        self.ap = ap
        self.res = res


class T:
    def __init__(self, name, handle):
        self.name = name
        self.h = handle

    def __getitem__(self, key):
        return V(self.h[key], (self.name,))

    def sub(self, skey, key=slice(None)):
        return V(self.h[key], ((self.name, skey),))

    def v(self, ap, skey=None):
        return V(ap, ((self.name,) if skey is None else ((self.name, skey),)))


class HalfBank:
    def __init__(self, t, i):
        self.t = t
        self.i = i

    def __getitem__(self, key):
        rows, cols = key
        c0 = (cols.start or 0) + 512 * self.i
        c1 = (cols.stop if cols.stop is not None else 512) + 512 * self.i
        return V(self.t.h[rows, c0:c1], ((self.t.name, self.i),))


class K:
    def __init__(self, nc, waited=None):
        self.nc = nc
        self.prev_waited = waited
        self.waited = {n: set() for n in ("pe", "act", "dve", "pool")}
        self.seq = {n: 0 for n in ("pe", "act", "dve", "pool")}
        self.cnt2seq = {n: {} for n in ("pe", "act", "dve", "pool")}
        self.es = ExitStack()
        self.engs = {"pe": nc.tensor, "act": nc.scalar, "dve": nc.vector, "pool": nc.gpsimd, "sp": nc.sync}
        self.sem = {}
        self.cnt = {}
        for n in ("pe", "act", "dve", "pool"):
            self.sem[n] = self.es.enter_context(nc.semaphore("s_" + n))
            self.cnt[n] = 0
        self.ndma = 32
        self.dsem = [self.es.enter_context(nc.semaphore("s_dma%d" % i)) for i in range(self.ndma)]
        self.dcnt = [0] * self.ndma
        self.dnext = 0
        self.seen = {n: {} for n in self.engs}
        self.writers = {}
        self.readers = {}
        self.n_inst = 0
        self.n_wait = 0
        self.taps = {}
        self.uid = 0

    def sb(self, name, shape, dtype, es=None):
        self.uid += 1
        name = "%s_%d" % (name, self.uid)
        h = (es or self.es).enter_context(self.nc.sbuf_tensor(name, list(shape), dtype))
        return T(name, h)

    def ps(self, name, shape, dtype=F32):
        h = self.es.enter_context(self.nc.psum_tensor(name, list(shape), dtype))
        return T(name, h)

    def dram(self, name, shape, dtype, kind):
        h = self.nc.dram_tensor(name, list(shape), dtype, kind=kind)
        return T(name, h)

    def _semobj(self, semkey):
        return self.sem[semkey] if isinstance(semkey, str) else self.dsem[semkey]

    def _deps(self, eng, reads, writes):
        deps = {}

        def add(d, same_ok):
            semkey, val, e = d
            if e == eng and same_ok:
                return
            if deps.get(semkey, 0) < val:
                deps[semkey] = val

        for r in reads:
            w = self.writers.get(r)
            if w is not None:
                add(w, eng == "pe")
        for r in writes:
            w = self.writers.get(r)
            if w is not None:
                add(w, eng == "pe")
            for rd in self.readers.get(r, ()):
                add(rd, eng == "pe")
        seen = self.seen[eng]
        e = self.engs[eng]
        for semkey, val in deps.items():
            if seen.get(semkey, 0) >= val:
                continue
            e.wait_ge(self._semobj(semkey), val)
            seen[semkey] = val
            self.n_wait += 1
            if isinstance(semkey, str):
                self.waited[semkey].add(self.cnt2seq[semkey].get(val, -1))

    def _commit(self, tok, reads, writes):
        for r in reads:
            lst = self.readers.setdefault(r, [])
            lst[:] = [x for x in lst if x[0] != tok[0]]
            lst.append(tok)
        for r in writes:
            self.writers[r] = tok
            self.readers[r] = []

    @staticmethod
    def _res(vs):
        out = []
        for v in vs:
            if v is None:
                continue
            out.extend(v.res)
        return out

    def op(self, eng, fn, reads, writes, *args, signal=True, **kw):
        rr, ww = self._res(reads), self._res(writes)
        self._deps(eng, rr, ww)
        ins = fn(*args, **kw)
        self.seq[eng] += 1
        if self.prev_waited is not None and eng != "pe":
            signal = self.seq[eng] in self.prev_waited[eng]
        if signal:
            self.cnt[eng] += 1
            ins.then_inc(self.sem[eng], 1)
            tok = (eng, self.cnt[eng], eng)
            self.cnt2seq[eng][self.cnt[eng]] = self.seq[eng]
        else:
            tok = (eng, self.cnt[eng] + 1, eng)
        self._commit(tok, rr, ww)
        self.n_inst += 1
        return ins

    def dma(self, out, in_, queue="sp", **kw):
        rr, ww = self._res([in_]), self._res([out])
        i = self.dnext
        self.dnext = (self.dnext + 1) % self.ndma
        if self.dcnt[i] > 0 and self.seen[queue].get(i, 0) < self.dcnt[i]:
            self.engs[queue].wait_ge(self.dsem[i], self.dcnt[i])
            self.seen[queue][i] = self.dcnt[i]
        self._deps(queue, rr, ww)
        ins = self.engs[queue].dma_start(out=out.ap, in_=in_.ap, **kw)
        self.dcnt[i] += 16
        ins.then_inc(self.dsem[i], 16)
        self._commit((i, self.dcnt[i], "dma"), rr, ww)
        self.n_inst += 1
        return ins

    def barrier(self):
        for en in ("pe", "act", "dve", "pool", "sp"):
            e = self.engs[en]
            for sk in ("pe", "act", "dve", "pool"):
                if sk != en and self.cnt[sk] > self.seen[en].get(sk, 0):
                    e.wait_ge(self.sem[sk], self.cnt[sk])
                    self.seen[en][sk] = self.cnt[sk]
                    self.waited[sk].add(self.cnt2seq[sk].get(self.cnt[sk], -1))
            for i in range(self.ndma):
                if self.dcnt[i] > self.seen[en].get(i, 0):
                    e.wait_ge(self.dsem[i], self.dcnt[i])
                    self.seen[en][i] = self.dcnt[i]
        self.writers = {}
        self.readers = {}

    def mm(self, out, lhsT, rhs, start=True, stop=True, **kw):
        return self.op("pe", self.nc.tensor.matmul, [lhsT, rhs], [out], out.ap, lhsT.ap, rhs.ap,
                       signal=bool(stop), start=start, stop=stop, **kw)

    def tr(self, out, in_, ident):
        return self.op("pe", self.nc.tensor.transpose, [in_, ident], [out], out.ap, in_.ap, ident.ap)

    def act(self, out, in_, func, bias=None, scale=None):
        kw = {}
        rd = [in_]
        if bias is not None:
            if isinstance(bias, V):
                kw["bias"] = bias.ap
                rd.append(bias)
            else:
                kw["bias"] = bias
        if scale is not None:
            if isinstance(scale, V):
                kw["scale"] = scale.ap
                rd.append(scale)
            else:
                kw["scale"] = scale
        return self.op("act", self.nc.scalar.activation, rd, [out], out.ap, in_.ap, func, **kw)

    def tt(self, out, a, b, op, eng="dve"):
        e = self.engs[eng]
        return self.op(eng, e.tensor_tensor, [a, b], [out], out.ap, a.ap, b.ap, op)

    def ts(self, out, a, s1, op0, s2=None, op1=None, eng="dve"):
        e = self.engs[eng]
        rd = [a]
        a1, a2 = s1, s2
        if isinstance(s1, V):
            rd.append(s1)
            a1 = s1.ap
        if isinstance(s2, V):
            rd.append(s2)
            a2 = s2.ap
        kw = {}
        if op1 is not None:
            kw["op1"] = op1
        return self.op(eng, e.tensor_scalar, rd, [out], out.ap, a.ap, a1, a2, op0, **kw)

    def stt(self, out, a, s, b, op0, op1):
        rd = [a, b]
        sv = s
        if isinstance(s, V):
            rd.append(s)
            sv = s.ap
        return self.op("dve", self.nc.vector.scalar_tensor_tensor, rd, [out], out.ap, a.ap, sv, b.ap, op0, op1)

    def copy(self, out, in_, eng="dve"):
        if eng == "act":
            return self.op("act", self.nc.scalar.copy, [in_], [out], out.ap, in_.ap)
        e = self.engs[eng]
        return self.op(eng, e.tensor_copy, [in_], [out], out.ap, in_.ap)

    def memset(self, out, val, eng="pool"):
        e = self.engs[eng]
        return self.op(eng, e.memset, [], [out], out.ap, val)

    def recip(self, out, in_):
        return self.op("dve", self.nc.vector.reciprocal, [in_], [out], out.ap, in_.ap)

    def tap(self, name, view, shape, dtype=F32):
        t = self.dram("tap_" + name, shape, dtype, "ExternalOutput")
        self.dma(t[:], view, queue="sp")
        self.taps[name] = t

    def finish(self):
        for i in range(self.ndma):
            if self.dcnt[i] > 0:
                self.nc.sync.wait_ge(self.dsem[i], self.dcnt[i])

    def close(self):
        self.es.close()


class Prog:
    def __init__(self, nseq=NSEQ, layers=(0, 1), first=True, last=True, taps=(), stop=None, skip=(), waited=None):
        self.nseq = nseq
        self.layers = layers
        self.first = first
        self.last = last
        self.want = set(taps)
        self.stop = stop
        self.skip = set(skip)
        nc = bass.Bass("TRN2", target_bir_lowering=False)
        self.nc = nc
        k = K(nc, waited)
        self.k = k
        I = {}
        self.I = I

        def inp(name, shape, dt=F32):
            I[name] = k.dram(name, shape, dt, "ExternalInput")

        inp("x", [nseq, S, D])
        inp("c_l", [128, 8, nseq])
        inp("w_in", [L, D, N_IN])
        inp("b_in", [L, N_IN])
        inp("bcols", [128, L, len(BCOL_OFFS)])
        inp("pe_kT", [L, 128, 32])
        inp("pe_vT", [L, 128, 32])
        inp("cmp_wk1", [L, 2048, 128])
        inp("cmp_wk2", [L, 128, 64])
        inp("cmp_wv1", [L, 2048, 128])
        inp("cmp_wv2", [L, 128, 64])
        inp("lb_l", [128, L, 4])
        inp("normg_l", [128, L])
        inp("w_branch_a", [L, 512, D])
        inp("w_branch_b", [L, 512, D])
        inp("w_out", [L, D, D])
        inp("w_ada", [L, D, 6 * D])
        inp("b_ada", [L, 6 * D])
        inp("b_adaT", [128, L, 48])
        inp("ln1_g", [L, D])
        inp("ln1_b", [L, D])
        inp("w_mlp1", [L, D, 4 * D])
        inp("w_mlp2", [L, 4 * D, D])
        inp("ln2_g", [L, D])
        inp("ln2_b", [L, D])
        inp("k_cmw", [8, 128, 512], BF16)
        inp("k_cv", [127, S], BF16)
        inp("k_e", [32, S], BF16)
        inp("k_ov", [127, 33], BF16)
        inp("k_rst", [128, S])
        inp("k_bdm", [128, 128])
        inp("k_ident", [128, 128])
        inp("k_identb", [128, 128], BF16)
        inp("k_cos", [128, 16, 32])
        inp("k_sin", [128, 16, 32])
        inp("k_vm", [128, 16, 32])
        inp("k_addc", [128, 16, 32])
        self.W16 = {}
        for nme, shp in (("w_in", [L, D, N_IN]), ("w_branch_a", [L, 512, D]), ("w_branch_b", [L, 512, D]),
                         ("w_out", [L, D, D]), ("w_mlp1", [L, D, 4 * D]), ("cmp_wk1", [L, 2048, 128]),
                         ("cmp_wv1", [L, 2048, 128]), ("cmp_wk2", [L, 128, 64]), ("cmp_wv2", [L, 128, 64])):
            self.W16[nme] = k.dram("s16_" + nme, shp, BF16, "Internal")
        self.W16["w_mlp2"] = k.dram("s16_w_mlp2", [L, 8, 128, 32, 128], BF16, "Internal")
        self.out = k.dram("out", [nseq, S, D], F32, "ExternalOutput")
        self.xres = k.dram("xres", [nseq, S, D], F32, "Internal")
        self.grow = k.dram("grow", [L, nseq, 2, D], F32, "Internal")

        self.pf = [k.ps("pf%d" % i, [128, 512], F32) for i in range(6)]
        self.pb = [k.ps("pb%d" % i, [128, 1024], BF16) for i in range(2)]
        self.psi = 0
        self.pmi = 0
        self.pfi = 0
        self.pai = 0
        self.pbi = 0

        self.IDF = k.sb("idf", [128, 128], F32)
        self.IDB = k.sb("idb", [128, 128], BF16)
        self.ONESF = k.sb("onesf", [128, 128], F32)
        self.MODT = k.sb("modt", [128, L, 48, nseq], F32)
        self.LB = k.sb("lb", [128, L, 4], F32)
        self.OML = k.sb("oml", [128, L, 4], F32)
        self.BCOL = k.sb("bcol", [128, L, len(BCOL_OFFS)], F32)
        self.NORMG = k.sb("normg", [128, L], F32)
        self.EPS = k.sb("eps", [128, 2], F32)
        self.UT = k.sb("ut", [128, 8, S], BF16)
        self.YAB = k.sb("yab", [128, 8, S], BF16)
        self.WB = [k.sb("wb%d" % i, [128, 4096], BF16) for i in range(3)]
        self.wbi = 0
        self.STG = [k.sb("stg%d" % i, [128, 256], F32) for i in range(2)]
        self.stgi = 0

        k.dma(self.IDF[:], I["k_ident"][:])
        k.dma(self.IDB[:], I["k_identb"][:])
        k.dma(self.BCOL[:], I["bcols"][:])
        k.dma(self.NORMG[:], I["normg_l"][:])
        k.memset(self.ONESF[:], 1.0)
        k.memset(self.EPS[:, 0:1], LN_EPS)
        k.memset(self.EPS[:, 1:2], RMS_EPS)

        self.prologue()
        for s in range(nseq):
            for l in layers:
                self.layer(s, l)
        k.finish()
        k.close()

    def bank(self):
        p = self.pf[self.pfi]
        self.pfi = (self.pfi + 1) % 4
        return p

    def sbank(self):
        p = self.pf[self.psi]
        self.psi = (self.psi + 1) % 3
        return p

    def mbank(self):
        return self.abank()

    def abank(self):
        p = self.pf[4 + self.pai]
        self.pai = (self.pai + 1) % 2
        return p

    def bbank(self):
        p = self.pb[self.pbi]
        self.pbi = (self.pbi + 1) % 2
        return p

    def wbuf(self):
        w = self.WB[self.wbi]
        self.wbi = (self.wbi + 1) % len(self.WB)
        return w

    def stg(self):
        t = self.STG[self.stgi]
        self.stgi = (self.stgi + 1) % len(self.STG)
        return t

    def cast_load(self, dst_t, dst_ap, src_ap):
        k = self.k
        shp = list(dst_ap.shape)
        p0 = dst_ap.base_partition()
        P = shp[0]
        if len(shp) == 2:
            B = shp[1]
            for b0 in range(0, B, 2048):
                b1 = min(B, b0 + 2048)
                st = self.stg()
                sv = st.h[p0:p0 + P, 0:b1 - b0]
                k.dma(st.v(sv), V(src_ap[:, b0:b1], ()))
                k.copy(dst_t.v(dst_ap[:, b0:b1]), st.v(sv), eng="pool")
            return
        A, B = shp[1], shp[2]
        per = max(1, 2048 // B)
        for a0 in range(0, A, per):
            a1 = min(A, a0 + per)
            st = self.stg()
            sv = st.h[p0:p0 + P, 0:(a1 - a0) * B].rearrange("p (a b) -> p a b", a=a1 - a0)
            k.dma(st.v(sv), V(src_ap[:, a0:a1, :], ()))
            k.copy(dst_t.v(dst_ap[:, a0:a1, :]), st.v(sv), eng="pool")

    def load_w(self, src_ap, kch, ncols, pretiled=False):
        w = self.wbuf()
        view = w.h[:, 0:kch * ncols].rearrange("p (a b) -> p a b", a=kch)
        src = src_ap if pretiled else src_ap.rearrange("(a p) n -> p a n", p=128)
        self.k.dma(w.v(view), V(src, ()))
        return w, view

    def tapv(self, name, view, shape, dtype=F32):
        if name in self.want:
            self.k.tap(name, view, shape, dtype)

    def convert_weights(self):
        k, I = self.k, self.I
        es = ExitStack()
        SF = [k.sb("cvf%d" % i, [128, 2048], F32, es) for i in range(4)]
        SH = [k.sb("cvh%d" % i, [128, 2048], BF16, es) for i in range(4)]
        cnt = [0]
        engs = ("pool", "dve", "act")

        def piece(src_ap, dst_ap, P, n):
            i = cnt[0]
            cnt[0] += 1
            f, h = SF[i % 4], SH[i % 4]
            shp = list(src_ap.shape)
            if len(shp) == 2:
                fv, hv = f.h[0:P, 0:n], h.h[0:P, 0:n]
            else:
                fv = f.h[0:P, 0:n].rearrange("p (a b) -> p a b", a=shp[1])
                hv = h.h[0:P, 0:n].rearrange("p (a b) -> p a b", a=shp[1])
            k.dma(f.v(fv), V(src_ap, ()))
            k.copy(h.v(hv), f.v(fv), eng=engs[i % 3])
            k.dma(V(dst_ap, (("w16", i),)), h.v(hv), queue="act")

        for l in self.layers:
            for nme in ("w_in", "w_branch_a", "w_branch_b", "w_out", "w_mlp1", "cmp_wk1", "cmp_wv1", "cmp_wk2", "cmp_wv2"):
                src = I[nme].h[l].rearrange("(p a) n -> p (a n)", p=128)
                dst = self.W16[nme].h[l].rearrange("(p a) n -> p (a n)", p=128)
                tot = src.shape[1]
                for j0 in range(0, tot, 2048):
                    j1 = min(tot, j0 + 2048)
                    piece(src[:, j0:j1], dst[:, j0:j1], 128, j1 - j0)
            for nch in range(8):
                srcv = I["w_mlp2"].h[l, :, nch * 128:(nch + 1) * 128].rearrange("(a p) n -> p a n", p=128)
                for hh in range(2):
                    piece(srcv[:, hh * 16:(hh + 1) * 16, :], self.W16["w_mlp2"].h[l, nch, :, hh * 16:(hh + 1) * 16, :], 128, 2048)
        k.barrier()
        es.close()

    def prologue(self):
        k, I, nseq = self.k, self.I, self.nseq
        self.convert_weights()
        es = ExitStack()
        condT = k.sb("condT", [128, 8, nseq], F32, es)
        k.dma(condT[:], I["c_l"][:])
        k.act(condT[:], condT[:], AF.Silu)
        badaT = k.sb("badaT", [128, L, 48], F32, es)
        k.dma(badaT[:], I["b_adaT"][:])
        wp = [k.sb("wada%d" % i, [128, 8, 512], F32, es) for i in range(2)]
        brow = k.sb("brow", [1, 512], F32, es)
        grow_sb = k.sb("growsb", [1, 512], F32, es)
        for l in self.layers:
            for piece in range(12):
                w = wp[piece % 2]
                k.dma(w[:], V(I["w_ada"].h[l, :, piece * 512:(piece + 1) * 512].rearrange("(a p) n -> p a n", p=128), ()))
                for j in range(4):
                    ch = piece * 4 + j
                    ps = self.bank()
                    for kc in range(8):
                        k.mm(ps[:, 0:nseq], w[:, kc, j * 128:(j + 1) * 128], condT[:, kc, :], start=(kc == 0), stop=(kc == 7))
                    k.ts(self.MODT[:, l, ch, :], ps[:, 0:nseq], badaT[:, l, ch:ch + 1], ALU.add)
                if piece in (4, 5, 10, 11):
                    which = 0 if piece < 6 else 1
                    half = piece % 2
                    k.dma(brow[:], V(I["b_ada"].h[l:l + 1, piece * 512:(piece + 1) * 512], ()))
                    for b in range(nseq):
                        ps = self.bank()
                        for kc in range(8):
                            k.mm(ps[0:1, :], condT[:, kc, b:b + 1], w[:, kc, :], start=(kc == 0), stop=(kc == 7))
                        k.tt(grow_sb[:], ps[0:1, :], brow[:], ALU.add)
                        k.ts(grow_sb[:], grow_sb[:], 1.0, ALU.add)
                        k.dma(self.grow.v(self.grow.h[l, b, which:which + 1, half * 512:(half + 1) * 512]), grow_sb[:])
            k.ts(self.MODT[:, l, 8:16, :], self.MODT[:, l, 8:16, :], 1.0, ALU.add)
            k.ts(self.MODT[:, l, 32:40, :], self.MODT[:, l, 32:40, :], 1.0, ALU.add)
        z = k.sb("lbz", [128, L, 4], F32, es)
        e = k.sb("lbe", [128, L, 4], F32, es)
        ssum = k.sb("lbs", [128, 4], F32, es)
        cum = k.sb("lbc", [128, L, 4], F32, es)
        k.dma(z[:], I["lb_l"][:])
        k.act(e[:], z[:], AF.Exp)
        k.tt(ssum[:], e[:, 0, :], e[:, 1, :], ALU.add)
        k.recip(ssum[:], ssum[:])
        for l in range(L):
            k.tt(e[:, l, :], e[:, l, :], ssum[:], ALU.mult)
        k.copy(cum[:, 0, :], e[:, 0, :])
        k.tt(cum[:, 1, :], e[:, 0, :], e[:, 1, :], ALU.add)
        for l in range(L):
            k.tt(self.LB[:, l, :], cum[:, l, :], cum[:, 0, :], ALU.subtract)
        k.ts(self.OML[:], self.LB[:], -1.0, ALU.mult, 1.0, ALU.add)
        k.barrier()
        es.close()

    def layer(self, s, l):
        k, I = self.k, self.I
        src = I["x"] if (self.first and l == self.layers[0]) else self.xres
        self.make_ut(s, l, src, 0)
        self.tapv("ut", self.UT[:], [128, 8, S], BF16)
        if self.stop == "ut":
            return
        if 'nsa' not in self.skip:
            self.nsa(s, l)
        self.tapv("yaT", self.YAB.sub("a", (slice(None), slice(0, 4), slice(None))), [128, 4, S], BF16)
        if self.stop == "nsa":
            return
        if 'hgrn' not in self.skip:
            self.hgrn(s, l)
        self.tapv("ybT", self.YAB.sub("b", (slice(None), slice(4, 8), slice(None))), [128, 4, S], BF16)
        if self.stop in ("hgrn", "hg1", "hg2"):
            return
        self.tail(s, l, src)

    def make_ut(self, s, l, src, sub):
        k = self.k
        es = ExitStack()
        xt = [k.sb("xt%d" % i, [128, 4, D], F32, es) for i in range(2)]
        sh_c, sc_c = (0, 8) if sub == 0 else (24, 32)
        for tc in range(4):
            x4 = xt[tc % 2]
            k.dma(x4[:], V(src.h[s, tc * 512:(tc + 1) * 512, :].rearrange("(a p) d -> p a d", p=128), src[:].res))
            for fc in range(8):
                ps = self.bank()
                for a in range(4):
                    k.tr(ps[:, a * 128:(a + 1) * 128], x4[:, a, fc * 128:(fc + 1) * 128], self.IDF[:])
                k.act(self.UT[:, fc, tc * 512:(tc + 1) * 512], ps[:], AF.Identity,
                      bias=self.MODT[:, l, sh_c + fc, s:s + 1], scale=self.MODT[:, l, sc_c + fc, s:s + 1])
        k.barrier()
        es.close()

    def nsa(self, s, l):
        k, I = self.k, self.I
        es = ExitStack()
        sb = lambda n, sh, dt: k.sb(n, sh, dt, es)
        QT = sb("qt", [128, 4, S], BF16)
        KST = [sb("kst%d" % g, [128, S], BF16) for g in range(2)]
        KWT = [sb("kwt%d" % g, [128, S], BF16) for g in range(2)]
        VS = sb("vs", [128, 16, 2, 65], BF16)
        VW = sb("vw", [128, 16, 2, 65], BF16)
        GATE = sb("gate", [128, 16, 24], F32)
        KCC = [sb("kcc%d" % g, [128, 128], BF16) for g in range(2)]
        VCC = sb("vcc", [127, 2, 65], BF16)
        k.memset(VS[:, :, :, 64:65], 1.0)
        k.memset(VW[:, :, :, 64:65], 1.0)
        win = self.W16["w_in"].h
        WA, wav = self.load_w(win[l, :, 0:512], 8, 512)
        WBb, wbv = self.load_w(win[l, :, 512:1024], 8, 512)
        WC, wcv = self.load_w(win[l, :, 1024:1304], 8, 280)

        es12 = ExitStack()
        KCT = k.sb("kct", [128, S], BF16, es12)
        VCT = k.sb("vct", [128, S], BF16, es12)
        es1 = ExitStack()
        COS = k.sb("cos", [128, 16, 32], F32, es1)
        SIN = k.sb("sin", [128, 16, 32], F32, es1)
        k.dma(COS[:], I["k_cos"][:])
        k.dma(SIN[:], I["k_sin"][:])
        BROW = k.sb("brow", [128, 1304], F32, es1)
        k.dma(BROW[:], V(I["b_in"].h[l, 0:1304].partition_broadcast(128), ()))
        R = [k.sb("r%d" % i, [128, 896], F32, es1) for i in range(2)]
        TA = k.sb("ta", [128, 448], F32, es1)
        TB = k.sb("tb", [128, 448], F32, es1)
        RO = k.sb("ro", [128, 14, 64], BF16, es1)
        RB = [k.sb("rb%d" % i, [128, 1152], BF16, es1) for i in range(4)]
        GL = k.sb("gl", [128, 24], F32, es1)
        for tc in range(4):
            for tl in range(4):
                tt = tc * 4 + tl
                pa, pb_, pc = self.bank(), self.bank(), self.bank()
                for kc in range(8):
                    lhs = self.UT[:, kc, tt * 128:(tt + 1) * 128]
                    k.mm(pa[:], lhs, WA.v(wav[:, kc, :]), start=(kc == 0), stop=(kc == 7))
                for kc in range(8):
                    lhs = self.UT[:, kc, tt * 128:(tt + 1) * 128]
                    k.mm(pb_[:], lhs, WBb.v(wbv[:, kc, :]), start=(kc == 0), stop=(kc == 7))
                for kc in range(8):
                    lhs = self.UT[:, kc, tt * 128:(tt + 1) * 128]
                    k.mm(pc[:, 0:280], lhs, WC.v(wcv[:, kc, :]), start=(kc == 0), stop=(kc == 7))
                r = R[tt % 2]
                k.tt(r[:, 0:512], pa[:], BROW[:, 0:512], ALU.add)
                k.tt(r[:, 512:640], pb_[:, 0:128], BROW[:, 512:640], ALU.add)
                k.tt(r[:, 640:768], pb_[:, 256:384], BROW[:, 768:896], ALU.add)
                k.tt(r[:, 768:896], pc[:, 0:128], BROW[:, 1024:1152], ALU.add)
                k.tt(VS.v(VS.h[:, tt, :, 0:64]), V(pb_.h[:, 384:512].rearrange("p (g d) -> p g d", g=2), pb_[:].res),
                     V(BROW.h[:, 896:1024].rearrange("p (g d) -> p g d", g=2), BROW[:].res), ALU.add)
                k.tt(VW.v(VW.h[:, tt, :, 0:64]), V(pc.h[:, 128:256].rearrange("p (g d) -> p g d", g=2), pc[:].res),
                     V(BROW.h[:, 1152:1280].rearrange("p (g d) -> p g d", g=2), BROW[:].res), ALU.add)
                k.tt(GL[:], pc[:, 256:280], BROW[:, 1280:1304], ALU.add)
                k.act(GATE[:, tt, :], GL[:], AF.Sigmoid)
                rv = r.h[:, :].rearrange("p (h two d) -> p h two d", two=2, d=32)
                t1 = r.v(rv[:, :, 0, :])
                t2 = r.v(rv[:, :, 1, :])
                cosb = COS.v(COS.h[:, tt:tt + 1, :].to_broadcast([128, 14, 32]))
                sinb = SIN.v(SIN.h[:, tt:tt + 1, :].to_broadcast([128, 14, 32]))
                ta = TA.v(TA.h[:, :].rearrange("p (h d) -> p h d", d=32))
                tb = TB.v(TB.h[:, :].rearrange("p (h d) -> p h d", d=32))
                k.tt(ta, t1, cosb, ALU.mult)
                k.tt(tb, t2, sinb, ALU.mult)
                k.tt(RO.v(RO.h[:, :, 0:32]), ta, tb, ALU.subtract)
                k.tt(ta, t2, cosb, ALU.mult)
                k.tt(tb, t1, sinb, ALU.mult)
                k.tt(RO.v(RO.h[:, :, 32:64]), ta, tb, ALU.add)
                rb = RB[tl]
                k.copy(rb.v(rb.h[:, 0:640].rearrange("p (h d) -> p h d", d=64)), RO.v(RO.h[:, 0:10, :]), eng="pool")
                k.copy(rb.v(rb.h[:, 640:1152].rearrange("p (h c d) -> p h c d", c=2, d=64)),
                       RO.v(RO.h[:, 10:14, :].unsqueeze(2).to_broadcast([128, 4, 2, 64])), eng="pool")
            tsl = slice(tc * 512, (tc + 1) * 512)
            dests = [QT.v(QT.h[:, j, tsl]) for j in range(4)] + [KCT[:, tsl], KST[0][:, tsl], KST[1][:, tsl],
                                                                   KWT[0][:, tsl], KWT[1][:, tsl]]
            for j in range(9):
                pbk = self.bbank()
                for tl in range(4):
                    k.tr(pbk[:, tl * 128:(tl + 1) * 128], RB[tl][:, j * 128:(j + 1) * 128], self.IDB[:])
                k.copy(dests[j], pbk[:, 0:512], eng=("act" if j % 2 else "dve"))
            ps = self.bank()
            for kc in range(8):
                k.mm(ps[:], WBb.v(wbv[:, kc, 128:256]), self.UT[:, kc, tsl], start=(kc == 0), stop=(kc == 7))
            k.act(VCT[:, tsl], ps[:], AF.Identity, bias=self.BCOL[:, l, BC_VC:BC_VC + 1])
        k.barrier()
        es1.close()

        es2 = ExitStack()
        W1K = k.sb("w1k", [128, 32, 128], BF16, es2)
        W1V = k.sb("w1v", [128, 32, 128], BF16, es2)
        W2K = k.sb("w2k", [128, 128], BF16, es2)
        W2V = k.sb("w2v", [128, 64], BF16, es2)
        PEK = k.sb("pek", [128, 32], BF16, es2)
        PEV = k.sb("pev", [128, 32], BF16, es2)
        k.memset(VCC[:, :, 64:65], 1.0)
        for half in range(2):
            hs = slice(64 * half, 64 * half + 64)
            k.dma(W1K.v(W1K.h[hs, :, :]), V(self.W16["cmp_wk1"].h[l].rearrange("(i d) h -> d i h", d=64), ()))
            k.dma(W1V.v(W1V.h[hs, :, :]), V(self.W16["cmp_wv1"].h[l].rearrange("(i d) h -> d i h", d=64), ()))
            k.dma(W2K.v(W2K.h[:, hs]), V(self.W16["cmp_wk2"].h[l], ()))
        k.dma(W2V[:], V(self.W16["cmp_wv2"].h[l], ()))
        self.cast_load(PEK, PEK.h[:, :], I["pe_kT"].h[l])
        self.cast_load(PEV, PEV.h[:, :], I["pe_vT"].h[l])
        CB = k.sb("cb", [128, 4], F32, es2)
        HID = k.sb("hid", [128, 4, 128], BF16, es2)
        for g in range(2):
            gs = slice(64 * g, 64 * g + 64)
            for kv, (W1, PE_, SRC) in enumerate(((W1K, PEK, KCT), (W1V, PEV, VCT))):
                idx = g * 2 + kv
                pcb = self.bank()
                for i in range(32):
                    k.mm(pcb[:, 0:1], W1.v(W1.h[gs, i, :]), PE_.v(PE_.h[gs, i:i + 1]), start=(i == 0), stop=(i == 31))
                k.copy(CB[:, idx:idx + 1], pcb[:, 0:1])
                ph = self.bank()
                for i in range(32):
                    k.mm(ph[:, 0:127], W1.v(W1.h[gs, i, :]), SRC.v(SRC.h[gs, i:i + 16 * 126 + 1:16]), start=(i == 0), stop=(i == 31))
                k.act(HID.v(HID.h[:, idx, 0:127]), ph[:, 0:127], AF.Silu, bias=CB[:, idx:idx + 1])
                po = self.bank()
                if kv == 0:
                    k.mm(po[:, 0:127], W2K[:], HID.v(HID.h[:, idx, 0:127]))
                    k.copy(KCC[g][:, 0:127], po[:, 0:127])
                else:
                    k.mm(po[0:127, 0:64], HID.v(HID.h[:, idx, 0:127]), W2V[:])
                    k.copy(VCC.v(VCC.h[:, g, 0:64]), po[0:127, 0:64])
        k.barrier()
        es2.close()
        es12.close()

        es3 = ExitStack()
        CMW = k.sb("cmw", [128, 8, 512], BF16, es3)
        CV = k.sb("cv", [127, S], BF16, es3)
        E = k.sb("e", [32, S], BF16, es3)
        OV = k.sb("ov", [127, 33], BF16, es3)
        VM = k.sb("vm", [128, 16, 32], F32, es3)
        ADDC = k.sb("addc", [128, 16, 32], F32, es3)
        k.dma(CMW[:], V(I["k_cmw"].h[:].rearrange("a p n -> p a n"), ()))
        k.dma(CV[:], I["k_cv"][:])
        k.dma(E[:], I["k_e"][:])
        k.dma(OV[:], I["k_ov"][:])
        k.dma(VM[:], I["k_vm"][:])
        k.dma(ADDC[:], I["k_addc"][:])
        NSELT = [k.sb("nselt%d" % g, [32, S], BF16, es3) for g in range(2)]
        PT = [k.sb("pt%d" % i, [128, 512], BF16, es3) for i in range(4)]
        OE = [k.sb("oe%d" % i, [65, 512], F32, es3) for i in range(4)]
        OEC = [k.sb("oec%d" % i, [65, 512], F32, es3) for i in range(4)]
        PCT = [k.sb("pct%d" % i, [127, 512], BF16, es3) for i in range(4)]
        YAs = [k.sb("ya%d" % i, [128, 4, 256], F32, es3) for i in range(2)]
        YAb = k.sb("yab16", [128, 4, 256], BF16, es3)
        TMP = k.sb("tmp", [128, 256], F32, es3)
        RS = k.sb("rs", [128, 4], F32, es3)
        RSC = k.sb("rsc", [128, 4, 4], F32, es3)
        IMPN = k.sb("impn", [128, 16, 32], F32, es3)
        IMP = k.sb("imp", [128, 4, 32], F32, es3)
        M8 = k.sb("m8", [128, 4, 8], F32, es3)
        LT = k.sb("lt", [128, 4, 32], F32, es3)
        NS = k.sb("ns", [128, 4, 32], BF16, es3)
        pti = [0]

        def next_pt():
            p = PT[pti[0] % 4]
            pti[0] += 1
            return p

        def combine(br, g, tc, first, OEs, YA):
            for tl in range(4):
                tt = tc * 4 + tl
                tp = self.mbank()
                for r in range(4):
                    k.tr(tp[:, r * 65:(r + 1) * 65], OEs[r][0:65, tl * 128:(tl + 1) * 128], self.IDF[0:65, 0:65])
                tpv = tp.h[:, 0:260].rearrange("p (r c) -> p r c", c=65)
                rs = RSC[:, tl, :] if br == 0 else RS[:]
                k.ts(rs, tp.v(tpv[:, :, 64]), 1e-30, ALU.max)
                k.recip(rs, rs)
                gv = GATE.v(GATE.h[:, tt, g * 12:(g + 1) * 12].rearrange("p (r b) -> p r b", b=3)[:, :, br])
                k.tt(RS[:], rs, gv, ALU.mult)
                rsb = RS.v(RS.h[:, :].unsqueeze(2).to_broadcast([128, 4, 64]))
                dst = YA.v(YA.h[:, tl, :].rearrange("p (r d) -> p r d", d=64))
                if first:
                    k.tt(dst, tp.v(tpv[:, :, 0:64]), rsb, ALU.mult)
                else:
                    tmpv = TMP.v(TMP.h[:, :].rearrange("p (r d) -> p r d", d=64))
                    k.tt(tmpv, tp.v(tpv[:, :, 0:64]), rsb, ALU.mult)
                    k.tt(YA[:, tl, :], YA[:, tl, :], TMP[:], ALU.add, eng="pool")

        stages = [(g, tc) for g in range(2) for tc in range(4)]

        def hb(g, r):
            h = 4 * g + r
            return h // 2, slice(64 * (h % 2), 64 * (h % 2) + 64)

        def stage_c1(idx):
            g, tc = stages[idx]
            YA = YAs[idx % 2]
            tsl = slice(tc * 512, (tc + 1) * 512)
            for r in range(4):
                pair, bs = hb(g, r)
                q = QT.v(QT.h[bs, pair, tsl])
                sc = self.bank()
                k.mm(sc[0:127, :], KCC[g].v(KCC[g].h[bs, 0:127]), q, start=True, stop=False)
                k.mm(sc[0:127, :], self.IDB[0:127, 0:127], CV[:, tsl], start=False, stop=True)
                k.act(PCT[r][:], sc[0:127, :], AF.Exp, scale=0.125)
                oa = self.abank()
                k.mm(oa[0:65, :], VCC.v(VCC.h[:, g, :]), PCT[r][:])
                k.copy(OEC[r][:], oa[0:65, :], eng="act")
            combine(0, g, tc, True, OEC, YA)
            pi = self.mbank()
            for tl in range(4):
                for r in range(4):
                    c0 = (tl * 4 + r) * 32
                    k.mm(pi[:, c0:c0 + 32], PCT[r][:, tl * 128:(tl + 1) * 128], OV[:, 0:32])
            k.tt(IMPN.v(IMPN.h[:, :, :]), pi.v(pi.h[:, :].rearrange("p (a j) -> p a j", j=32)),
                 RSC.v(RSC.h[:, :, :].rearrange("p a b -> p (a b)").unsqueeze(2).to_broadcast([128, 16, 32])), ALU.mult)
            iv = IMPN.h[:, :, :].rearrange("p (t r) j -> p t r j", r=4)
            k.tt(IMP[:], IMPN.v(iv[:, :, 0, :]), IMPN.v(iv[:, :, 1, :]), ALU.add)
            k.tt(IMP[:], IMP[:], IMPN.v(iv[:, :, 2, :]), ALU.add)
            k.tt(IMP[:], IMP[:], IMPN.v(iv[:, :, 3, :]), ALU.add)
            k.tt(IMP[:], IMP[:], VM[:, tc * 4:(tc + 1) * 4, :], ALU.mult)
            k.tt(IMP[:], IMP[:], ADDC[:, tc * 4:(tc + 1) * 4, :], ALU.add)
            for tl in range(4):
                k.op("dve", self.nc.vector.max, [IMP[:]], [M8[:]], M8.h[:, tl, :], IMP.h[:, tl, :])
            k.tt(LT[:], IMP[:], M8.v(M8.h[:, :, 7:8].to_broadcast([128, 4, 32])), ALU.is_lt)
            k.ts(NS[:], LT[:], NEG, ALU.mult)

        def stage_c2(idx):
            g, tc = stages[idx]
            pbk = self.bbank()
            for tl in range(4):
                k.tr(pbk[0:32, tl * 128:(tl + 1) * 128], NS[:, tl, :], self.IDB[:])
            k.copy(NSELT[g][:, tc * 512:(tc + 1) * 512], pbk[0:32, 0:512], eng="act")

        def stage_sw(idx):
            g, tc = stages[idx]
            YA = YAs[idx % 2]
            tsl = slice(tc * 512, (tc + 1) * 512)
            jobs = []
            for br in (1, 2):
                if br == 1:
                    kcs = list(range(0, 4 * tc + 4))
                else:
                    kcs = list(range(max(0, 4 * tc - 4), 4 * tc + 4))
                for r in range(4):
                    for kc in kcs:
                        jobs.append((br, r, kc, kc == kcs[0], kc == kcs[-1]))
            state = {}

            def cols(j):
                br, r, kc, first, last = j
                if kc >= 4 * tc:
                    return 128 * (kc - 4 * tc), 512
                if br == 2:
                    return 0, 128 * (kc - (4 * tc - 4) + 1)
                return 0, 512

            def scores(j):
                br, r, kc, first, last = j
                pair, bs = hb(g, r)
                c0, c1 = cols(j)
                qs = slice(tc * 512 + c0, tc * 512 + c1)
                q = QT.v(QT.h[bs, pair, qs])
                ksl = slice(kc * 128, (kc + 1) * 128)
                sc = self.sbank()
                if br == 1:
                    diag = kc >= 4 * tc
                    k.mm(sc[:, c0:c1], KST[g].v(KST[g].h[bs, ksl]), q, start=True, stop=False)
                    k.mm(sc[:, c0:c1], E[:, ksl], NSELT[g][:, qs], start=False, stop=not diag)
                    if diag:
                        k.mm(sc[:, c0:c1], self.IDB[:], CMW[:, kc - 4 * tc, c0:c1], start=False, stop=True)
                else:
                    mi = (kc - 4 * tc) if kc >= 4 * tc else (4 + kc - (4 * tc - 4))
                    k.mm(sc[:, c0:c1], KWT[g].v(KWT[g].h[bs, ksl]), q, start=True, stop=False)
                    k.mm(sc[:, c0:c1], self.IDB[:], CMW[:, mi, c0:c1], start=False, stop=True)
                state[j] = sc

            def rest(j):
                br, r, kc, first, last = j
                c0, c1 = cols(j)
                sc = state.pop(j)
                if first:
                    state[("oa", br, r)] = self.abank()
                oa = state[("oa", br, r)]
                pt = next_pt()
                k.act(pt[:, c0:c1], sc[:, c0:c1], AF.Exp, scale=0.125)
                Vt = VS if br == 1 else VW
                k.mm(oa[0:65, c0:c1], Vt.v(Vt.h[:, kc, g, :]), pt[:, c0:c1], start=first, stop=last, skip_group_check=True)
                if last:
                    k.copy(OE[r][:], oa[0:65, :], eng="act")
                    if r == 3:
                        combine(br, g, tc, False, OE, YA)

            LA = 2
            for i in range(min(LA, len(jobs))):
                scores(jobs[i])
            for i in range(len(jobs)):
                if i + LA < len(jobs):
                    scores(jobs[i + LA])
                rest(jobs[i])
            k.copy(YAb[:], YA[:], eng="act")
            for fcl in range(2):
                pbk = self.bbank()
                for tl in range(4):
                    k.tr(pbk[:, tl * 128:(tl + 1) * 128], YAb[:, tl, fcl * 128:(fcl + 1) * 128], self.IDB[:])
                k.copy(self.YAB.sub("a", (slice(None), 2 * g + fcl, tsl)), pbk[:, 0:512])

        stage_c1(0)
        stage_c2(0)
        for idx in range(len(stages)):
            if idx + 1 < len(stages):
                stage_c1(idx + 1)
            stage_sw(idx)
            if idx + 1 < len(stages):
                stage_c2(idx + 1)
        k.barrier()
        es3.close()
        es.close()

    def hgrn(self, s, l):
        k, I = self.k, self.I
        es = ExitStack()
        sb = lambda n, sh, dt: k.sb(n, sh, dt, es)
        win = self.W16["w_in"].h
        RST = sb("rst", [128, S], F32)
        BDM = sb("bdm", [128, 128], F32)
        k.dma(RST[:], I["k_rst"][:])
        k.dma(BDM[:], I["k_bdm"][:])
        BROWI = sb("browi", [128, 512], F32)
        k.dma(BROWI[:], V(I["b_in"].h[l, 2328:2840].partition_broadcast(128), ()))
        VH = sb("vh", [128, 16, 512], BF16)
        wi, wiv = self.load_w(win[l, :, 2328:2840], 8, 512)
        for tt in range(16):
            ps = self.bank()
            for kc in range(8):
                k.mm(ps[:], self.UT[:, kc, tt * 128:(tt + 1) * 128], wi.v(wiv[:, kc, :]), start=(kc == 0), stop=(kc == 7))
            k.tt(VH[:, tt, :], ps[:], BROWI[:], ALU.add)
        QPs = [sb("qp%d" % i, [128, S], BF16) for i in range(2)]
        KPs = [sb("kp%d" % i, [128, S], BF16) for i in range(2)]
        KPTs = [sb("kpt%d" % i, [128, 16, 128], BF16) for i in range(2)]
        GSs = [sb("gs%d" % i, [128, S], BF16) for i in range(2)]
        EBLs = [sb("ebl%d" % i, [128, 32], F32) for i in range(2)]
        SB16s = [sb("sb16%d" % i, [128, 32, 128], BF16) for i in range(2)]
        F1 = sb("f1", [128, S], F32)
        F2 = sb("f2", [128, S], F32)
        F3 = sb("f3", [128, S], F32)
        EB = sb("eb", [128, S], F32)
        SST = sb("sst", [128, 128], F32)
        STMP = sb("stmp", [128, 128], F32)
        ATM = [sb("atm%d" % i, [128, 128], BF16) for i in range(2)]
        O2 = sb("o2", [128, 512], F32)
        RSTD = sb("rstd", [128, 512], F32)
        T1 = sb("t1", [128, 512], F32)

        def stage_p(hd):
            QP, KP, GS, EBL = QPs[hd % 2], KPs[hd % 2], GSs[hd % 2], EBLs[hd % 2]
            wq, wqv = self.load_w(win[l, :, 1304 + 128 * hd:1304 + 128 * (hd + 1)], 8, 128)
            wf, wfv = self.load_w(win[l, :, 1816 + 128 * hd:1816 + 128 * (hd + 1)], 8, 128)
            wg, wgv = self.load_w(win[l, :, 2840 + 128 * hd:2840 + 128 * (hd + 1)], 8, 128)
            for tc in range(4):
                tsl = slice(tc * 512, (tc + 1) * 512)
                pq, pf_, pg = self.bank(), self.bank(), self.bank()
                for (p_, w_, wv_) in ((pq, wq, wqv), (pf_, wf, wfv), (pg, wg, wgv)):
                    for kc in range(8):
                        k.mm(p_[:], w_.v(wv_[:, kc, :]), self.UT[:, kc, tsl], start=(kc == 0), stop=(kc == 7))
                k.act(F1[:, tsl], pq[:], AF.Silu, bias=self.BCOL[:, l, BC_QB + hd:BC_QB + hd + 1])
                k.act(GS[:, tsl], pg[:], AF.Silu, bias=self.BCOL[:, l, BC_GB + hd:BC_GB + hd + 1])
                k.act(F2[:, tsl], pf_[:], AF.Sigmoid, bias=self.BCOL[:, l, BC_FB + hd:BC_FB + hd + 1])
            k.ts(F2[:], F2[:], self.OML[:, l, hd:hd + 1], ALU.mult, self.LB[:, l, hd:hd + 1], ALU.add)
            k.act(F3[:], F2[:], AF.Ln)
            k.ts(F2[:], F2[:], -1.0, ALU.mult, 1.0, ALU.add)
            k.op("dve", self.nc.vector.tensor_tensor_scan, [RST[:], F3[:]], [EB[:]], EB[:].ap, RST[:].ap, F3[:].ap, 0.0,
                 ALU.mult, ALU.add)
            k.act(F3[:], EB[:], AF.Exp, scale=-1.0)
            k.act(EB[:], EB[:], AF.Exp)
            k.stt(QP[:], F1[:], 128.0 ** -0.5, EB[:], ALU.mult, ALU.mult)
            k.tt(KP[:], F2[:], F3[:], ALU.mult)
            k.copy(EBL[:], EB[:, 63:S:64], eng="pool")

        def stage_r(hd):
            QP, KP, GS, EBL = QPs[hd % 2], KPs[hd % 2], GSs[hd % 2], EBLs[hd % 2]
            KPT, SB16 = KPTs[hd % 2], SB16s[hd % 2]
            for tc in range(4):
                pbk = self.bbank()
                for tl in range(4):
                    tt = tc * 4 + tl
                    k.tr(pbk[:, tl * 128:(tl + 1) * 128], KP[:, tt * 128:(tt + 1) * 128], self.IDB[:])
                k.copy(KPT.v(KPT.h[:, tc * 4:(tc + 1) * 4, :].rearrange("p a b -> p (a b)")), pbk[:, 0:512], eng="act")
            k.memset(SST[:], 0.0, eng="dve")
            k.memset(SB16[:, 0, :], 0.0, eng="dve")
            vcols = slice(hd * 128, (hd + 1) * 128)
            for c4 in range(8):
                pmh = [self.bank(), self.bank()]
                for ci in range(4):
                    c = c4 * 4 + ci
                    tt, half = c // 2, c % 2
                    hs = slice(64 * half, 64 * half + 64)
                    k.mm(pmh[half][:, (ci // 2) * 128:(ci // 2 + 1) * 128], KPT.v(KPT.h[hs, tt, :]), VH.v(VH.h[hs, tt, vcols]))
                for ci in range(4):
                    c = c4 * 4 + ci
                    if c == 31:
                        break
                    half = c % 2
                    k.tt(STMP[:], pmh[half][:, (ci // 2) * 128:(ci // 2 + 1) * 128], SST[:], ALU.add)
                    k.ts(SST[:], STMP[:], EBL[:, c:c + 1], ALU.mult)
                    k.copy(SB16[:, c + 1, :], SST[:], eng="act")
            for tc in range(4):
                tsl = slice(tc * 512, (tc + 1) * 512)
                po = self.abank()
                for tl in range(4):
                    tt = tc * 4 + tl
                    t_sl = slice(tt * 128, (tt + 1) * 128)
                    pa = self.bank()
                    k.mm(pa[:, 0:128], KP[:, t_sl], QP[:, t_sl])
                    atm = ATM[tt % 2]
                    k.tt(atm[:], pa[:, 0:128], BDM[:], ALU.mult)
                    osl = slice(tl * 128, (tl + 1) * 128)
                    k.mm(po[:, osl], VH.v(VH.h[:, tt, vcols]), atm[:], start=True, stop=False)
                    for half in range(2):
                        c = 2 * tt + half
                        k.mm(po[:, tl * 128 + 64 * half: tl * 128 + 64 * half + 64], SB16[:, c, :],
                             QP[:, 64 * c:64 * c + 64], start=False, stop=(half == 1))
                k.act(O2[:], po[:], AF.Square)
                pss = self.bank()
                k.mm(pss[:], self.ONESF[:], O2[:])
                k.act(RSTD[:], pss[:], AF.Ln, scale=1.0 / 128.0, bias=self.EPS[:, 1:2])
                k.act(RSTD[:], RSTD[:], AF.Exp, scale=-0.5)
                k.tt(T1[:], po[:], RSTD[:], ALU.mult)
                k.stt(self.YAB.sub("b", (slice(None), 4 + hd, tsl)), T1[:], self.NORMG[:, l:l + 1], GS[:, tsl], ALU.mult, ALU.mult)

        stage_p(0)
        for hd in range(4):
            if hd + 1 < 4:
                stage_p(hd + 1)
            stage_r(hd)
        k.barrier()
        es.close()

    def layernorm(self, z, gB, bB, es_tmp):
        k = self.k
        st, mv, rstd = es_tmp
        zv = z.res
        for j in range(2):
            k.op("dve", self.nc.vector.bn_stats, [z], [st[:]], st.h[:, j * 6:(j + 1) * 6], z.ap[:, j * 512:(j + 1) * 512])
        k.op("dve", self.nc.vector.bn_aggr, [st[:]], [mv[:]], mv[:].ap, st[:].ap)
        k.act(rstd[:], mv[:, 1:2], AF.Sqrt, bias=self.EPS[:, 0:1])
        k.recip(rstd[:], rstd[:])
        k.ts(z, z, mv[:, 0:1], ALU.subtract, rstd[:], ALU.mult)
        k.tt(z, z, gB[:], ALU.mult)
        k.tt(z, z, bB[:], ALU.add, eng="pool")

    def tail(self, s, l, src):
        k, I = self.k, self.I
        es = ExitStack()
        sb = lambda n, sh, dt: k.sb(n, sh, dt, es)
        win = self.W16["w_in"].h
        MG = sb("mg", [128, 8, S], BF16)
        SIGA = sb("siga", [128, 512], BF16)
        SIGB = sb("sigb", [128, 512], BF16)
        M1 = sb("m1", [128, 512], F32)
        M2 = sb("m2", [128, 512], F32)
        BRB = sb("brb", [128, 4, 128], BF16)
        for nch in range(8):
            csl = slice(nch * 128, (nch + 1) * 128)
            wga, wgav = self.load_w(win[l, :, 3352 + 128 * nch:3352 + 128 * (nch + 1)], 8, 128)
            wgb, wgbv = self.load_w(win[l, :, 4376 + 128 * nch:4376 + 128 * (nch + 1)], 8, 128)
            wbr, wbrv = self.load_w(self.W16["w_branch_a"].h[l, :, csl], 4, 128)
            wbb = BRB
            k.dma(wbb[:], V(self.W16["w_branch_b"].h[l, :, csl].rearrange("(a p) n -> p a n", p=128), ()))
            for tc in range(4):
                tsl = slice(tc * 512, (tc + 1) * 512)
                pga, pgb, pba, pbb = self.bank(), self.bank(), self.bank(), self.bank()
                for kc in range(8):
                    k.mm(pga[:], wga.v(wgav[:, kc, :]), self.UT[:, kc, tsl], start=(kc == 0), stop=(kc == 7))
                for kc in range(8):
                    k.mm(pgb[:], wgb.v(wgbv[:, kc, :]), self.UT[:, kc, tsl], start=(kc == 0), stop=(kc == 7))
                for kc in range(4):
                    k.mm(pba[:], wbr.v(wbrv[:, kc, :]), self.YAB.sub("a", (slice(None), kc, tsl)), start=(kc == 0), stop=(kc == 3))
                for kc in range(4):
                    k.mm(pbb[:], wbb[:, kc, :], self.YAB.sub("b", (slice(None), 4 + kc, tsl)), start=(kc == 0), stop=(kc == 3))
                k.act(SIGA[:], pga[:], AF.Sigmoid, bias=self.BCOL[:, l, BC_GMA + nch:BC_GMA + nch + 1])
                k.act(SIGB[:], pgb[:], AF.Sigmoid, bias=self.BCOL[:, l, BC_GMB + nch:BC_GMB + nch + 1])
                k.tt(M1[:], pba[:], SIGA[:], ALU.mult)
                k.tt(M2[:], pbb[:], SIGB[:], ALU.mult)
                k.tt(MG[:, nch, tsl], M1[:], M2[:], ALU.add, eng="pool")
        self.tapv("mgT", MG[:], [128, 8, S], BF16)
        if self.stop == "merge":
            k.barrier()
            es.close()
            return
        G1 = sb("g1", [128, D], F32)
        G2 = sb("g2", [128, D], F32)
        LG1 = sb("lg1", [128, D], F32)
        LB1 = sb("lb1", [128, D], F32)
        LG2 = sb("lg2", [128, D], F32)
        LB2 = sb("lb2", [128, D], F32)
        k.dma(G1[:], V(self.grow.h[l, s, 0, :].partition_broadcast(128), self.grow[:].res))
        k.dma(G2[:], V(self.grow.h[l, s, 1, :].partition_broadcast(128), self.grow[:].res))
        k.dma(LG1[:], V(I["ln1_g"].h[l, :].partition_broadcast(128), ()))
        k.dma(LB1[:], V(I["ln1_b"].h[l, :].partition_broadcast(128), ()))
        k.dma(LG2[:], V(I["ln2_g"].h[l, :].partition_broadcast(128), ()))
        k.dma(LB2[:], V(I["ln2_b"].h[l, :].partition_broadcast(128), ()))
        XTs = [[sb("xt%d_%d" % (j, i), [128, D], F32) for i in range(4)] for j in range(2)]
        ST = sb("st", [128, 12], F32)
        MV = sb("mv", [128, 2], F32)
        RSD = sb("rsd", [128, 1], F32)
        TMPZ = sb("tmpz", [128, 512], F32)
        Y2 = sb("y2", [128, 512], F32)
        H1 = self.YAB
        h1v = H1.h[:, :, :].rearrange("p a (b c) -> p (a b) c", c=512)
        h1res = ((H1.name, "a"), (H1.name, "b"))
        UT2 = sb("ut2", [128, 8, 512], BF16)
        last_layer = (l == self.layers[-1]) and self.last
        dst = self.out if last_layer else self.xres

        def stage_ea(tc):
            XT = XTs[tc % 2]
            wo = []
            for nh in range(2):
                wo.append(self.load_w(self.W16["w_out"].h[l, :, nh * 512:(nh + 1) * 512], 8, 512))
            for tl in range(4):
                tt = tc * 4 + tl
                x = XT[tl]
                k.dma(x[:], V(src.h[s, tt * 128:(tt + 1) * 128, :], src[:].res))
                for nh in range(2):
                    ps = self.bank()
                    w_, wv_ = wo[nh]
                    for kc in range(8):
                        k.mm(ps[:], MG[:, kc, tt * 128:(tt + 1) * 128], w_.v(wv_[:, kc, :]), start=(kc == 0), stop=(kc == 7))
                    k.tt(TMPZ[:], ps[:], G1[:, nh * 512:(nh + 1) * 512], ALU.mult)
                    k.stt(x[:, nh * 512:(nh + 1) * 512], x[:, nh * 512:(nh + 1) * 512], ALPHA, TMPZ[:], ALU.mult, ALU.add)
                self.layernorm(x[:], LG1, LB1, (ST, MV, RSD))

        def stage_eb(tc):
            XT = XTs[tc % 2]
            for fc in range(8):
                ps = self.bank()
                for tl in range(4):
                    k.tr(ps[:, tl * 128:(tl + 1) * 128], XT[tl][:, fc * 128:(fc + 1) * 128], self.IDF[:])
                k.act(UT2[:, fc, :], ps[:], AF.Identity, bias=self.MODT[:, l, 24 + fc, s:s + 1],
                      scale=self.MODT[:, l, 32 + fc, s:s + 1])

        def stage_f1(tc):
            for n4 in range(8):
                w_, wv_ = self.load_w(self.W16["w_mlp1"].h[l, :, n4 * 512:(n4 + 1) * 512], 8, 512)
                for j in range(4):
                    nch = n4 * 4 + j
                    ps = self.bank()
                    for kc in range(8):
                        k.mm(ps[:], w_.v(wv_[:, kc, j * 128:(j + 1) * 128]), UT2[:, kc, :], start=(kc == 0), stop=(kc == 7))
                    k.act(M1[:], ps[:], AF.Relu)
                    k.tt(V(h1v[:, nch, :], h1res), M1[:], M1[:], ALU.mult, eng="pool")

        def stage_f2(tc):
            XT = XTs[tc % 2]
            for nch2 in range(8):
                w_, wv_ = self.load_w(self.W16["w_mlp2"].h[l, nch2], 32, 128, pretiled=True)
                py = self.abank()
                for kc in range(32):
                    k.mm(py[:], w_.v(wv_[:, kc, :]), V(h1v[:, kc, :], h1res), start=(kc == 0), stop=(kc == 31))
                k.copy(Y2[:], py[:], eng="act")
                ptr = self.bank()
                for tl in range(4):
                    k.tr(ptr[:, tl * 128:(tl + 1) * 128], Y2[:, tl * 128:(tl + 1) * 128], self.IDF[:])
                cs = slice(nch2 * 128, (nch2 + 1) * 128)
                for tl in range(4):
                    x = XT[tl]
                    k.tt(TMPZ[:, 0:128], ptr[:, tl * 128:(tl + 1) * 128], G2[:, cs], ALU.mult)
                    k.stt(x[:, cs], x[:, cs], ALPHA, TMPZ[:, 0:128], ALU.mult, ALU.add)
            for tl in range(4):
                tt = tc * 4 + tl
                x = XT[tl]
                self.layernorm(x[:], LG2, LB2, (ST, MV, RSD))
                k.dma(V(dst.h[s, tt * 128:(tt + 1) * 128, :], dst[:].res), x[:])

        stage_ea(0)
        stage_eb(0)
        for tc in range(4):
            stage_f1(tc)
            if tc + 1 < 4:
                stage_ea(tc + 1)
            stage_f2(tc)
            if tc + 1 < 4:
                stage_eb(tc + 1)
        k.barrier()
        es.close()


def _consts():
    cmw = np.zeros((8, 128, 512), np.float32)
    kl = np.arange(128)[:, None]
    tl = np.arange(512)[None, :]
    for i in range(4):
        cmw[i] = np.where(128 * i + kl <= tl, 0.0, NEG)
        cmw[4 + i] = np.where(tl <= 128 * i + kl - 1, 0.0, NEG)
    n = np.arange(127)[:, None]
    t = np.arange(S)[None, :]
    cv = np.where(16 * n + 31 <= t, 0.0, NEG).astype(np.float32)
    j = np.arange(32)[:, None]
    e = (t // 64 == j).astype(np.float32)
    jj = np.arange(32)[None, :]
    ov = np.zeros((127, 33), np.float32)
    ov[:, :32] = ((16 * n < 64 * jj + 64) & (16 * n + 32 > 64 * jj)).astype(np.float32)
    ov[:, 32] = 1.0
    rst = np.ones((128, S), np.float32)
    rst[:, ::64] = 0.0
    s_ = np.arange(128)[:, None]
    t_ = np.arange(128)[None, :]
    bdm = ((s_ <= t_) & (s_ // 64 == t_ // 64)).astype(np.float32)
    ident = np.eye(128, dtype=np.float32)
    inv = (1.0 / (np.float32(10000.0) ** (np.arange(0, 64, 2, dtype=np.float32) / np.float32(64)))).astype(np.float32)
    ang = np.arange(S, dtype=np.float32)[:, None] * inv[None, :]
    cos = np.cos(ang).astype(np.float32).reshape(16, 128, 32).transpose(1, 0, 2)
    sin = np.sin(ang).astype(np.float32).reshape(16, 128, 32).transpose(1, 0, 2)
    pos = np.arange(S)
    tb = pos // 64
    jb = np.arange(32)
    valid = jb[None, :] <= tb[:, None]
    forced = valid & ((jb[None, :] == 0) | (jb[None, :] == tb[:, None]) | (jb[None, :] == tb[:, None] - 1))
    vm = (valid & ~forced).astype(np.float32).reshape(16, 128, 32).transpose(1, 0, 2)
    addc = np.where(forced, 1.0e4, np.where(valid, 0.0, -1.0)).astype(np.float32).reshape(16, 128, 32).transpose(1, 0, 2)
    import ml_dtypes
    bf = ml_dtypes.bfloat16
    c = dict(k_cmw=cmw.astype(bf), k_cv=cv.astype(bf), k_e=e.astype(bf), k_ov=ov.astype(bf), k_rst=rst, k_bdm=bdm,
             k_ident=ident, k_identb=ident.astype(bf), k_cos=cos, k_sin=sin, k_vm=vm, k_addc=addc)
    return {k_: np.ascontiguousarray(v) for k_, v in c.items()}


def prep_inputs(inputs, seqs):
    f = lambda a: np.ascontiguousarray(np.asarray(a, dtype=np.float32))
    m = {}
    m["x"] = f(inputs["x"][seqs])
    c = np.asarray(inputs["c"], np.float32)[seqs]
    m["c_l"] = f(c.reshape(len(seqs), 8, 128).transpose(2, 1, 0))
    for nme in ("w_in", "b_in", "cmp_wk1", "cmp_wk2", "cmp_wv1", "cmp_wv2", "w_branch_a", "w_branch_b", "w_out",
                "w_ada", "b_ada", "ln1_g", "ln1_b", "w_mlp1", "w_mlp2", "ln2_g", "ln2_b"):
        m[nme] = f(inputs[nme])
    b_in = np.asarray(inputs["b_in"], np.float32)
    m["bcols"] = f(np.stack([b_in[:, o:o + 128] for o in BCOL_OFFS], axis=-1).transpose(1, 0, 2))
    pek = np.asarray(inputs["cmp_pe_k"], np.float32).transpose(0, 2, 1)
    pev = np.asarray(inputs["cmp_pe_v"], np.float32).transpose(0, 2, 1)
    m["pe_kT"] = f(np.concatenate([pek, pek], axis=1))
    m["pe_vT"] = f(np.concatenate([pev, pev], axis=1))
    m["lb_l"] = f(np.asarray(inputs["hgrn_lb_logits"], np.float32).reshape(L, 4, 128).transpose(2, 0, 1))
    m["normg_l"] = f(np.asarray(inputs["hgrn_norm_g"], np.float32).T)
    m["b_adaT"] = f(np.asarray(inputs["b_ada"], np.float32).reshape(L, 48, 128).transpose(2, 0, 1))
    m.update(_consts())
    return m


def build_prog(**kw):
    p1 = Prog(**kw)
    return Prog(waited=p1.k.waited, **kw)


def kernel(**inputs):
    n = 8
    prog = build_prog()
    in_maps = [prep_inputs(inputs, list(range(c * NSEQ, (c + 1) * NSEQ))) for c in range(n)]
    res = run_bass_kernel_spmd(prog.nc, in_maps, core_ids=list(range(n)))
    out = np.concatenate([np.asarray(r["out"], np.float32) for r in res.results], axis=0)
    return out
```
